# Optimizing a Trainium2 kernel written in Bass

```python
import math
import jax, jax.numpy as jnp
from jax import lax
import numpy as np

D_MODEL = 1024
BATCH = 32
SEQ = 2048
DEPTH = 4
DEC_BATCH = 16
DEC_SEQ = 2048
PAST_LEN = 128

HEAD_DIM = 64
GRID_W = 64
NA_HEADS = 4
NA_WIN_H = 8
NA_WIN_W = 16
SW_HEADS = 4
SW_KV_HEADS = 2
SW_HALF_WINDOW = 128
SW_BLOCK = 128
DIL_PAIRS = ((128, 1), (512, 4), (2048, 16))
DIL_HEADS_PER_GROUP = 2
DIL_HEADS = DIL_HEADS_PER_GROUP * len(DIL_PAIRS)
DIFF_HEADS = 4
DIFF_HALF = HEAD_DIM // 2
DIFF_Q_BLOCK = 128
T5_BUCKETS = 32
T5_MAX_DIST = 128
T5_B0 = 0
T5_C0 = T5_B0 + SW_HEADS
T5_D0 = T5_C0 + DIL_HEADS
T5_HEADS = T5_D0 + DIFF_HEADS
A_W = 3 * NA_HEADS * HEAD_DIM
B_W = (SW_HEADS + 2 * SW_KV_HEADS) * HEAD_DIM
C_W = 3 * DIL_HEADS * HEAD_DIM
D_W = 3 * DIFF_HEADS * HEAD_DIM
PROJ_WIDTH = A_W + B_W + C_W + D_W
SPLIT_POINTS = (A_W, A_W + B_W, A_W + B_W + C_W)
MIX_WIDTH = (NA_HEADS + SW_HEADS + DIL_HEADS + DIFF_HEADS) * HEAD_DIM
D_FF = 2816
RMS_EPS = 1e-6
NEG_INF = -1e30

kernel_name = 'hybrid_parallel_heads_bidir_encoder'


def rms_norm(x, g):
    xf = x.astype(jnp.float32)
    y = xf * lax.rsqrt(jnp.mean(xf * xf, axis=-1, keepdims=True) + RMS_EPS)
    return (y * g.astype(jnp.float32)).astype(x.dtype)


def t5_bucket(rel):
    nb = T5_BUCKETS // 2
    max_exact = nb // 2
    base = jnp.where(rel > 0, nb, 0)
    n = jnp.abs(rel)
    nf = jnp.maximum(n, 1).astype(jnp.float32)
    large = max_exact + (jnp.log(nf / max_exact) / math.log(T5_MAX_DIST / max_exact)
                         * (nb - max_exact)).astype(jnp.int32)
    large = jnp.minimum(large, nb - 1)
    return base + jnp.where(n < max_exact, n, large)


def banded_attention(q, k, v, half_window, block, stride, bias_table, sink):
    Bsz, L, H, d = q.shape
    Hk = k.shape[2]
    G = H // Hk
    nb = -(-L // block)
    pad = nb * block - L
    qb = jnp.pad(q, ((0, 0), (0, pad), (0, 0), (0, 0))).reshape(Bsz, nb, block, Hk, G, d)
    kp = jnp.pad(k, ((0, 0), (block, block + pad), (0, 0), (0, 0)))
    vp = jnp.pad(v, ((0, 0), (block, block + pad), (0, 0), (0, 0)))
    kidx = np.arange(nb)[:, None] * block + np.arange(3 * block)[None, :]
    kb = kp[:, kidx]
    vb = vp[:, kidx]
    qi = np.arange(nb)[:, None] * block + np.arange(block)[None, :]
    kj = kidx - block
    rel = kj[:, None, :] - qi[:, :, None]
    mask = (np.abs(rel) <= half_window) & (kj[:, None, :] >= 0) & (kj[:, None, :] < L)
    bias = bias_table.astype(jnp.float32)[t5_bucket(jnp.asarray(rel * stride, dtype=jnp.int32))]
    bias = bias.transpose(0, 3, 1, 2).reshape(nb, Hk, G, block, 3 * block)
    s = jnp.einsum('bnqhgd,bnkhd->bnhgqk', qb, kb, preferred_element_type=jnp.float32)
    s = s * (d ** -0.5) + bias[None]
    s = jnp.where(mask[None, :, None, None], s, NEG_INF)
    m = jnp.max(s, axis=-1, keepdims=True)
    if sink is not None:
        sk = sink.astype(jnp.float32).reshape(1, 1, Hk, G, 1, 1)
        m = jnp.maximum(m, sk)
        p = jnp.exp(s - m)
        denom = jnp.sum(p, axis=-1, keepdims=True) + jnp.exp(sk - m)
    else:
        p = jnp.exp(s - m)
        denom = jnp.sum(p, axis=-1, keepdims=True)
    o = jnp.einsum('bnhgqk,bnkhd->bnhgqd', p, vb.astype(jnp.float32)) / denom
    lse = (m + jnp.log(denom))[..., 0]
    o = o.transpose(0, 1, 4, 2, 3, 5).reshape(Bsz, nb * block, H, d)[:, :L]
    lse = lse.transpose(0, 1, 4, 2, 3).reshape(Bsz, nb * block, H)[:, :L]
    return o, lse


def neighborhood_attention(proj_a, qkn, rpb):
    Bsz, N, _ = proj_a.shape
    H, d = NA_HEADS, HEAD_DIM
    qkv = proj_a.reshape(Bsz, N, 3, H, d)
    q = rms_norm(qkv[:, :, 0], qkn[0])
    k = rms_norm(qkv[:, :, 1], qkn[1])
    v = qkv[:, :, 2]
    rows = N // GRID_W
    kh = min(NA_WIN_H, rows)
    n_cb = GRID_W // NA_WIN_W
    kb_w = 2 * NA_WIN_W
    r = np.arange(rows)
    rs = np.clip(r - kh // 2, 0, rows - kh)
    key_rows = rs[:, None] + np.arange(kh)[None, :]
    dr_idx = key_rows - r[:, None] + NA_WIN_H - 1
    j = np.arange(n_cb)
    kbs = np.clip(j * NA_WIN_W - NA_WIN_W // 2, 0, GRID_W - kb_w)
    key_cols = kbs[:, None] + np.arange(kb_w)[None, :]
    qc = j[:, None] * NA_WIN_W + np.arange(NA_WIN_W)[None, :]
    cs = np.clip(qc - NA_WIN_W // 2, 0, GRID_W - NA_WIN_W)
    col_ok = (key_cols[:, None, :] >= cs[:, :, None]) & (key_cols[:, None, :] < cs[:, :, None] + NA_WIN_W)
    dc_idx = np.clip(key_cols[:, None, :] - qc[:, :, None] + NA_WIN_W - 1, 0, 2 * NA_WIN_W - 2)
    n_k = kh * kb_w
    tok = (key_rows[:, None, :, None] * GRID_W + key_cols[None, :, None, :]).reshape(rows, n_cb, n_k)
    mask = np.broadcast_to(col_ok[:, :, None, :], (n_cb, NA_WIN_W, kh, kb_w)).reshape(n_cb, NA_WIN_W, n_k)
    bias = rpb.astype(jnp.float32)[:, dr_idx[:, None, None, :, None], dc_idx[None, :, :, None, :]]
    bias = bias.reshape(H, rows, n_cb, NA_WIN_W, n_k)
    qb = q.reshape(Bsz, rows, n_cb, NA_WIN_W, H, d)
    kb = k[:, tok]
    vb = v[:, tok]
    s = jnp.einsum('brjqhd,brjkhd->bhrjqk', qb, kb, preferred_element_type=jnp.float32)
    s = jnp.where(mask, s * (d ** -0.5) + bias[None], NEG_INF)
    p = jax.nn.softmax(s, axis=-1)
    o = jnp.einsum('bhrjqk,brjkhd->brjqhd', p, vb.astype(jnp.float32))
    return o.reshape(Bsz, N, H * d)


def sliding_window_gqa(proj_b, qkn, sink, t5_heads):
    Bsz, N, _ = proj_b.shape
    d = HEAD_DIM
    qw = SW_HEADS * d
    kw = SW_KV_HEADS * d
    q = rms_norm(proj_b[..., :qw].reshape(Bsz, N, SW_HEADS, d), qkn[0])
    k = rms_norm(proj_b[..., qw:qw + kw].reshape(Bsz, N, SW_KV_HEADS, d), qkn[1])
    v = proj_b[..., qw + kw:].reshape(Bsz, N, SW_KV_HEADS, d)
    o, _ = banded_attention(q, k, v, SW_HALF_WINDOW, SW_BLOCK, 1, t5_heads, sink)
    return o.reshape(Bsz, N, qw)


def dilated_attention(proj_c, qkn, t5_heads):
    Bsz, N, _ = proj_c.shape
    H, d, G = DIL_HEADS, HEAD_DIM, DIL_HEADS_PER_GROUP
    qkv = proj_c.reshape(Bsz, N, 3, H, d)
    q = rms_norm(qkv[:, :, 0], qkn[0])
    k = rms_norm(qkv[:, :, 1], qkn[1])
    v = qkv[:, :, 2]
    outs, lses = [], []
    for g, (w, r) in enumerate(DIL_PAIRS):
        L = N // r
        hw = w // (2 * r)
        hs = slice(g * G, (g + 1) * G)

        def to_sub(t):
            return t.reshape(Bsz, L, r, G, d).transpose(0, 2, 1, 3, 4).reshape(Bsz * r, L, G, d)

        o, lse = banded_attention(to_sub(q[:, :, hs]), to_sub(k[:, :, hs]), to_sub(v[:, :, hs]),
                                  hw, hw, r, t5_heads[:, hs], None)
        outs.append(o.reshape(Bsz, r, L, G, d).transpose(0, 2, 1, 3, 4).reshape(Bsz, N, G, d))
        lses.append(lse.reshape(Bsz, r, L, G).transpose(0, 2, 1, 3).reshape(Bsz, N, G))
    alpha = jax.nn.softmax(jnp.stack(lses, axis=0), axis=0)
    o = jnp.concatenate([alpha[g][..., None] * outs[g] for g in range(len(DIL_PAIRS))], axis=2)
    return o.reshape(Bsz, N, H * d)


def differential_attention(proj_d, qkn, lam_p, subln_g, t5_heads, lambda_init):
    Bsz, N, _ = proj_d.shape
    H, e, d = DIFF_HEADS, DIFF_HALF, HEAD_DIM
    qkv = proj_d.reshape(Bsz, N, 3, H, d)
    q = rms_norm(qkv[:, :, 0].reshape(Bsz, N, H, 2, e), qkn[0])
    k = rms_norm(qkv[:, :, 1].reshape(Bsz, N, H, 2, e), qkn[1])
    v = qkv[:, :, 2].astype(jnp.float32)
    lp = lam_p.astype(jnp.float32)
    lam = jnp.exp(jnp.sum(lp[0] * lp[1])) - jnp.exp(jnp.sum(lp[2] * lp[3])) + lambda_init
    nqb = N // DIFF_Q_BLOCK
    q_blocks = q.reshape(Bsz, nqb, DIFF_Q_BLOCK, H, 2, e).transpose(1, 0, 2, 3, 4, 5)
    kpos = jnp.arange(N, dtype=jnp.int32)
    table = t5_heads.astype(jnp.float32)

    def one_block(args):
        qb, q0 = args
        qpos = q0 + jnp.arange(DIFF_Q_BLOCK, dtype=jnp.int32)
        bias = table[t5_bucket(kpos[None, :] - qpos[:, None])].transpose(2, 0, 1)
        s = jnp.einsum('bqhie,bkhie->bihqk', qb, k, preferred_element_type=jnp.float32)
        p = jax.nn.softmax(s * (e ** -0.5) + bias[None, None], axis=-1)
        a = p[:, 0] - lam * p[:, 1]
        return jnp.einsum('bhqk,bkhd->bqhd', a, v)

    o = lax.map(one_block, (q_blocks, jnp.arange(nqb, dtype=jnp.int32) * DIFF_Q_BLOCK))
    o = o.transpose(1, 0, 2, 3, 4).reshape(Bsz, N, H, d)
    o = rms_norm(o, subln_g) * (1.0 - lambda_init)
    return o.reshape(Bsz, N, H * d)


def conv_glu_ffn(h, w_up, conv_w, conv_b, w_down):
    u = h @ w_up
    up = jnp.pad(u, ((0, 0), (1, 1), (0, 0)))
    u = up[:, :-2] * conv_w[0] + up[:, 1:-1] * conv_w[1] + up[:, 2:] * conv_w[2] + conv_b
    val, gate = jnp.split(u, 2, axis=-1)
    return (val * jax.nn.silu(gate)) @ w_down


def _trunk(x, c, norm_attn_g, norm_ffn_g, w_ada, b_ada, w_in, qkn_a, qkn_b, qkn_c, qkn_d,
           rpb_a, sink_b, t5_table, lam_d, subln_d, w_out, w_up, conv_w, conv_b, w_down):
    Bsz, N, D = x.shape
    for l in range(DEPTH):
        mod = (jax.nn.silu(c) @ w_ada[l] + b_ada[l]).reshape(Bsz, 6, 1, D)
        shift_a, scale_a, gate_a = mod[:, 0], mod[:, 1], mod[:, 2]
        shift_f, scale_f, gate_f = mod[:, 3], mod[:, 4], mod[:, 5]
        h = rms_norm(x, norm_attn_g[l]) * (1.0 + scale_a) + shift_a
        proj = h @ w_in[l]
        pa, pb, pc, pd = jnp.split(proj, SPLIT_POINTS, axis=-1)
        lambda_init = 0.8 - 0.6 * math.exp(-0.3 * l)
        o_a = neighborhood_attention(pa, qkn_a[l], rpb_a[l])
        o_b = sliding_window_gqa(pb, qkn_b[l], sink_b[l], t5_table[:, T5_B0:T5_C0])
        o_c = dilated_attention(pc, qkn_c[l], t5_table[:, T5_C0:T5_D0])
        o_d = differential_attention(pd, qkn_d[l], lam_d[l], subln_d[l], t5_table[:, T5_D0:T5_HEADS], lambda_init)
        mixed = jnp.concatenate([o_a, o_b, o_c, o_d], axis=-1).astype(x.dtype)
        x = x + gate_a * (mixed @ w_out[l])
        h = rms_norm(x, norm_ffn_g[l]) * (1.0 + scale_f) + shift_f
        x = x + gate_f * conv_glu_ffn(h, w_up[l], conv_w[l], conv_b[l], w_down[l])
    return x


def setup_inputs(seed: int = 0) -> dict:
    key = jax.random.key(seed)
    ks = jax.random.split(key, 24)
    D = D_MODEL

    def nrm(k, shape, scale):
        return jax.random.normal(k, shape, jnp.float32) * scale

    return {
        'x_prompt': nrm(ks[0], (BATCH, SEQ, D), 1.0),
        'x_sample': nrm(ks[1], (DEC_BATCH, DEC_SEQ, D), 1.0),
        'c_prompt': nrm(ks[2], (BATCH, D), 1.0),
        'c_sample': nrm(ks[3], (DEC_BATCH, D), 1.0),
        'norm_attn_g': 1.0 + nrm(ks[4], (DEPTH, D), 0.05),
        'norm_ffn_g': 1.0 + nrm(ks[5], (DEPTH, D), 0.05),
        'w_ada': nrm(ks[6], (DEPTH, D, 6 * D), 0.5 * D ** -0.5),
        'b_ada': nrm(ks[7], (DEPTH, 6 * D), 0.02),
        'w_in': nrm(ks[8], (DEPTH, D, PROJ_WIDTH), D ** -0.5),
        'qkn_a': 1.0 + nrm(ks[9], (DEPTH, 2, HEAD_DIM), 0.05),
        'qkn_b': 1.0 + nrm(ks[10], (DEPTH, 2, HEAD_DIM), 0.05),
        'qkn_c': 1.0 + nrm(ks[11], (DEPTH, 2, HEAD_DIM), 0.05),
        'qkn_d': 1.0 + nrm(ks[12], (DEPTH, 2, DIFF_HALF), 0.05),
        'rpb_a': nrm(ks[13], (DEPTH, NA_HEADS, 2 * NA_WIN_H - 1, 2 * NA_WIN_W - 1), 0.1),
        'sink_b': nrm(ks[14], (DEPTH, SW_HEADS), 0.5),
        't5_table': nrm(ks[15], (T5_BUCKETS, T5_HEADS), 0.1),
        'lam_d': nrm(ks[16], (DEPTH, 4, DIFF_HALF), 0.1),
        'subln_d': 1.0 + nrm(ks[17], (DEPTH, HEAD_DIM), 0.05),
        'w_out': nrm(ks[18], (DEPTH, MIX_WIDTH, D), MIX_WIDTH ** -0.5),
        'w_up': nrm(ks[19], (DEPTH, D, 2 * D_FF), D ** -0.5),
        'conv_w': nrm(ks[20], (DEPTH, 3, 2 * D_FF), 0.3) + jnp.array([0.0, 1.0, 0.0], jnp.float32)[None, :, None],
        'conv_b': nrm(ks[21], (DEPTH, 2 * D_FF), 0.02),
        'w_down': nrm(ks[22], (DEPTH, D_FF, D), D_FF ** -0.5),
    }


def reference(x_prompt, x_sample, c_prompt, c_sample, norm_attn_g, norm_ffn_g, w_ada, b_ada, w_in,
              qkn_a, qkn_b, qkn_c, qkn_d, rpb_a, sink_b, t5_table, lam_d, subln_d, w_out, w_up,
              conv_w, conv_b, w_down):
    y_prompt = _trunk(x_prompt, c_prompt, norm_attn_g, norm_ffn_g, w_ada, b_ada, w_in, qkn_a, qkn_b,
                      qkn_c, qkn_d, rpb_a, sink_b, t5_table, lam_d, subln_d, w_out, w_up, conv_w,
                      conv_b, w_down)
    y_sample = _trunk(x_sample, c_sample, norm_attn_g, norm_ffn_g, w_ada, b_ada, w_in, qkn_a, qkn_b,
                      qkn_c, qkn_d, rpb_a, sink_b, t5_table, lam_d, subln_d, w_out, w_up, conv_w,
                      conv_b, w_down)
    return (y_prompt, y_sample)
```

```python
import math
from contextlib import ExitStack
import numpy as np
import concourse.bass as bass
import concourse.mybir as mybir
from concourse.bass_utils import run_bass_kernel_spmd

F32 = mybir.dt.float32
BF16 = mybir.dt.bfloat16
AF = mybir.ActivationFunctionType
ALU = mybir.AluOpType
AX = mybir.AxisListType
DBG = {}

D = 1024
N = 2048
DEPTH = 4
DFF = 2816
NJ = 22
EPS = 1e-6
MASKV = -30000.0
NPT = 6
PDEPTH = 3
QUARTERS = [(0, 6), (6, 6), (12, 5), (17, 5)]
FFN_BLOCKS = []
for _b in range(5):
    _o0 = 510 * _b
    _o1 = min(_o0 + 510, N)
    FFN_BLOCKS.append((_o0, _o1, max(_o0 - 1, 0), min(_o1 + 1, N)))

A0, B0, C0, D0 = 0, 768, 1280, 2432


def t5_bucket_np(rel):
    rel = np.asarray(rel, dtype=np.int32)
    nb = 16
    max_exact = 8
    base = np.where(rel > 0, nb, 0)
    n = np.abs(rel)
    nf = np.maximum(n, 1).astype(np.float32)
    large = max_exact + (np.log(nf / np.float32(max_exact)) / np.float32(math.log(128 / max_exact))
                         * np.float32(nb - max_exact)).astype(np.int32)
    large = np.minimum(large, nb - 1)
    return base + np.where(n < max_exact, n, large)


STRIPS = [
    ("B", 1, lambda r: np.abs(r) <= 128, 0, 4),
    ("C0", 1, lambda r: np.abs(r) <= 64, 4, 2),
    ("C1", 2, lambda r: (np.abs(r) <= 256) & (r % 4 == 0), 6, 2),
    ("C2", 8, lambda r: (np.abs(r) <= 1024) & (r % 16 == 0), 8, 2),
    ("D", 4, lambda r: np.ones_like(r, dtype=bool), 10, 4),
]
STRIP_W = {}
STRIP_OFF = {}
_off = 0
for _n, _band, _v, _hb, _nh in STRIPS:
    STRIP_W[_n] = 128 * (2 * _band + 1)
    STRIP_OFF[_n] = _off
    _off += STRIP_W[_n] + 128
F_LEN = _off
STRIP_INFO = {n: (band, hb, nh) for n, band, v, hb, nh in STRIPS}


def build_onehot():
    oh = np.zeros((33, F_LEN), np.float32)
    for name, band, valid, hb, nh in STRIPS:
        L = STRIP_W[name] + 128
        y = np.arange(L)
        rel = 127 + 128 * band - y
        ok = valid(rel)
        bk = t5_bucket_np(rel)
        idx = np.where(ok, bk, 32)
        oh[idx, STRIP_OFF[name] + y] = 1.0
    return oh


def _kt(w, cols):
    sub = w[:, cols]
    n = sub.shape[1]
    return sub.reshape(8, 128, n).transpose(1, 0, 2).reshape(128, 8 * n)


def layer_chunks(w_in, w_out, w_up, w_down):
    ch = []
    r = np.arange
    for p in range(2):
        ch.append((("A", p, "q"), _kt(w_in, A0 + 0 + p * 128 + r(128))))
        ch.append((("A", p, "k"), _kt(w_in, A0 + 256 + p * 128 + r(128))))
        ch.append((("A", p, "v"), _kt(w_in, A0 + 512 + p * 128 + r(128))))
        ch.append((("A", p, "o"), w_out[p * 128:(p + 1) * 128, :].reshape(128, 1024)))
    ch.append((("B", "q"), _kt(w_in, B0 + r(256))))
    kc = np.concatenate([B0 + 256 + r(64), B0 + 256 + r(64), B0 + 320 + r(64), B0 + 320 + r(64)])
    ch.append((("B", "k"), _kt(w_in, kc)))
    ch.append((("B", "v"), _kt(w_in, B0 + 384 + r(128))))
    ch.append((("B", "o"), w_out[256:512, :].reshape(2, 128, 1024).transpose(1, 0, 2).reshape(128, 2048)))
    ch.append((("C", "q0"), _kt(w_in, C0 + r(256))))
    ch.append((("C", "q1"), _kt(w_in, C0 + 256 + r(128))))
    ch.append((("C", "k0"), _kt(w_in, C0 + 384 + r(256))))
    ch.append((("C", "k1"), _kt(w_in, C0 + 640 + r(128))))
    ch.append((("C", "v0"), _kt(w_in, C0 + 768 + r(192))))
    ch.append((("C", "v1"), _kt(w_in, C0 + 960 + r(192))))
    ch.append((("C", "o0"), w_out[512:768, :].reshape(2, 128, 1024).transpose(1, 0, 2).reshape(128, 2048)))
    ch.append((("C", "o1"), w_out[768:896, :].reshape(128, 1024)))
    ch.append((("D", "q0"), _kt(w_in, D0 + r(128))))
    ch.append((("D", "k0"), _kt(w_in, D0 + 256 + r(128))))
    ch.append((("D", "v"), _kt(w_in, D0 + 512 + r(256))))
    ch.append((("D", "q1"), _kt(w_in, D0 + 128 + r(128))))
    ch.append((("D", "k1"), _kt(w_in, D0 + 384 + r(128))))
    ch.append((("D", "o"), w_out[896:1152, :].reshape(2, 128, 1024).transpose(1, 0, 2).reshape(128, 2048)))
    for qi, (j0, nj) in enumerate(QUARTERS):
        for jj in range(nj):
            j = j0 + jj
            cols = np.concatenate([j * 128 + r(128), DFF + j * 128 + r(128)])
            ch.append((("U", j), _kt(w_up, cols)))
        wd = w_down[j0 * 128:(j0 + nj) * 128, :].reshape(nj, 128, 8, 128)
        for mp in range(4):
            blk = wd[:, :, 2 * mp:2 * mp + 2, :].transpose(1, 2, 0, 3).reshape(128, 2 * nj * 128)
            ch.append((("Dn", qi, mp), blk))
    return ch


def chunk_plan():
    z_in = np.zeros((1024, 1), np.float32)

    class _Z:
        def __init__(self, shape):
            self.shape = shape

    plan = []
    for p in range(2):
        plan += [(("A", p, "q"), 1024), (("A", p, "k"), 1024), (("A", p, "v"), 1024), (("A", p, "o"), 1024)]
    plan += [(("B", "q"), 2048), (("B", "k"), 2048), (("B", "v"), 1024), (("B", "o"), 2048)]
    plan += [(("C", "q0"), 2048), (("C", "q1"), 1024), (("C", "k0"), 2048), (("C", "k1"), 1024),
             (("C", "v0"), 1536), (("C", "v1"), 1536), (("C", "o0"), 2048), (("C", "o1"), 1024)]
    plan += [(("D", "q0"), 1024), (("D", "k0"), 1024), (("D", "v"), 2048), (("D", "q1"), 1024), (("D", "k1"), 1024),
             (("D", "o"), 2048)]
    for qi, (j0, nj) in enumerate(QUARTERS):
        for jj in range(nj):
            plan.append((("U", j0 + jj), 2048))
        for mp in range(4):
            plan.append((("Dn", qi, mp), 2 * nj * 128))
    return plan


PLAN = chunk_plan()
PLAN_OFF = {}
_o = 0
for _k, _n in PLAN:
    PLAN_OFF[_k] = (_o, _n)
    _o += _n
WL_COLS = _o

VC = {}
_c = 0


def _vc(name, n):
    global _c
    VC[name] = (_c, n)
    _c += n


_vc("b_ada", DEPTH * 48)
_vc("gnorm", DEPTH * 2 * 8)
_vc("qkg", DEPTH * 4 * 2)
_vc("conv", DEPTH * 4 * 44)
_vc("t5c", 8)
_vc("sink", DEPTH * 4)
_vc("linit", DEPTH)
_vc("lam", DEPTH * 4 * 32)
_vc("subln", DEPTH * 64)
NVEC = _c


def lambda_init(l):
    return 0.8 - 0.6 * math.exp(-0.3 * l)


class Tok:
    __slots__ = ("w", "r")

    def __init__(self):
        self.w = None
        self.r = {}


EPOCH = 16000


class _Eng:
    def __init__(self, name):
        self.name = name
        self.items = []
        self.count = 0
        self.epoch = 0
        self.finals = {}
        self.seen = {}

    def key(self):
        return ("e", self.name, self.epoch)

    def bump(self):
        if self.count >= EPOCH:
            self.finals[self.key()] = self.count
            self.epoch += 1
            self.count = 0
        self.count += 1
        return (self.key(), self.count)


class Sched:
    ENG = ("pe", "act", "dve", "pool", "sp")

    def __init__(self):
        self.E = {n: _Eng(n) for n in self.ENG}
        self.toks = {}
        self.dry = False
        self.dma_key = {}
        self.dma_cnt = {}
        self.out_keys = set()

    def tok(self, *key):
        t = self.toks.get(key)
        if t is None:
            t = self.toks[key] = Tok()
        return t

    def _deps(self, reads, writes):
        deps = {}
        for t in reads:
            if t.w is not None and deps.get(t.w[0], 0) < t.w[1]:
                deps[t.w[0]] = t.w[1]
        for t in writes:
            if t.w is not None and deps.get(t.w[0], 0) < t.w[1]:
                deps[t.w[0]] = t.w[1]
            for k, v in t.r.items():
                if deps.get(k, 0) < v:
                    deps[k] = v
        return deps

    def _waits(self, en, deps):
        eng = self.E[en]
        for k, v in deps.items():
            if en == "pe" and k[0] == "e" and k[1] == "pe":
                continue
            if eng.seen.get(k, 0) >= v:
                continue
            eng.seen[k] = v
            eng.items.append(("w", k, v))

    def op(self, en, fn, reads=(), writes=()):
        if self.dry:
            return
        eng = self.E[en]
        self._waits(en, self._deps(reads, writes))
        me = eng.bump()
        key = me[0]
        eng.items.append(("i", fn, key))
        for t in writes:
            t.w = me
            t.r = {}
        for t in reads:
            if t.r.get(key, 0) < me[1]:
                t.r[key] = me[1]

    def dma(self, qn, fn, n, reads=(), writes=(), owner=None, is_out=False):
        if self.dry:
            return
        eng = self.E[qn]
        self._waits(qn, self._deps(reads, writes))
        own = owner if owner is not None else writes[0]
        key = self.dma_key.get(id(own))
        if key is None:
            key = ("d", len(self.dma_key))
            self.dma_key[id(own)] = key
            self.dma_cnt[key] = 0
        self.dma_cnt[key] += 16 * n
        cnt = self.dma_cnt[key]
        if is_out:
            self.out_keys.add(key)
        eng.items.append(("d", fn, key))
        me = (key, cnt)
        for t in writes:
            t.w = me
            t.r = {}
        for t in reads:
            if t.r.get(key, 0) < cnt:
                t.r[key] = cnt

    def fence(self, toks):
        if self.dry:
            return
        deps = self._deps((), toks)
        for t in toks:
            t.w = None
            t.r = dict(deps)

    def barrier(self):
        if self.dry:
            return
        allv = {}
        for n, e in self.E.items():
            for k, v in e.finals.items():
                allv[k] = v
            if e.count:
                allv[e.key()] = e.count
        for k, v in self.dma_cnt.items():
            if v:
                allv[k] = v
        for n in self.ENG:
            self._waits(n, dict(allv))

    def final_wait(self):
        deps = {k: self.dma_cnt[k] for k in self.out_keys}
        self._waits("sp", deps)

    def all_keys(self):
        keys = []
        for n in self.ENG:
            e = self.E[n]
            keys += [("e", n, ep) for ep in range(e.epoch + 1)]
        keys += list(self.dma_cnt.keys())
        return keys


class Pipe:
    def __init__(self, depth=2, pdelay=2):
        self.depth = depth
        self.pdelay = pdelay
        self.q = []
        self.pq = []

    def push(self, s1, s23, post=None):
        s1()
        self.q.append((s23, post))
        while len(self.q) > self.depth:
            self._pop()

    def _pop(self):
        s23, post = self.q.pop(0)
        s23()
        self.pq = [(c - 1, p) for c, p in self.pq]
        while self.pq and self.pq[0][0] <= 0:
            self.pq.pop(0)[1]()
        if post is not None:
            self.pq.append((self.pdelay, post))

    def flush(self):
        while self.q:
            self._pop()
        while self.pq:
            self.pq.pop(0)[1]()


def seq(*fns):
    def f(h):
        r = None
        for g in fns:
            r = g(h)
        return r
    return f


class Builder:
    def __init__(self, nseq, layers, mixers=("A", "B", "C", "D"), ffn=True):
        self.nseq = nseq
        self.layers = layers
        self.mixers = mixers
        self.ffn = ffn
        self.S = Sched()
        self.nc = bass.Bass("TRN2", target_bir_lowering=False)
        self.wq = []
        self.wi = 0
        self.w_issued = 0
        self.RING = 3
        self._banks_rr = {}

    def setup_mem(self):
        nc = self.nc
        ns = self.nseq
        dt = nc.dram_tensor
        self.x_d = dt("x", [ns, N, D], F32, kind="ExternalInput").ap()
        self.y_d = dt("y", [ns, N, D], F32, kind="ExternalOutput").ap()
        self.cT_d = dt("cT", [128, 8 * ns], F32, kind="ExternalInput").ap()
        self.wada_d = dt("wada", [DEPTH * 48, 128, 1024], F32, kind="ExternalInput").ap()
        self.wl_d = dt("wl", [DEPTH, 128, WL_COLS], F32, kind="ExternalInput").ap()
        self.vecs_d = dt("vecs", [128, NVEC], F32, kind="ExternalInput").ap()
        self.cst_d = dt("cst", [128, 128 * 7], F32, kind="ExternalInput").ap()
        self.t5aug_d = dt("t5aug", [33, 14], F32, kind="ExternalInput").ap()
        self.oh_d = dt("oh", [33, F_LEN], F32, kind="ExternalInput").ap()
        self.rpb_d = dt("rpbrp", [16, 15 * 127], F32, kind="ExternalInput").ap()
        self.Fd = dt("Fd", [14, F_LEN], BF16, kind="Internal").ap()
        self.RAd = dt("RAd", [DEPTH, 128, 2 * 960], BF16, kind="Internal").ap()
        self.modD = dt("modD", [ns, 128, DEPTH * 48], F32, kind="Internal").ap()
        self.wlb = dt("wlb", [DEPTH, 128, WL_COLS], BF16, kind="Internal").ap()

        total_bytes = 207 * 1024
        self.arena = nc.alloc_sbuf_tensor("arena", [128, total_bytes // 2], BF16)
        self._aoff = 0

        def carve(nbytes, dtype, shape):
            assert nbytes % 4 == 0
            o = self._aoff
            self._aoff += nbytes
            assert self._aoff <= total_bytes, ("SBUF overflow", self._aoff)
            ap = self.arena[:, o // 2:(o + nbytes) // 2]
            if dtype == F32:
                ap = ap.bitcast(F32)
            return self._shape(ap, shape)

        self.carve = carve
        K = 1024
        self.xT = carve(64 * K, F32, [8, N])
        self.hT = carve(32 * K, BF16, [8, N])
        self.identF = carve(512, F32, [128])
        self.cbf = carve(6 * 256, BF16, [6, 128])
        self.vecs = carve(NVEC * 4, F32, [NVEC])
        self.modc = carve(DEPTH * 48 * 4, F32, [DEPTH * 48])
        self.small = carve(256 * 4, F32, [256])
        self.ring = [carve(4 * K, BF16, [2048]) for _ in range(self.RING)]
        self.sq = [carve(1 * K, BF16, [512]) for _ in range(2)]
        self.lnt = carve(2 * K, F32, [512])
        self.rstd = [carve(2 * K, F32, [512]) for _ in range(2)]
        self.tmpf = [carve(2 * K, F32, [512]) for _ in range(2)]
        self.ov0 = self._aoff
        self.mixedT = carve(12 * K, BF16, [3, N])
        self.QT = carve(12 * K, BF16, [3, N])
        self.KT = carve(12 * K, BF16, [3, N])
        self.Vr = carve(12480, BF16, [6240])
        self.PT = [carve(1 * K, BF16, [512]) for _ in range(NPT)]
        self.ostage = [carve(2 * K, BF16, [1024]) for _ in range(2)]
        self.strip = carve(12800, BF16, [6400])
        self.att_end = self._aoff
        self._aoff = self.ov0
        self.gT = carve(24 * K, BF16, [6, N])
        self.acc = [carve(4 * K, F32, [2, 512]) for _ in range(4)]
        self.sg = [carve(2 * K, F32, [512]) for _ in range(4)]
        self.xstage = [carve(4 * K, F32, [1024]) for _ in range(2)]
        self._aoff = self.ov0
        self.oh_sb = carve(F_LEN * 4, F32, [F_LEN])
        self.f_sb = carve(F_LEN * 2, BF16, [F_LEN])
        self.t5aug_sb = carve(64, F32, [14])
        self.cst_sb = carve(128 * 7 * 4, F32, [7, 128])
        self.ra_f = carve(15 * 64 * 4, F32, [15, 64])
        self.ra_b = [carve(960 * 2, BF16, [960]) for _ in range(2)]
        self.wada_sb = [carve(4 * K, F32, [8, 128]) for _ in range(2)]
        self.modT = carve(DEPTH * 48 * ns * 4, F32, [ns, DEPTH * 48])
        self.sc = carve(8 * ns * 4, F32, [8, ns])
        self._aoff = max(self.att_end, self._aoff)
        self.sbuf_used = self._aoff
        self.psum = nc.alloc_psum_tensor("ps", [128, 8, 512], F32)

    @staticmethod
    def _shape(ap, shape):
        if len(shape) == 1:
            return ap
        names = "abcdef"[:len(shape)]
        s = "p (" + " ".join(names) + ") -> p " + " ".join(names)
        kw = {names[i]: shape[i] for i in range(len(shape) - 1)}
        return ap.rearrange(s, **kw)

    class Bank:
        def __init__(self, b, idx):
            self.idx = idx
            self.f32 = b.psum[:, idx, :]
            self.bf = b.psum[:, idx, :].bitcast(BF16)
            self.tok = b.S.tok("ps", idx)
            self.fresh = True

        def first(self):
            f = self.fresh
            self.fresh = False
            return f

    def banks_init(self):
        self.banks = [Builder.Bank(self, i) for i in range(8)]
        self.pools = {"S": [0, 1, 2, 3], "ACC": [4, 5, 6], "AUX": [7], "ALL7": [0, 1, 2, 3, 4, 5, 6],
                      "ALL8": [0, 1, 2, 3, 4, 5, 6, 7]}

    def bank(self, pool):
        lst = self.pools[pool]
        i = self._banks_rr.get(pool, 0)
        self._banks_rr[pool] = i + 1
        b = self.banks[lst[i % len(lst)]]
        b.fresh = True
        return b

    def MM(self, bank, out, lhsT, rhs, **kw):
        st = bank.first()
        return lambda h: h.matmul(out, lhsT, rhs, start=st, stop=True, skip_group_check=True, **kw)

    def wnext(self, l, key):
        if self.S.dry:
            self.wq.append((l, key))
            return self.ring[0], self.S.tok("ring", 0)
        i = self.wi
        assert self.wq[i] == (l, key), (self.wq[i], l, key)
        self.wi += 1
        while self.w_issued < min(len(self.wq), i + self.RING):
            self._wissue(self.w_issued)
            self.w_issued += 1
        s = i % self.RING
        return self.ring[s], self.S.tok("ring", s)

    def _wissue(self, i):
        l, key = self.wq[i]
        s = i % self.RING
        off, n = PLAN_OFF[key]
        dst = self.ring[s][:, 0:n]
        src = self.wlb[l, :, off:off + n]
        t = self.S.tok("ring", s)
        self.S.dma("sp", lambda h, sem, dst=dst, src=src: h.dma_start(out=dst, in_=src).then_inc(sem, 16),
                   1, writes=[t])

    def body(self):
        S = self.S
        S.dma("sp", lambda h, sem: h.dma_start(out=self.modc, in_=self.modD[self.loop_i]).then_inc(sem, 16), 1,
              writes=[S.tok("modc")])
        for en in ("dve", "act", "pool", "pe"):
            S.op(en, lambda h: h.nop(), reads=[S.tok("modc")])
        self.load_x(None)
        for l in self.layers:
            self.layer(None, l)
        self.store_x(None)

    def cb(self, i):
        return self.cbf[:, i, :]

    def vec(self, name, i=0, n=1):
        o, _ = VC[name]
        return self.vecs[:, o + i:o + i + n]

    def prologue(self):
        S = self.S
        ns = self.nseq
        tP = S.tok("pro")
        ld = lambda dst, src: (lambda h, sem: h.dma_start(out=dst, in_=src).then_inc(sem, 16))
        S.dma("sp", ld(self.vecs, self.vecs_d), 1, writes=[S.tok("vecs")])
        S.dma("sp", ld(self.cst_sb, self.cst_d.rearrange("p (a b) -> p a b", a=7)), 1, writes=[S.tok("cst")])
        S.dma("sp", ld(self.sc, self.cT_d.rearrange("p (a b) -> p a b", a=8)), 1, writes=[S.tok("sc")])
        S.dma("sp", ld(self.t5aug_sb[0:33, 0:14], self.t5aug_d), 1, writes=[S.tok("t5aug")])
        S.dma("sp", ld(self.oh_sb[0:33, :], self.oh_d), 1, writes=[S.tok("oh")])
        S.op("dve", lambda h: h.tensor_copy(out=self.identF, in_=self.cst_sb[:, 0, :]),
             reads=[S.tok("cst")], writes=[S.tok("c0")])
        S.op("dve", lambda h: h.tensor_copy(out=self.cbf, in_=self.cst_sb[:, 1:7, :]),
             reads=[S.tok("cst")], writes=[S.tok("c1")])
        S.op("act", lambda h: h.activation(out=self.sc, in_=self.sc, func=AF.Silu),
             reads=[S.tok("sc")], writes=[S.tok("sc")])
        for l in self.layers:
            for j in range(48):
                i = l * 48 + j
                wb = self.wada_sb[i % 2]
                tw = S.tok("wada", i % 2)
                S.dma("sp", ld(wb, self.wada_d[i].rearrange("p (a b) -> p a b", a=8)), 1, writes=[tw])
                bk = self.bank("ALL7")
                fns = [self.MM(bk, bk.f32[:, 0:ns], wb[:, kc, :], self.sc[:, kc, :]) for kc in range(8)]
                S.op("pe", seq(*fns), reads=[tw, S.tok("sc")], writes=[bk.tok])
                S.op("dve", lambda h, bk=bk, i=i: h.tensor_scalar(
                    out=self.modT[:, :, i], in0=bk.f32[:, 0:ns], scalar1=self.vec("b_ada", i), scalar2=None,
                    op0=ALU.add), reads=[bk.tok, S.tok("vecs")], writes=[S.tok("modT")])
        for c0 in range(0, F_LEN, 512):
            cn = min(512, F_LEN - c0)
            bk = self.bank("ALL7")
            S.op("pe", self.MM(bk, bk.f32[0:14, 0:cn], self.t5aug_sb[0:33, 0:14], self.oh_sb[0:33, c0:c0 + cn]),
                 reads=[S.tok("t5aug"), S.tok("oh")], writes=[bk.tok])
            S.op("dve", lambda h, bk=bk, c0=c0, cn=cn: h.tensor_copy(out=self.f_sb[0:14, c0:c0 + cn], in_=bk.f32[0:14, 0:cn]),
                 reads=[bk.tok], writes=[S.tok("fsb")])
        S.dma("sp", ld(self.Fd, self.f_sb[0:14, :]), 1, reads=[S.tok("fsb")], writes=[S.tok("Fd")])
        if "A" in self.mixers and not DBG.get("A_nostrip"):
            maskA = self.cst_sb[0:64, 0, :]
            for l in self.layers:
                for hh in range(4):
                    i = l * 4 + hh
                    pb = 64 * (hh % 2)

                    def ldra(h, sem, i=i, pb=pb):
                        for dr in range(15):
                            src = bass.AP(self.rpb_d.tensor, (i * 15 + dr) * 127, [[1, 64], [1, 64]])
                            h.dma_start(out=self.ra_f[pb:pb + 64, dr, :], in_=src).then_inc(sem, 16)
                    S.dma("sp", ldra, 15, writes=[S.tok("raf")])
                    rb = self.ra_b[i % 2]
                    S.op("dve", lambda h, rb=rb, pb=pb: h.tensor_tensor(
                        out=rb[pb:pb + 64, :].rearrange("p (a b) -> p a b", a=15), in0=self.ra_f[pb:pb + 64, :, :],
                        in1=self.cst_sb[pb:pb + 64, 6, 64:128].unsqueeze(1).broadcast_to([64, 15, 64]), op=ALU.add),
                        reads=[S.tok("raf"), S.tok("cst")], writes=[S.tok("rab", i % 2)])
                    S.dma("sp", ld(self.RAd[l, pb:pb + 64, (hh // 2) * 960:(hh // 2 + 1) * 960], rb[pb:pb + 64, :]), 1,
                          reads=[S.tok("rab", i % 2)], writes=[S.tok("RAd", i % 2)])
        sm = self.small
        o, _ = VC["qkg"]
        for l in range(DEPTH):
            for m in range(4):
                dd = 32.0 if m == 3 else 64.0
                c = (l * 4 + m) * 2
                S.op("dve", lambda h, c=c: h.tensor_copy(out=sm[:, c:c + 1], in_=self.vecs[:, o + c:o + c + 1]),
                     reads=[S.tok("vecs")], writes=[tP])
                S.op("dve", lambda h, c=c, dd=dd: h.tensor_scalar(
                    out=sm[:, c + 1:c + 2], in0=self.vecs[:, o + c + 1:o + c + 2], scalar1=math.sqrt(dd),
                    scalar2=None, op0=ALU.mult), reads=[S.tok("vecs")], writes=[tP])
        for l in range(DEPTH):
            for b in range(4):
                kc_ = o + (l * 4 + 3) * 2 + 1
                S.op("dve", lambda h, l=l, b=b, kc_=kc_: h.tensor_scalar(
                    out=sm[:, 160 + l * 4 + b:161 + l * 4 + b], in0=self.cst_sb[:, 5, 32 * b:32 * b + 1],
                    scalar1=self.vecs[:, kc_:kc_ + 1], scalar2=math.sqrt(32.0), op0=ALU.mult, op1=ALU.mult),
                    reads=[S.tok("vecs"), S.tok("cst")], writes=[tP])
        S.op("act", lambda h: h.activation(out=sm[:, 32:48], in_=self.vec("sink", 0, 16), func=AF.Exp),
             reads=[S.tok("vecs")], writes=[tP])
        lamv = self.vec("lam", 0, DEPTH * 128).rearrange("p (l f e) -> p l f e", l=DEPTH, f=4)
        S.op("dve", lambda h: h.tensor_tensor(out=self.tmpf[0][:, 0:DEPTH * 32].rearrange("p (l e) -> p l e", l=DEPTH),
                                              in0=lamv[:, :, 0, :], in1=lamv[:, :, 1, :], op=ALU.mult),
             reads=[S.tok("vecs")], writes=[S.tok("tmpf", 0)])
        S.op("dve", lambda h: h.tensor_tensor(out=self.tmpf[1][:, 0:DEPTH * 32].rearrange("p (l e) -> p l e", l=DEPTH),
                                              in0=lamv[:, :, 2, :], in1=lamv[:, :, 3, :], op=ALU.mult),
             reads=[S.tok("vecs")], writes=[S.tok("tmpf", 1)])
        S.op("dve", lambda h: h.tensor_reduce(out=sm[:, 52:56],
                                              in_=self.tmpf[0][:, 0:DEPTH * 32].rearrange("p (l e) -> p l e", l=DEPTH),
                                              axis=AX.X, op=ALU.add), reads=[S.tok("tmpf", 0)], writes=[tP])
        S.op("dve", lambda h: h.tensor_reduce(out=sm[:, 56:60],
                                              in_=self.tmpf[1][:, 0:DEPTH * 32].rearrange("p (l e) -> p l e", l=DEPTH),
                                              axis=AX.X, op=ALU.add), reads=[S.tok("tmpf", 1)], writes=[tP])
        S.op("act", lambda h: h.activation(out=sm[:, 52:60], in_=sm[:, 52:60], func=AF.Exp), reads=[tP], writes=[tP])
        S.op("dve", lambda h: h.tensor_tensor(out=sm[:, 48:52], in0=sm[:, 56:60], in1=sm[:, 52:56], op=ALU.subtract),
             reads=[tP], writes=[tP])
        S.op("dve", lambda h: h.tensor_tensor(out=sm[:, 48:52], in0=sm[:, 48:52], in1=self.vec("linit", 0, DEPTH),
                                              op=ALU.subtract), reads=[tP, S.tok("vecs")], writes=[tP])
        S.op("dve", lambda h: h.tensor_scalar(out=sm[:, 64:128], in0=self.vec("gnorm", 0, 64), scalar1=32.0,
                                              scalar2=None, op0=ALU.mult), reads=[S.tok("vecs")], writes=[tP])
        for l in self.layers:
            pieces = []
            for c0 in range(0, WL_COLS, 8192):
                cn = min(8192, WL_COLS - c0)
                pieces.append((self.wlb[l, :, c0:c0 + cn], self.wl_d[l, :, c0:c0 + cn]))

            def cast(h, sem, pieces=pieces):
                for dst, src in pieces:
                    h.dma_start(out=dst, in_=src).then_inc(sem, 16)
            S.dma("pool", cast, len(pieces), writes=[S.tok("wlb", l)])
        for s_ in range(ns):
            S.dma("sp", ld(self.modD[s_], self.modT[:, s_, :]), 1, reads=[S.tok("modT")], writes=[S.tok("modD", s_)])
        S.barrier()

    def qkg(self, l, m, which):
        c = (l * 4 + m) * 2 + which
        return self.small[:, c:c + 1]

    def mod(self, l, i, s):
        return self.modc[:, l * 48 + i * 8:l * 48 + i * 8 + 8]

    def ov_fence(self):
        S = self.S
        if S.dry:
            return
        toks = [t for k, t in S.toks.items() if k[0] in ("QT", "KT", "V", "mixedT", "PT", "ost", "strip", "gT", "acc",
                                                          "sg", "xst")]
        S.fence(toks)

    def load_x(self, s):
        S = self.S
        self.ov_fence()
        for t in range(16):
            st = self.xstage[t % 2]
            ts = S.tok("xst", t % 2)
            S.dma("sp", lambda h, sem, st=st, t=t: h.dma_start(
                out=st, in_=self.x_d[self.loop_i, t * 128:(t + 1) * 128, :]).then_inc(sem, 16), 1, writes=[ts])
            for half in range(2):
                bk = self.bank("ALL7")
                fns = [(lambda h, bk=bk, st=st, c=c, half=half: h.transpose(
                    bk.f32[:, c * 128:(c + 1) * 128], st[:, (half * 4 + c) * 128:(half * 4 + c + 1) * 128],
                    self.identF)) for c in range(4)]
                S.op("pe", seq(*fns), reads=[ts], writes=[bk.tok])
                eng = "dve" if half == 0 else "act"
                dst = self.xT[:, half * 4:half * 4 + 4, t * 128:(t + 1) * 128]
                srcp = bk.f32.rearrange("p (a b) -> p a b", a=4)
                if eng == "dve":
                    S.op("dve", lambda h, dst=dst, srcp=srcp: h.tensor_copy(out=dst, in_=srcp),
                         reads=[bk.tok], writes=[S.tok("xT", t // 4)])
                else:
                    S.op("act", lambda h, dst=dst, srcp=srcp: h.activation(out=dst, in_=srcp, func=AF.Copy),
                         reads=[bk.tok], writes=[S.tok("xT", t // 4)])

    def store_x(self, s):
        S = self.S
        self.ov_fence()
        for t in range(16):
            st = self.xstage[t % 2]
            ts = S.tok("xst", t % 2)
            for half in range(2):
                bk = self.bank("ALL7")
                fns = [(lambda h, bk=bk, c=c, half=half, t=t: h.transpose(
                    bk.f32[:, c * 128:(c + 1) * 128], self.xT[:, half * 4 + c, t * 128:(t + 1) * 128],
                    self.identF)) for c in range(4)]
                S.op("pe", seq(*fns), reads=[S.tok("xT", t // 4)], writes=[bk.tok])
                dst = st[:, half * 512:(half + 1) * 512]
                if half == 0:
                    S.op("dve", lambda h, dst=dst, bk=bk: h.tensor_copy(out=dst, in_=bk.f32), reads=[bk.tok], writes=[ts])
                else:
                    S.op("act", lambda h, dst=dst, bk=bk: h.activation(out=dst, in_=bk.f32, func=AF.Copy),
                         reads=[bk.tok], writes=[ts])
            S.dma("sp", lambda h, sem, st=st, t=t: h.dma_start(
                out=self.y_d[self.loop_i, t * 128:(t + 1) * 128, :], in_=st).then_inc(sem, 16), 1,
                  reads=[ts], writes=[S.tok("ydram", t % 2)], is_out=True)

    def norm(self, l, s, which):
        S = self.S
        sm = self.small
        ga = sm[:, 128 + which * 8:128 + which * 8 + 8]
        tg = S.tok("ga", which)
        gn = sm[:, 64 + (l * 2 + which) * 8:64 + (l * 2 + which) * 8 + 8]
        scale = self.mod(l, 1 + 3 * which, s)
        shift = self.mod(l, 0 + 3 * which, s)
        S.op("dve", lambda h: h.scalar_tensor_tensor(out=ga, in0=scale, scalar=1.0, in1=gn, op0=ALU.add, op1=ALU.mult),
             writes=[tg])
        ones = self.cb(2)
        for tb in range(4):
            blk = slice(tb * 512, (tb + 1) * 512)
            bk = self.bank("AUX")
            for c in range(8):
                sq = self.sq[c % 2]
                tq = S.tok("sq", c % 2)
                S.op("act", lambda h, sq=sq, c=c, blk=blk: h.activation(out=sq, in_=self.xT[:, c, blk], func=AF.Square),
                     reads=[S.tok("xT", tb)], writes=[tq])
                S.op("pe", self.MM(bk, bk.f32, ones, sq), reads=[tq], writes=[bk.tok])
            rs = self.rstd[tb % 2]
            tr = S.tok("rstd", tb % 2)
            S.op("act", lambda h, bk=bk: h.activation(out=self.lnt, in_=bk.f32, func=AF.Ln, bias=self.epsD, scale=1.0),
                 reads=[bk.tok], writes=[S.tok("lnt")])
            S.op("act", lambda h, rs=rs: h.activation(out=rs, in_=self.lnt, func=AF.Exp, scale=-0.5),
                 reads=[S.tok("lnt")], writes=[tr])
            for c in range(8):
                tf = self.tmpf[c % 2]
                tt = S.tok("tmpf", c % 2)
                S.op("dve", lambda h, tf=tf, c=c, blk=blk, rs=rs: h.tensor_tensor(
                    out=tf, in0=self.xT[:, c, blk], in1=rs, op=ALU.mult),
                    reads=[S.tok("xT", tb), tr], writes=[tt])
                S.op("act", lambda h, tf=tf, c=c, blk=blk: h.activation(
                    out=self.hT[:, c, blk], in_=tf, func=AF.Identity, bias=shift[:, c:c + 1], scale=ga[:, c:c + 1]),
                    reads=[tt, tg], writes=[S.tok("hT", tb)])

    def proj_qk(self, wslot, wtok, wcol0, dst, dtok_name, dchunk, gvec, bd, dsz, multi=None):
        S = self.S
        w = wslot[:, 0:8 * 0 + 2048]
        for tb in range(4):
            blk = slice(tb * 512, (tb + 1) * 512)
            bk = self.bank("S")
            fns = [self.MM(bk, bk.f32, self._w(wslot, kc, wcol0, 128), self.hT[:, kc, blk]) for kc in range(8)]
            S.op("pe", seq(*fns), reads=[wtok, S.tok("hT", tb)], writes=[bk.tok])
            sq = self.sq[tb % 2]
            tq = S.tok("sq", tb % 2)
            S.op("act", lambda h, sq=sq, bk=bk: h.activation(out=sq, in_=bk.f32, func=AF.Square),
                 reads=[bk.tok], writes=[tq])
            b2 = self.bank("AUX")
            S.op("pe", self.MM(b2, b2.f32, bd, sq), reads=[tq], writes=[b2.tok])
            rs = self.rstd[tb % 2]
            tr = S.tok("rstd", tb % 2)
            epsb = self.eps64 if dsz == 64 else self.eps32
            S.op("act", lambda h, b2=b2, epsb=epsb: h.activation(out=self.lnt, in_=b2.f32, func=AF.Ln, bias=epsb, scale=1.0),
                 reads=[b2.tok], writes=[S.tok("lnt")])
            S.op("act", lambda h, rs=rs: h.activation(out=rs, in_=self.lnt, func=AF.Exp, scale=-0.5),
                 reads=[S.tok("lnt")], writes=[tr])
            if multi is None:
                S.op("dve", lambda h, bk=bk, rs=rs, blk=blk: h.scalar_tensor_tensor(
                    out=dst[:, dchunk, blk], in0=bk.f32, scalar=gvec, in1=rs, op0=ALU.mult, op1=ALU.mult),
                    reads=[bk.tok, tr], writes=[S.tok(dtok_name, dchunk, tb)])
            else:
                for (dv, tn, ti, gcol) in multi:
                    S.op("dve", lambda h, bk=bk, rs=rs, blk=blk, dv=dv, gcol=gcol: h.scalar_tensor_tensor(
                        out=dv[:, blk], in0=bk.f32, scalar=gcol, in1=rs, op0=ALU.mult, op1=ALU.mult),
                        reads=[bk.tok, tr], writes=[S.tok(tn, ti, tb)])

    def _w(self, wslot, kc, col0, ncols, wtot=None):
        return wslot[:, 0:8 * self._wtot].rearrange("p (a b) -> p a b", a=8)[:, kc, col0:col0 + ncols]

    def proj_v(self, wslot, wtok, ncols, nh, V, hoff, rows64=False):
        S = self.S
        if not rows64:
            for t in range(16):
                bk = self.bank("S")
                fns = [self.MM(bk, bk.f32[:, 0:ncols], self.hT[:, kc, t * 128:(t + 1) * 128],
                               self._w(wslot, kc, 0, ncols)) for kc in range(8)]
                S.op("pe", seq(*fns), reads=[wtok, S.tok("hT", t // 4)], writes=[bk.tok])
                S.op("dve", lambda h, bk=bk, t=t: h.tensor_copy(
                    out=V[:, t, hoff:hoff + nh, 0:64], in_=bk.f32[:, 0:ncols].rearrange("p (a b) -> p a b", a=nh)),
                    reads=[bk.tok], writes=[S.tok("V", t // 4)])
        else:
            for r2 in range(16):
                bk = self.bank("S")
                fns = []
                for rr in range(2):
                    r = 2 * r2 + rr
                    fns += [self.MM(bk, bk.f32[0:64, rr * ncols:(rr + 1) * ncols], self.hT[:, kc, r * 64:(r + 1) * 64],
                                    self._w(wslot, kc, 0, ncols)) for kc in range(8)]
                S.op("pe", seq(*fns), reads=[wtok, S.tok("hT", r2 // 4)], writes=[bk.tok])
                S.op("dve", lambda h, bk=bk, r2=r2: h.tensor_copy(
                    out=V[0:64, 2 * r2:2 * r2 + 2, :, 0:64],
                    in_=bk.f32[0:64, 0:2 * ncols].rearrange("p (r a b) -> p r a b", r=2, a=nh)),
                    reads=[bk.tok], writes=[S.tok("V", r2 // 4)])

    def ones_col(self, V, nparts=128):
        S = self.S
        vt = [S.tok("V", i) for i in range(4)]
        S.op("pool", lambda h: h.memset(V[0:nparts, :, :, 64:65], 1.0), writes=vt)

    def wout_pass(self, l, s, slots, nks, kbase0=0, tb_outer=False):
        S = self.S
        gate = self.mod(l, 2, s)
        order = [(m, tb) for tb in range(4) for m in range(8)] if tb_outer else \
                [(m, tb) for m in range(8) for tb in range(4)]
        for (m, tb) in order:
            if True:
                blk = slice(tb * 512, (tb + 1) * 512)
                bk = self.bank("ALL7")
                fns = []
                reads = []
                kbase = kbase0
                for (slot, tk, nk) in slots:
                    wv = slot[:, 0:nk * 1024].rearrange("p (a b) -> p a b", a=nk)
                    for kc in range(nk):
                        fns.append(self.MM(bk, bk.f32, wv[:, kc, m * 128:(m + 1) * 128], self.mixedT[:, kbase + kc, blk]))
                        reads.append(S.tok("mixedT", kbase + kc, tb))
                    reads.append(tk)
                    kbase += nk
                S.op("pe", seq(*fns), reads=reads, writes=[bk.tok])
                S.op("dve", lambda h, bk=bk, m=m, blk=blk: h.scalar_tensor_tensor(
                    out=self.xT[:, m, blk], in0=bk.f32, scalar=gate[:, m:m + 1], in1=self.xT[:, m, blk],
                    op0=ALU.mult, op1=ALU.add), reads=[bk.tok], writes=[S.tok("xT", tb)])

    def transposes(self, src_fn, ntile_cols, nchunks, dst_cols, tb, src_tok, kparts=128, c0=0):
        S = self.S
        bk = self.bank("AUX")
        ident = self.cb(0)
        fns = [(lambda h, c=c: h.transpose(bk.bf[:, c * ntile_cols:(c + 1) * ntile_cols], src_fn(c),
                                           ident[0:kparts, 0:kparts])) for c in range(nchunks)]
        S.op("pe", seq(*fns), reads=[src_tok], writes=[bk.tok])
        S.op("act", lambda h: h.activation(
            out=self.mixedT[:, c0:c0 + nchunks, dst_cols],
            in_=bk.bf[:, 0:nchunks * ntile_cols].rearrange("p (a b) -> p a b", a=nchunks), func=AF.Copy),
            reads=[bk.tok], writes=[S.tok("mixedT", c0 + c, tb) for c in range(nchunks)])

    def load_strip(self, names):
        S = self.S
        ts = S.tok("strip")
        views = {}
        fl = []
        off = 0
        for nm in names:
            band, hb, nh = STRIP_INFO[nm]
            W = STRIP_W[nm]
            v = self.strip[:, off:off + nh * W].rearrange("p (a b) -> p a b", a=nh)
            views[nm] = v
            for hh in range(nh):
                src = bass.AP(self.Fd.tensor, (hb + hh) * F_LEN + STRIP_OFF[nm], [[1, 128], [1, W]])
                fl.append((v[:, hh, :], src))
            off += nh * W

        def f(h, sem):
            for dst, src in fl:
                h.dma_start(out=dst, in_=src).then_inc(sem, 16)
        S.dma("sp", f, len(fl), writes=[ts])
        return views

    def banded_heads(self, pipe, qt, heads, acc, post):
        S = self.S
        J = self.cb(1)
        steps = []
        for hd in heads:
            band = hd["band"]
            kts = [kt for kt in range(qt - band, qt + band + 1) if 0 <= kt < 16]
            for g0 in range(0, len(kts), 4):
                steps.append((hd, kts[g0:g0 + 4]))
        for si, (hd, grp) in enumerate(steps):
            st = {}

            def s1(hd=hd, grp=grp, st=st):
                band = hd["band"]
                bk = self.bank("S")
                st["bk"] = bk
                fns = []
                fnj = []
                reads = [S.tok("QT", hd["qc"], qt // 4), S.tok("strip")]
                for i, kt in enumerate(grp):
                    o = bk.f32[:, i * 128:(i + 1) * 128]
                    pb = hd["pb"]
                    fns.append(self.MM(bk, o, self.KT[pb:pb + 64, hd["kc"], kt * 128:(kt + 1) * 128],
                                       self.QT[pb:pb + 64, hd["qc"], qt * 128:(qt + 1) * 128]))
                    tk = S.tok("KT", hd["kc"], kt // 4)
                    if tk not in reads:
                        reads.append(tk)
                for i, kt in enumerate(grp):
                    o = bk.f32[:, i * 128:(i + 1) * 128]
                    off = 128 * (band - (kt - qt))
                    fnj.append(self.MM(bk, o, J, hd["strip"][:, off:off + 128]))
                S.op("pe", seq(*(fns + fnj)), reads=reads, writes=[bk.tok])

            def s23(hd=hd, grp=grp, st=st):
                bk = st["bk"]
                n = len(grp) * 128
                pt = self.PT[self._pti % NPT]
                tp = S.tok("PT", self._pti % NPT)
                self._pti += 1
                S.op("act", lambda h, bk=bk, pt=pt, n=n: h.activation(out=pt[:, 0:n], in_=bk.f32[:, 0:n], func=AF.Exp),
                     reads=[bk.tok], writes=[tp])
                fns = []
                reads = [tp]
                for i, kt in enumerate(grp):
                    fns.append(self.MM(acc, acc.f32[:, hd["col"] * 65:hd["col"] * 65 + 65], pt[:, i * 128:(i + 1) * 128],
                                       hd["V"][:, kt, hd["vh"], :]))
                    tv = S.tok("V", kt // 4)
                    if tv not in reads:
                        reads.append(tv)
                S.op("pe", seq(*fns), reads=reads, writes=[acc.tok])
            pipe.push(s1, s23, post if si == len(steps) - 1 else None)

    def layer(self, s, l):
        S = self.S
        self._pti = 0
        self.ov_fence()
        self.norm(l, s, 0)
        if "A" in self.mixers:
            self.mixer_A(s, l)
        if "B" in self.mixers:
            self.mixer_B(s, l)
        if "C" in self.mixers:
            self.mixer_C(s, l)
        if "D" in self.mixers:
            self.mixer_D(s, l)
        if self.ffn:
            self.ov_fence()
            self.norm(l, s, 1)
            self.ffn_block(s, l)

    def mixer_A(self, s, l):
        S = self.S
        bd64 = self.cb(3)
        J64 = self.cb(5)
        V = self.Vr[:, 0:32 * 2 * 65].rearrange("p (r a b) -> p r a b", r=32, a=2)
        ts = S.tok("strip")
        stripA = self.strip[:, 0:1920].rearrange("p (a b) -> p a b", a=2)
        src = self.RAd[l, :, :]
        if not DBG.get("A_nostrip"):
            S.dma("sp", lambda h, sem: h.dma_start(out=self.strip[:, 0:1920], in_=src).then_inc(sem, 16), 1, writes=[ts])
        for p in range(2):
            self._wtot = 128
            wq, tq = self.wnext(l, ("A", p, "q"))
            self.proj_qk(wq, tq, 0, self.QT, "QT", 0, self.qkg(l, 0, 0), bd64, 64)
            wk, tk = self.wnext(l, ("A", p, "k"))
            self.proj_qk(wk, tk, 0, self.KT, "KT", 0, self.qkg(l, 0, 1), bd64, 64)
            wv, tv = self.wnext(l, ("A", p, "v"))
            self.ones_col(V, 64)
            self.proj_v(wv, tv, 128, 2, V, 0, rows64=True)
            wo, to = self.wnext(l, ("A", p, "o"))
            pipe = Pipe(PDEPTH)
            ident = self.cb(0)
            for r in range(32):
                rs_ = min(max(r - 4, 0), 24)
                dr0 = rs_ - r + 7
                ost = self.ostage[(r // 4) % 2]
                tos = S.tok("ost", (r // 4) % 2)
                acc = self.bank("ACC")

                def post_row(r=r, acc=acc, ost=ost, tos=tos):
                    rd = self.tmpf[r % 2]
                    trd = S.tok("tmpf", r % 2)
                    accv = acc.f32[0:64, 0:130].rearrange("p (a b) -> p a b", a=2)
                    S.op("dve", lambda h: h.reciprocal(out=rd[0:64, 0:2], in_=accv[:, :, 64]),
                         reads=[acc.tok], writes=[trd])
                    S.op("dve", lambda h: h.tensor_tensor(
                        out=ost[0:64, (r % 4) * 128:(r % 4) * 128 + 128].rearrange("p (a b) -> p a b", a=2),
                        in0=accv[:, :, 0:64], in1=rd[0:64, 0:2].unsqueeze(2).broadcast_to([64, 2, 64]), op=ALU.mult),
                        reads=[acc.tok, trd], writes=[tos])
                    if r % 4 == 3:
                        r0 = r - 3
                        bk = self.bank("AUX")
                        fns = [(lambda h, c=c: h.transpose(bk.bf[:, c * 64:(c + 1) * 64], ost[0:64, c * 128:(c + 1) * 128],
                                                           ident[0:64, 0:64])) for c in range(4)]
                        S.op("pe", seq(*fns), reads=[tos], writes=[bk.tok])
                        S.op("act", lambda h: h.activation(
                            out=self.mixedT[:, 0, r0 * 64:r0 * 64 + 256], in_=bk.bf[:, 0:256], func=AF.Copy),
                            reads=[bk.tok], writes=[S.tok("mixedT", 0, r0 // 8)])

                for hh in range(2):
                    pb = 64 * hh
                    st = {}

                    def s1(r=r, rs_=rs_, dr0=dr0, pb=pb, st=st, p=p):
                        bk = self.bank("S")
                        st["bk"] = bk
                        fns = []
                        reads = [S.tok("QT", 0, r // 8), ts]
                        for i in range(8):
                            kr = rs_ + i
                            fns.append(self.MM(bk, bk.f32[0:64, i * 64:(i + 1) * 64],
                                               self.KT[pb:pb + 64, 0, kr * 64:(kr + 1) * 64],
                                               self.QT[pb:pb + 64, 0, r * 64:(r + 1) * 64]))
                            tk_ = S.tok("KT", 0, kr // 8)
                            if tk_ not in reads:
                                reads.append(tk_)
                        fns.append(self.MM(bk, bk.f32[0:64, :], J64[pb:pb + 64, 0:64],
                                           stripA[pb:pb + 64, p, dr0 * 64:dr0 * 64 + 512]))
                        S.op("pe", seq(*fns), reads=reads, writes=[bk.tok])

                    def s23(rs_=rs_, hh=hh, st=st, acc=acc):
                        bk = st["bk"]
                        pt = self.PT[self._pti % NPT]
                        tp = S.tok("PT", self._pti % NPT)
                        self._pti += 1
                        S.op("act", lambda h: h.activation(out=pt[0:64, :], in_=bk.f32[0:64, :], func=AF.Exp),
                             reads=[bk.tok], writes=[tp])
                        fns = []
                        reads = [tp]
                        for i in range(8):
                            kr = rs_ + i
                            fns.append(self.MM(acc, acc.f32[0:64, hh * 65:hh * 65 + 65], pt[0:64, i * 64:(i + 1) * 64],
                                               V[0:64, kr, hh, :]))
                            tv_ = S.tok("V", kr // 8)
                            if tv_ not in reads:
                                reads.append(tv_)
                        S.op("pe", seq(*fns), reads=reads, writes=[acc.tok])
                    pipe.push(s1, s23, post_row if hh == 1 else None)
            pipe.flush()
            self.wout_pass(l, s, [(wo, to, 1)], None)

    def mixer_B(self, s, l):
        S = self.S
        bd64 = self.cb(3)
        V = self.Vr[:, 0:16 * 2 * 65].rearrange("p (t a b) -> p t a b", t=16, a=2)
        sv = self.load_strip(["B"])["B"]
        self._wtot = 256
        wq, tq = self.wnext(l, ("B", "q"))
        for c in range(2):
            self.proj_qk(wq, tq, c * 128, self.QT, "QT", c, self.qkg(l, 1, 0), bd64, 64)
        wk, tk = self.wnext(l, ("B", "k"))
        for c in range(2):
            self.proj_qk(wk, tk, c * 128, self.KT, "KT", c, self.qkg(l, 1, 1), bd64, 64)
        self._wtot = 128
        wv, tv = self.wnext(l, ("B", "v"))
        self.ones_col(V)
        self.proj_v(wv, tv, 128, 2, V, 0)
        wo, to = self.wnext(l, ("B", "o"))
        esink = self.small[:, 32 + l * 4:32 + l * 4 + 4]
        pipe = Pipe(PDEPTH)
        for qt in range(16):
            acc = self.bank("ACC")
            heads = [dict(qc=h // 2, pb=64 * (h % 2), kc=h // 2, vh=h // 2, strip=sv[:, h, :], band=1, col=h, V=V)
                     for h in range(4)]

            def post(qt=qt, acc=acc):
                accv = acc.f32[:, 0:260].rearrange("p (a b) -> p a b", a=4)
                rd = self.tmpf[qt % 2]
                trd = S.tok("tmpf", qt % 2)
                S.op("dve", lambda h, rd=rd, accv=accv: h.tensor_tensor(out=rd[:, 0:4], in0=accv[:, :, 64], in1=esink,
                                                                       op=ALU.add), reads=[acc.tok], writes=[trd])
                S.op("dve", lambda h, rd=rd: h.reciprocal(out=rd[:, 4:8], in_=rd[:, 0:4]), reads=[trd], writes=[trd])
                ost = self.ostage[qt % 2]
                tos = S.tok("ost", qt % 2)
                S.op("dve", lambda h, rd=rd, accv=accv, ost=ost: h.tensor_tensor(
                    out=ost[:, 0:256].rearrange("p (a b) -> p a b", a=4), in0=accv[:, :, 0:64],
                    in1=rd[:, 4:8].unsqueeze(2).broadcast_to([128, 4, 64]), op=ALU.mult),
                    reads=[acc.tok, trd], writes=[tos])
                self.transposes(lambda c, ost=ost: ost[:, c * 128:(c + 1) * 128], 128, 2,
                                slice(qt * 128, (qt + 1) * 128), qt // 4, tos)
            self.banded_heads(pipe, qt, heads, acc, post)
        pipe.flush()
        self.wout_pass(l, s, [(wo, to, 2)], None)

    def mixer_C(self, s, l):
        S = self.S
        bd64 = self.cb(3)
        V = self.Vr[:, 0:16 * 6 * 65].rearrange("p (t a b) -> p t a b", t=16, a=6)
        svs = self.load_strip(["C0", "C1", "C2"])
        self._wtot = 256
        w, t = self.wnext(l, ("C", "q0"))
        for c in range(2):
            self.proj_qk(w, t, c * 128, self.QT, "QT", c, self.qkg(l, 2, 0), bd64, 64)
        self._wtot = 128
        w, t = self.wnext(l, ("C", "q1"))
        self.proj_qk(w, t, 0, self.QT, "QT", 2, self.qkg(l, 2, 0), bd64, 64)
        self._wtot = 256
        w, t = self.wnext(l, ("C", "k0"))
        for c in range(2):
            self.proj_qk(w, t, c * 128, self.KT, "KT", c, self.qkg(l, 2, 1), bd64, 64)
        self._wtot = 128
        w, t = self.wnext(l, ("C", "k1"))
        self.proj_qk(w, t, 0, self.KT, "KT", 2, self.qkg(l, 2, 1), bd64, 64)
        self.ones_col(V)
        self._wtot = 192
        w, t = self.wnext(l, ("C", "v0"))
        self.proj_v(w, t, 192, 3, V, 0)
        w, t = self.wnext(l, ("C", "v1"))
        self.proj_v(w, t, 192, 3, V, 3)
        bands = (1, 2, 8)
        names = ("C0", "C1", "C2")
        pipe = Pipe(PDEPTH)
        for qt in range(16):
            acc = self.bank("ACC")
            heads = []
            for g in range(3):
                for j in range(2):
                    hh = 2 * g + j
                    heads.append(dict(qc=g, pb=64 * j, kc=g, vh=hh, strip=svs[names[g]][:, j, :], band=bands[g],
                                      col=hh, V=V))

            def post(qt=qt, acc=acc):
                accv = acc.f32[:, 0:390].rearrange("p (g j b) -> p g j b", g=3, j=2)
                rd = self.tmpf[qt % 2]
                trd = S.tok("tmpf", qt % 2)
                S.op("dve", lambda h, rd=rd, accv=accv: h.tensor_copy(
                    out=rd[:, 8:14].rearrange("p (g j) -> p g j", g=3), in_=accv[:, :, :, 64]),
                    reads=[acc.tok], writes=[trd])
                S.op("dve", lambda h, rd=rd: h.tensor_tensor(out=rd[:, 0:2], in0=rd[:, 8:10], in1=rd[:, 10:12], op=ALU.add),
                     reads=[trd], writes=[trd])
                S.op("dve", lambda h, rd=rd: h.tensor_tensor(out=rd[:, 0:2], in0=rd[:, 0:2], in1=rd[:, 12:14], op=ALU.add),
                     reads=[trd], writes=[trd])
                S.op("dve", lambda h, rd=rd: h.reciprocal(out=rd[:, 4:6], in_=rd[:, 0:2]), reads=[trd], writes=[trd])
                ost = self.ostage[qt % 2]
                tos = S.tok("ost", qt % 2)
                for g in range(3):
                    S.op("dve", lambda h, rd=rd, accv=accv, ost=ost, g=g: h.tensor_tensor(
                        out=ost[:, g * 128:(g + 1) * 128].rearrange("p (a b) -> p a b", a=2), in0=accv[:, g, :, 0:64],
                        in1=rd[:, 4:6].unsqueeze(2).broadcast_to([128, 2, 64]), op=ALU.mult),
                        reads=[acc.tok, trd], writes=[tos])
                self.transposes(lambda c, ost=ost: ost[:, c * 128:(c + 1) * 128], 128, 3,
                                slice(qt * 128, (qt + 1) * 128), qt // 4, tos)
            self.banded_heads(pipe, qt, heads, acc, post)
        pipe.flush()
        wo0, to0 = self.wnext(l, ("C", "o0"))
        self.wout_pass(l, s, [(wo0, to0, 2)], None)
        wo1, to1 = self.wnext(l, ("C", "o1"))
        self.wout_pass(l, s, [(wo1, to1, 1)], None, kbase0=2)

    def mixer_D(self, s, l):
        S = self.S
        bd32 = self.cb(4)
        J = self.cb(1)
        V = self.Vr[:, 0:16 * 4 * 65].rearrange("p (t a b) -> p t a b", t=16, a=4)
        sv = self.load_strip(["D"])["D"]
        nlam = self.small[:, 48 + l:49 + l]
        sub = self.vec("subln", l * 64, 64)
        om = 1.0 - lambda_init(l)
        ktm = [(self.KT[:, 0, :], "KT", 0), (self.KT[:, 1, :], "KT", 1), (self.KT[:, 2, :], "KT", 2),
               (self.QT[:, 2, :], "QT", 2)]
        for c in range(2):
            self._wtot = 128
            w, t = self.wnext(l, ("D", "q%d" % c))
            self.proj_qk(w, t, 0, self.QT, "QT", c, self.qkg(l, 3, 0), bd32, 32)
            w, t = self.wnext(l, ("D", "k%d" % c))
            multi = [(ktm[b][0], ktm[b][1], ktm[b][2], self.small[:, 160 + l * 4 + b:161 + l * 4 + b]) for b in range(4)]
            self.proj_qk(w, t, 0, None, None, None, None, bd32, 32, multi=multi)
            if c == 0:
                self.ones_col(V)
                self._wtot = 256
                w, t = self.wnext(l, ("D", "v"))
                self.proj_v(w, t, 256, 4, V, 0)
            pipe = Pipe(PDEPTH)
            for qb in range(4):
                ost = self.ostage[(2 * c + qb) % 2]
                tos = S.tok("ost", (2 * c + qb) % 2)
                for hh in (2 * c, 2 * c + 1):
                    accs = [self.bank("ACC"), self.bank("ACC")]

                    def post_head(qb=qb, hh=hh, accs=accs, ost=ost, tos=tos):
                        a0 = accs[0].f32[:, 0:260].rearrange("p (a b) -> p a b", a=4)
                        a1 = accs[1].f32[:, 0:260].rearrange("p (a b) -> p a b", a=4)
                        rd = self.rstd[hh % 2]
                        trd = S.tok("rstd", hh % 2)
                        t0 = self.tmpf[0]
                        t1 = self.tmpf[1]
                        t0v = t0[:, 0:256].rearrange("p (a b) -> p a b", a=4)
                        t1v = t1[:, 0:256].rearrange("p (a b) -> p a b", a=4)
                        S.op("dve", lambda h: h.reciprocal(out=rd[:, 0:4], in_=a0[:, :, 64]),
                             reads=[accs[0].tok], writes=[trd])
                        S.op("dve", lambda h: h.reciprocal(out=rd[:, 4:8], in_=a1[:, :, 64]),
                             reads=[accs[1].tok, trd], writes=[trd])
                        S.op("dve", lambda h: h.tensor_scalar(out=rd[:, 4:8], in0=rd[:, 4:8], scalar1=nlam, scalar2=None,
                                                              op0=ALU.mult), reads=[trd], writes=[trd])
                        S.op("dve", lambda h: h.tensor_tensor(
                            out=t0v, in0=a0[:, :, 0:64], in1=rd[:, 0:4].unsqueeze(2).broadcast_to([128, 4, 64]), op=ALU.mult),
                            reads=[accs[0].tok, trd], writes=[S.tok("tmpf", 0)])
                        S.op("dve", lambda h: h.tensor_tensor(
                            out=t1v, in0=a1[:, :, 0:64], in1=rd[:, 4:8].unsqueeze(2).broadcast_to([128, 4, 64]), op=ALU.mult),
                            reads=[accs[1].tok, trd], writes=[S.tok("tmpf", 1)])
                        S.op("dve", lambda h: h.tensor_tensor(out=t0[:, 0:256], in0=t0[:, 0:256], in1=t1[:, 0:256], op=ALU.add),
                             reads=[S.tok("tmpf", 1)], writes=[S.tok("tmpf", 0)])
                        S.op("act", lambda h: h.activation(out=t1[:, 0:256], in_=t0[:, 0:256], func=AF.Square),
                             reads=[S.tok("tmpf", 0)], writes=[S.tok("tmpf", 1)])
                        S.op("dve", lambda h: h.tensor_reduce(out=rd[:, 8:12], in_=t1v, axis=AX.X, op=ALU.add),
                             reads=[S.tok("tmpf", 1)], writes=[trd])
                        S.op("act", lambda h: h.activation(out=rd[:, 8:12], in_=rd[:, 8:12], func=AF.Ln, bias=self.eps64,
                                                           scale=1.0), reads=[trd], writes=[trd])
                        S.op("act", lambda h: h.activation(out=rd[:, 8:12], in_=rd[:, 8:12], func=AF.Exp, scale=-0.5),
                             reads=[trd], writes=[trd])
                        S.op("dve", lambda h: h.tensor_tensor(
                            out=t0v, in0=t0v, in1=rd[:, 8:12].unsqueeze(2).broadcast_to([128, 4, 64]), op=ALU.mult),
                            reads=[trd], writes=[S.tok("tmpf", 0)])
                        S.op("dve", lambda h: h.scalar_tensor_tensor(
                            out=ost[:, 0:1024].rearrange("p (q a b) -> p q a b", q=4, a=4)[:, :, hh, :], in0=t0v,
                            scalar=8.0 * om, in1=sub.unsqueeze(1).broadcast_to([128, 4, 64]), op0=ALU.mult, op1=ALU.mult),
                            reads=[S.tok("tmpf", 0)], writes=[tos])
                        if hh % 2 == 1:
                            cc = hh // 2
                            for jq in range(4):
                                qt = 4 * qb + jq
                                self.transposes(lambda c_, jq=jq, cc=cc: ost[:, jq * 256 + cc * 128:jq * 256 + (cc + 1) * 128],
                                                128, 1, slice(qt * 128, (qt + 1) * 128), qb, tos, c0=cc)

                    for i in range(2):
                        pb = 32 * (2 * (hh % 2) + i)
                        acc = accs[i]
                        for kt in range(16):
                            d0 = kt - 4 * qb
                            mixed = -1 <= d0 <= 4
                            st = {}

                            def s1(pb=pb, kt=kt, d0=d0, mixed=mixed, st=st, c=c, hh=hh, qb=qb):
                                bk = self.bank("S")
                                st["bk"] = bk
                                kb = ktm[pb // 32]
                                fns = [self.MM(bk, bk.f32, kb[0][:, kt * 128:(kt + 1) * 128],
                                               self.QT[:, c, qb * 512:(qb + 1) * 512])]
                                if mixed:
                                    off = 128 * (4 - d0)
                                    fns.append(self.MM(bk, bk.f32, J, sv[:, hh, off:off + 512]))
                                S.op("pe", seq(*fns), reads=[S.tok("QT", c, qb), S.tok(kb[1], kb[2], kt // 4), S.tok("strip")],
                                     writes=[bk.tok])

                            def s23(kt=kt, d0=d0, mixed=mixed, st=st, hh=hh, acc=acc):
                                bk = st["bk"]
                                pt = self.PT[self._pti % NPT]
                                tp = S.tok("PT", self._pti % NPT)
                                self._pti += 1
                                if mixed:
                                    S.op("act", lambda h: h.activation(out=pt, in_=bk.f32, func=AF.Exp),
                                         reads=[bk.tok], writes=[tp])
                                else:
                                    cb_ = self.vec("t5c", (0 if d0 < 0 else 4) + hh)
                                    S.op("act", lambda h: h.activation(out=pt, in_=bk.f32, func=AF.Exp, bias=cb_, scale=1.0),
                                         reads=[bk.tok], writes=[tp])
                                fns = [self.MM(acc, acc.f32[:, jq * 65:jq * 65 + 65], pt[:, jq * 128:(jq + 1) * 128],
                                               V[:, kt, hh, :]) for jq in range(4)]
                                S.op("pe", seq(*fns), reads=[tp, S.tok("V", kt // 4)], writes=[acc.tok])
                            pipe.push(s1, s23, post_head if (i == 1 and kt == 15) else None)
            pipe.flush()
        wo, to = self.wnext(l, ("D", "o"))
        self.wout_pass(l, s, [(wo, to, 2)], None, tb_outer=True)

    def ffn_block(self, s, l):
        S = self.S
        gate = self.mod(l, 5, s)
        co, _ = VC["conv"]

        def cw(tap, jc):
            c = co + (l * 4 + tap) * 44 + jc
            return self.vecs[:, c:c + 1]
        self._wtot = 256
        blkc = 0
        for qi, (j0, nj) in enumerate(QUARTERS):
            for jj in range(nj):
                j = j0 + jj
                w, tw = self.wnext(l, ("U", j))
                for (o0, o1, i0, i1) in FFN_BLOCKS:
                    nin = i1 - i0
                    nout = o1 - o0
                    bks = []
                    for vg in range(2):
                        bk = self.bank("ALL8")
                        fns = [self.MM(bk, bk.f32[:, 0:nin], self._w(w, kc, vg * 128, 128), self.hT[:, kc, i0:i1])
                               for kc in range(8)]
                        rt = list({S.tok("hT", i0 // 512), S.tok("hT", (i1 - 1) // 512)})
                        S.op("pe", seq(*fns), reads=[tw] + rt, writes=[bk.tok])
                        bks.append(bk)
                    ab = self.acc[blkc % 4]
                    tas = [S.tok("acc", blkc % 4, 0), S.tok("acc", blkc % 4, 1)]
                    sgb = self.sg[blkc % 4]
                    tsg = S.tok("sg", blkc % 4)
                    blkc += 1
                    for vg in range(2):
                        ta = tas[vg]
                        bk = bks[vg]
                        jc = j + 22 * vg
                        a = ab[:, vg, :]
                        c1 = o0 - i0
                        S.op("act", lambda h, a=a, bk=bk, jc=jc, c1=c1, nout=nout: h.activation(
                            out=a[:, 0:nout], in_=bk.f32[:, c1:c1 + nout], func=AF.Identity, bias=cw(3, jc),
                            scale=cw(1, jc)), reads=[bk.tok], writes=[ta])
                        ta0 = max(o0, 1)
                        n0 = o1 - ta0
                        S.op("dve", lambda h, a=a, bk=bk, jc=jc, ta0=ta0, n0=n0, o0=o0, i0=i0: h.scalar_tensor_tensor(
                            out=a[:, ta0 - o0:ta0 - o0 + n0], in0=bk.f32[:, ta0 - 1 - i0:ta0 - 1 - i0 + n0],
                            scalar=cw(0, jc), in1=a[:, ta0 - o0:ta0 - o0 + n0], op0=ALU.mult, op1=ALU.add),
                            reads=[bk.tok, ta], writes=[ta])
                        e2 = min(o1, N - 1)
                        n2 = e2 - o0
                        S.op("dve", lambda h, a=a, bk=bk, jc=jc, n2=n2, o0=o0, i0=i0: h.scalar_tensor_tensor(
                            out=a[:, 0:n2], in0=bk.f32[:, o0 + 1 - i0:o0 + 1 - i0 + n2],
                            scalar=cw(2, jc), in1=a[:, 0:n2], op0=ALU.mult, op1=ALU.add),
                            reads=[bk.tok, ta], writes=[ta])
                    S.op("act", lambda h, ab=ab, sgb=sgb, nout=nout: h.activation(out=sgb[:, 0:nout], in_=ab[:, 1, 0:nout],
                                                                                func=AF.Silu),
                         reads=[tas[1]], writes=[tsg])
                    gt = list({S.tok("gT", jj, o0 // 512), S.tok("gT", jj, (o1 - 1) // 512)})
                    S.op("pool", lambda h, ab=ab, sgb=sgb, nout=nout, jj=jj, o0=o0, o1=o1: h.tensor_tensor(
                        out=self.gT[:, jj, o0:o1], in0=ab[:, 0, 0:nout], in1=sgb[:, 0:nout], op=ALU.mult),
                        reads=[tas[0], tsg], writes=gt)
            for mp in range(4):
                w, tw = self.wnext(l, ("Dn", qi, mp))
                wv = w[:, 0:2 * nj * 128].rearrange("p (m a b) -> p m a b", m=2, a=nj)
                for mm in range(2):
                    m = 2 * mp + mm
                    for tb in range(4):
                        blk = slice(tb * 512, (tb + 1) * 512)
                        bk = self.bank("ALL7")
                        fns = [self.MM(bk, bk.f32, wv[:, mm, jj, :], self.gT[:, jj, blk]) for jj in range(nj)]
                        S.op("pe", seq(*fns), reads=[tw] + [S.tok("gT", jj, tb) for jj in range(nj)], writes=[bk.tok])
                        S.op("dve", lambda h, bk=bk, m=m, blk=blk: h.scalar_tensor_tensor(
                            out=self.xT[:, m, blk], in0=bk.f32, scalar=gate[:, m:m + 1], in1=self.xT[:, m, blk],
                            op0=ALU.mult, op1=ALU.add), reads=[bk.tok], writes=[S.tok("xT", tb)])

    def build(self):
        nc = self.nc
        self.setup_mem()
        self.banks_init()
        self.epsD = self.small[:, 250:251]
        self.eps64 = self.small[:, 251:252]
        self.eps32 = self.small[:, 252:253]
        nc = self.nc
        handles = {"pe": nc.tensor, "act": nc.scalar, "dve": nc.vector, "pool": nc.gpsimd, "sp": nc.sync}
        battr = {"pe": "tensor", "act": "scalar", "dve": "vector", "pool": "gpsimd", "sp": "sync"}

        def replay(eng, h, semh):
            for it in eng.items:
                if it[0] == "w":
                    h.wait_ge(semh[it[1]], it[2])
                elif it[0] == "i":
                    it[1](h).then_inc(semh[it[2]], 1)
                else:
                    it[1](h, semh[it[2]])

        S1 = self.S
        S1.op("pool", lambda h: h.memset(self.small[:, 250:251], EPS * D), writes=[S1.tok("pro")])
        S1.op("pool", lambda h: h.memset(self.small[:, 251:252], EPS * 64), writes=[S1.tok("pro")])
        S1.op("pool", lambda h: h.memset(self.small[:, 252:253], EPS * 32), writes=[S1.tok("pro")])
        self.prologue()
        with ExitStack() as es:
            semh = {}
            for i, k in enumerate(S1.all_keys()):
                semh[k] = es.enter_context(nc.semaphore("p%d" % i))
            block = es.enter_context(nc.Block())
            for en in Sched.ENG:
                getattr(block, battr[en])(lambda h, en=en: replay(S1.E[en], h, semh))
        S2 = Sched()
        self.S = S2
        self.banks_init()
        S2.dry = True
        self._banks_rr = {}
        self.body()
        S2.dry = False
        self._banks_rr = {}
        self.wi = 0
        self.w_issued = 0
        self.body()
        S2._waits("sp", {k: v for k, v in S2.dma_cnt.items() if v})
        self.n_items = {en: len(S2.E[en].items) for en in Sched.ENG}
        with ExitStack() as es:
            semh = {}
            keys = S2.all_keys()
            for i, k in enumerate(keys):
                semh[k] = es.enter_context(nc.semaphore("b%d" % i))
            with nc.Fori(0, self.nseq) as li:
                self.loop_i = li
                for en in Sched.ENG:
                    replay(S2.E[en], handles[en], semh)
                nc.all_engine_barrier()
                for k in keys:
                    nc.sync.sem_clear(semh[k])
                nc.all_engine_barrier()
        return nc


def host_inputs(inp, seqs_x, seqs_c):
    f = np.float32
    w_ada = np.asarray(inp["w_ada"], f)
    wada = w_ada.reshape(DEPTH, 8, 128, 48, 128).transpose(0, 3, 2, 1, 4).reshape(DEPTH * 48, 128, 1024)
    wada = np.ascontiguousarray(wada)
    wl = np.empty((DEPTH, 128, WL_COLS), f)
    for l in range(DEPTH):
        ch = layer_chunks(np.asarray(inp["w_in"][l], f), np.asarray(inp["w_out"][l], f),
                          np.asarray(inp["w_up"][l], f), np.asarray(inp["w_down"][l], f))
        assert [k for k, _ in ch] == [k for k, _ in PLAN]
        o = 0
        for (k, a), (_, n) in zip(ch, PLAN):
            assert a.shape == (128, n), (k, a.shape, n)
            wl[l, :, o:o + n] = a
            o += n
    vecs = np.zeros((128, NVEC), f)

    def put(name, arr):
        o, n = VC[name]
        assert arr.shape == (128, n), (name, arr.shape, n)
        vecs[:, o:o + n] = arr
    b_ada = np.asarray(inp["b_ada"], f)
    put("b_ada", b_ada.reshape(DEPTH, 48, 128).transpose(2, 0, 1).reshape(128, DEPTH * 48))
    gn = np.stack([np.asarray(inp["norm_attn_g"], f), np.asarray(inp["norm_ffn_g"], f)], 1)
    put("gnorm", gn.reshape(DEPTH, 2, 8, 128).transpose(3, 0, 1, 2).reshape(128, DEPTH * 16))
    p = np.arange(128)
    qk = np.zeros((128, DEPTH, 4, 2), f)
    for m, nm in enumerate(["qkn_a", "qkn_b", "qkn_c", "qkn_d"]):
        g = np.asarray(inp[nm], f)
        d = g.shape[2]
        qk[:, :, m, :] = g[:, :, p % d].transpose(2, 0, 1)
    put("qkg", qk.reshape(128, DEPTH * 8))
    cw = np.concatenate([np.asarray(inp["conv_w"], f), np.asarray(inp["conv_b"], f)[:, None, :]], 1)
    put("conv", cw.reshape(DEPTH, 4, 44, 128).transpose(3, 0, 1, 2).reshape(128, DEPTH * 4 * 44))
    t5 = np.asarray(inp["t5_table"], f)
    put("t5c", np.broadcast_to(np.concatenate([t5[15, 10:14], t5[31, 10:14]])[None, :], (128, 8)))
    put("sink", np.broadcast_to(np.asarray(inp["sink_b"], f).reshape(1, 16), (128, 16)))
    put("linit", np.broadcast_to(np.array([lambda_init(l) for l in range(DEPTH)], f)[None, :], (128, DEPTH)))
    put("lam", np.broadcast_to(np.asarray(inp["lam_d"], f).reshape(1, -1), (128, DEPTH * 128)))
    put("subln", np.broadcast_to(np.asarray(inp["subln_d"], f).reshape(1, -1), (128, DEPTH * 64)))
    cst = np.zeros((128, 7, 128), f)
    cst[:, 0] = np.eye(128)
    cst[:, 1] = np.eye(128)
    cst[:, 2] = np.eye(128)[::-1]
    cst[:, 3] = 1.0
    cst[:, 4] = np.kron(np.eye(2), np.ones((64, 64)))
    cst[:, 5] = np.kron(np.eye(4), np.ones((32, 32)))
    pc = np.arange(64)
    kc_ = 63 - pc[:, None]
    c_ = pc[None, :]
    cs = np.clip(c_ - 8, 0, 48)
    okm = (kc_ >= cs) & (kc_ < cs + 16)
    for hb in (0, 64):
        cst[hb:hb + 64, 6, 0:64] = np.eye(64)[::-1]
        cst[hb:hb + 64, 6, 64:128] = np.where(okm, 0.0, MASKV)
    t5aug = np.concatenate([t5, np.full((1, 14), MASKV, f)], 0)
    oh = build_onehot()
    rpb = np.asarray(inp["rpb_a"], f)
    rp = np.zeros((DEPTH, 4, 15, 127), f)
    rp[..., 48:79] = rpb[..., ::-1]
    rp = rp.reshape(16, 15 * 127)
    maps = []
    for xs, cs_ in zip(seqs_x, seqs_c):
        ns = xs.shape[0]
        cT = np.ascontiguousarray(cs_.reshape(ns, 8, 128).transpose(2, 1, 0).reshape(128, 8 * ns))
        maps.append({"x": np.ascontiguousarray(xs), "cT": cT, "wada": wada, "wl": wl, "vecs": vecs,
                     "cst": cst.reshape(128, 7 * 128), "t5aug": t5aug, "oh": oh, "rpbrp": rp})
    return maps


_CACHE = {}


def get_program(nseq, layers=(0, 1, 2, 3), mixers=("A", "B", "C", "D"), ffn=True):
    key = (nseq, tuple(layers), tuple(mixers), ffn)
    if key not in _CACHE:
        b = Builder(nseq, list(layers), mixers, ffn)
        _CACHE[key] = b.build()
    return _CACHE[key]


def kernel(**inp):
    xp = np.asarray(inp["x_prompt"], np.float32)
    xs = np.asarray(inp["x_sample"], np.float32)
    cp = np.asarray(inp["c_prompt"], np.float32)
    cs = np.asarray(inp["c_sample"], np.float32)
    seqs_x, seqs_c = [], []
    for c in range(8):
        seqs_x.append(np.concatenate([xp[4 * c:4 * c + 4], xs[2 * c:2 * c + 2]], 0))
        seqs_c.append(np.concatenate([cp[4 * c:4 * c + 4], cs[2 * c:2 * c + 2]], 0))
    maps = host_inputs(inp, seqs_x, seqs_c)
    nc = get_program(6)
    res = run_bass_kernel_spmd(nc, maps, core_ids=list(range(8)))
    yp = np.empty_like(xp)
    ys = np.empty_like(xs)
    for c in range(8):
        y = res.results[c]["y"]
        yp[4 * c:4 * c + 4] = y[0:4]
        ys[2 * c:2 * c + 2] = y[4:6]
    return (yp, ys)
```

```python
import math
from contextlib import ExitStack
import numpy as np
import concourse.bass as bass
import concourse.mybir as mybir
from concourse.bass_utils import run_bass_kernel_spmd

F32 = mybir.dt.float32
BF16 = mybir.dt.bfloat16
AF = mybir.ActivationFunctionType
ALU = mybir.AluOpType
AX = mybir.AxisListType
DBG = {}

D = 1024
N = 2048
DEPTH = 4
DFF = 2816
NJ = 22
EPS = 1e-6
MASKV = -30000.0
NPT = 6
PDEPTH = 3
QUARTERS = [(0, 6), (6, 6), (12, 5), (17, 5)]
FFN_BLOCKS = []
for _b in range(5):
    _o0 = 510 * _b
    _o1 = min(_o0 + 510, N)
    FFN_BLOCKS.append((_o0, _o1, max(_o0 - 1, 0), min(_o1 + 1, N)))

A0, B0, C0, D0 = 0, 768, 1280, 2432


def t5_bucket_np(rel):
    rel = np.asarray(rel, dtype=np.int32)
    nb = 16
    max_exact = 8
    base = np.where(rel > 0, nb, 0)
    n = np.abs(rel)
    nf = np.maximum(n, 1).astype(np.float32)
    large = max_exact + (np.log(nf / np.float32(max_exact)) / np.float32(math.log(128 / max_exact))
                         * np.float32(nb - max_exact)).astype(np.int32)
    large = np.minimum(large, nb - 1)
    return base + np.where(n < max_exact, n, large)


STRIPS = [
    ("B", 1, lambda r: np.abs(r) <= 128, 0, 4),
    ("C0", 1, lambda r: np.abs(r) <= 64, 4, 2),
    ("C1", 2, lambda r: (np.abs(r) <= 256) & (r % 4 == 0), 6, 2),
    ("C2", 8, lambda r: (np.abs(r) <= 1024) & (r % 16 == 0), 8, 2),
    ("D", 4, lambda r: np.ones_like(r, dtype=bool), 10, 4),
]
STRIP_W = {}
STRIP_OFF = {}
_off = 0
for _n, _band, _v, _hb, _nh in STRIPS:
    STRIP_W[_n] = 128 * (2 * _band + 1)
    STRIP_OFF[_n] = _off
    _off += STRIP_W[_n] + 128
F_LEN = _off
STRIP_INFO = {n: (band, hb, nh) for n, band, v, hb, nh in STRIPS}


def build_onehot():
    oh = np.zeros((33, F_LEN), np.float32)
    for name, band, valid, hb, nh in STRIPS:
        L = STRIP_W[name] + 128
        y = np.arange(L)
        rel = 127 + 128 * band - y
        ok = valid(rel)
        bk = t5_bucket_np(rel)
        idx = np.where(ok, bk, 32)
        oh[idx, STRIP_OFF[name] + y] = 1.0
    return oh


def _kt(w, cols):
    sub = w[:, cols]
    n = sub.shape[1]
    return sub.reshape(8, 128, n).transpose(1, 0, 2).reshape(128, 8 * n)


def layer_chunks(w_in, w_out, w_up, w_down):
    ch = []
    r = np.arange
    for p in range(2):
        ch.append((("A", p, "q"), _kt(w_in, A0 + 0 + p * 128 + r(128))))
        ch.append((("A", p, "k"), _kt(w_in, A0 + 256 + p * 128 + r(128))))
        ch.append((("A", p, "v"), _kt(w_in, A0 + 512 + p * 128 + r(128))))
        ch.append((("A", p, "o"), w_out[p * 128:(p + 1) * 128, :].reshape(128, 1024)))
    ch.append((("B", "q"), _kt(w_in, B0 + r(256))))
    kc = np.concatenate([B0 + 256 + r(64), B0 + 256 + r(64), B0 + 320 + r(64), B0 + 320 + r(64)])
    ch.append((("B", "k"), _kt(w_in, kc)))
    ch.append((("B", "v"), _kt(w_in, B0 + 384 + r(128))))
    ch.append((("B", "o"), w_out[256:512, :].reshape(2, 128, 1024).transpose(1, 0, 2).reshape(128, 2048)))
    ch.append((("C", "q0"), _kt(w_in, C0 + r(256))))
    ch.append((("C", "q1"), _kt(w_in, C0 + 256 + r(128))))
    ch.append((("C", "k0"), _kt(w_in, C0 + 384 + r(256))))
    ch.append((("C", "k1"), _kt(w_in, C0 + 640 + r(128))))
    ch.append((("C", "v0"), _kt(w_in, C0 + 768 + r(192))))
    ch.append((("C", "v1"), _kt(w_in, C0 + 960 + r(192))))
    ch.append((("C", "o0"), w_out[512:768, :].reshape(2, 128, 1024).transpose(1, 0, 2).reshape(128, 2048)))
    ch.append((("C", "o1"), w_out[768:896, :].reshape(128, 1024)))
    ch.append((("D", "q0"), _kt(w_in, D0 + r(128))))
    ch.append((("D", "k0"), _kt(w_in, D0 + 256 + r(128))))
    ch.append((("D", "v"), _kt(w_in, D0 + 512 + r(256))))
    ch.append((("D", "q1"), _kt(w_in, D0 + 128 + r(128))))
    ch.append((("D", "k1"), _kt(w_in, D0 + 384 + r(128))))
    ch.append((("D", "o"), w_out[896:1152, :].reshape(2, 128, 1024).transpose(1, 0, 2).reshape(128, 2048)))
    for qi, (j0, nj) in enumerate(QUARTERS):
        for jj in range(nj):
            j = j0 + jj
            cols = np.concatenate([j * 128 + r(128), DFF + j * 128 + r(128)])
            ch.append((("U", j), _kt(w_up, cols)))
        wd = w_down[j0 * 128:(j0 + nj) * 128, :].reshape(nj, 128, 8, 128)
        for mp in range(4):
            blk = wd[:, :, 2 * mp:2 * mp + 2, :].transpose(1, 2, 0, 3).reshape(128, 2 * nj * 128)
            ch.append((("Dn", qi, mp), blk))
    return ch


def chunk_plan():
    z_in = np.zeros((1024, 1), np.float32)

    class _Z:
        def __init__(self, shape):
            self.shape = shape

    plan = []
    for p in range(2):
        plan += [(("A", p, "q"), 1024), (("A", p, "k"), 1024), (("A", p, "v"), 1024), (("A", p, "o"), 1024)]
    plan += [(("B", "q"), 2048), (("B", "k"), 2048), (("B", "v"), 1024), (("B", "o"), 2048)]
    plan += [(("C", "q0"), 2048), (("C", "q1"), 1024), (("C", "k0"), 2048), (("C", "k1"), 1024),
             (("C", "v0"), 1536), (("C", "v1"), 1536), (("C", "o0"), 2048), (("C", "o1"), 1024)]
    plan += [(("D", "q0"), 1024), (("D", "k0"), 1024), (("D", "v"), 2048), (("D", "q1"), 1024), (("D", "k1"), 1024),
             (("D", "o"), 2048)]
    for qi, (j0, nj) in enumerate(QUARTERS):
        for jj in range(nj):
            plan.append((("U", j0 + jj), 2048))
        for mp in range(4):
            plan.append((("Dn", qi, mp), 2 * nj * 128))
    return plan


PLAN = chunk_plan()
PLAN_OFF = {}
_o = 0
for _k, _n in PLAN:
    PLAN_OFF[_k] = (_o, _n)
    _o += _n
WL_COLS = _o

VC = {}
_c = 0


def _vc(name, n):
    global _c
    VC[name] = (_c, n)
    _c += n


_vc("b_ada", DEPTH * 48)
_vc("gnorm", DEPTH * 2 * 8)
_vc("qkg", DEPTH * 4 * 2)
_vc("conv", DEPTH * 4 * 44)
_vc("t5c", 8)
_vc("sink", DEPTH * 4)
_vc("linit", DEPTH)
_vc("lam", DEPTH * 4 * 32)
_vc("subln", DEPTH * 64)
NVEC = _c


def lambda_init(l):
    return 0.8 - 0.6 * math.exp(-0.3 * l)


class Tok:
    __slots__ = ("w", "r")

    def __init__(self):
        self.w = None
        self.r = {}


EPOCH = 16000


class _Eng:
    def __init__(self, name):
        self.name = name
        self.items = []
        self.count = 0
        self.epoch = 0
        self.finals = {}
        self.seen = {}

    def key(self):
        return ("e", self.name, self.epoch)

    def bump(self):
        if self.count >= EPOCH:
            self.finals[self.key()] = self.count
            self.epoch += 1
            self.count = 0
        self.count += 1
        return (self.key(), self.count)


class Sched:
    ENG = ("pe", "act", "dve", "pool", "sp")

    def __init__(self):
        self.E = {n: _Eng(n) for n in self.ENG}
        self.toks = {}
        self.dry = False
        self.dma_key = {}
        self.dma_cnt = {}
        self.out_keys = set()

    def tok(self, *key):
        t = self.toks.get(key)
        if t is None:
            t = self.toks[key] = Tok()
        return t

    def _deps(self, reads, writes):
        deps = {}
        for t in reads:
            if t.w is not None and deps.get(t.w[0], 0) < t.w[1]:
                deps[t.w[0]] = t.w[1]
        for t in writes:
            if t.w is not None and deps.get(t.w[0], 0) < t.w[1]:
                deps[t.w[0]] = t.w[1]
            for k, v in t.r.items():
                if deps.get(k, 0) < v:
                    deps[k] = v
        return deps

    def _waits(self, en, deps):
        eng = self.E[en]
        for k, v in deps.items():
            if en == "pe" and k[0] == "e" and k[1] == "pe":
                continue
            if eng.seen.get(k, 0) >= v:
                continue
            eng.seen[k] = v
            eng.items.append(("w", k, v))

    def op(self, en, fn, reads=(), writes=()):
        if self.dry:
            return
        eng = self.E[en]
        self._waits(en, self._deps(reads, writes))
        me = eng.bump()
        key = me[0]
        eng.items.append(("i", fn, key))
        for t in writes:
            t.w = me
            t.r = {}
        for t in reads:
            if t.r.get(key, 0) < me[1]:
                t.r[key] = me[1]

    def dma(self, qn, fn, n, reads=(), writes=(), owner=None, is_out=False):
        if self.dry:
            return
        eng = self.E[qn]
        self._waits(qn, self._deps(reads, writes))
        own = owner if owner is not None else writes[0]
        key = self.dma_key.get(id(own))
        if key is None:
            key = ("d", len(self.dma_key))
            self.dma_key[id(own)] = key
            self.dma_cnt[key] = 0
        self.dma_cnt[key] += 16 * n
        cnt = self.dma_cnt[key]
        if is_out:
            self.out_keys.add(key)
        eng.items.append(("d", fn, key))
        me = (key, cnt)
        for t in writes:
            t.w = me
            t.r = {}
        for t in reads:
            if t.r.get(key, 0) < cnt:
                t.r[key] = cnt

    def fence(self, toks):
        if self.dry:
            return
        deps = self._deps((), toks)
        for t in toks:
            t.w = None
            t.r = dict(deps)

    def barrier(self):
        if self.dry:
            return
        allv = {}
        for n, e in self.E.items():
            for k, v in e.finals.items():
                allv[k] = v
            if e.count:
                allv[e.key()] = e.count
        for k, v in self.dma_cnt.items():
            if v:
                allv[k] = v
        for n in self.ENG:
            self._waits(n, dict(allv))

    def final_wait(self):
        deps = {k: self.dma_cnt[k] for k in self.out_keys}
        self._waits("sp", deps)

    def all_keys(self):
        keys = []
        for n in self.ENG:
            e = self.E[n]
            keys += [("e", n, ep) for ep in range(e.epoch + 1)]
        keys += list(self.dma_cnt.keys())
        return keys


class Pipe:
    def __init__(self, depth=2, pdelay=2):
        self.depth = depth
        self.pdelay = pdelay
        self.q = []
        self.pq = []

    def push(self, s1, s23, post=None):
        s1()
        self.q.append((s23, post))
        while len(self.q) > self.depth:
            self._pop()

    def _pop(self):
        s23, post = self.q.pop(0)
        s23()
        self.pq = [(c - 1, p) for c, p in self.pq]
        while self.pq and self.pq[0][0] <= 0:
            self.pq.pop(0)[1]()
        if post is not None:
            self.pq.append((self.pdelay, post))

    def flush(self):
        while self.q:
            self._pop()
        while self.pq:
            self.pq.pop(0)[1]()


def seq(*fns):
    def f(h):
        r = None
        for g in fns:
            r = g(h)
        return r
    return f


class Builder:
    def __init__(self, nseq, layers, mixers=("A", "B", "C", "D"), ffn=True):
        self.nseq = nseq
        self.layers = layers
        self.mixers = mixers
        self.ffn = ffn
        self.S = Sched()
        self.nc = bass.Bass("TRN2", target_bir_lowering=False)
        self.wq = []
        self.wi = 0
        self.w_issued = 0
        self.RING = 3
        self._banks_rr = {}

    def setup_mem(self):
        nc = self.nc
        ns = self.nseq
        dt = nc.dram_tensor
        self.x_d = dt("x", [ns, N, D], F32, kind="ExternalInput").ap()
        self.y_d = dt("y", [ns, N, D], F32, kind="ExternalOutput").ap()
        self.cT_d = dt("cT", [128, 8 * ns], F32, kind="ExternalInput").ap()
        self.wada_d = dt("wada", [DEPTH * 48, 128, 1024], F32, kind="ExternalInput").ap()
        self.wl_d = dt("wl", [DEPTH, 128, WL_COLS], F32, kind="ExternalInput").ap()
        self.vecs_d = dt("vecs", [128, NVEC], F32, kind="ExternalInput").ap()
        self.cst_d = dt("cst", [128, 128 * 7], F32, kind="ExternalInput").ap()
        self.t5aug_d = dt("t5aug", [33, 14], F32, kind="ExternalInput").ap()
        self.oh_d = dt("oh", [33, F_LEN], F32, kind="ExternalInput").ap()
        self.rpb_d = dt("rpbrp", [16, 15 * 127], F32, kind="ExternalInput").ap()
        self.Fd = dt("Fd", [14, F_LEN], BF16, kind="Internal").ap()
        self.RAd = dt("RAd", [DEPTH, 128, 2 * 960], BF16, kind="Internal").ap()
        self.modD = dt("modD", [ns, 128, DEPTH * 48], F32, kind="Internal").ap()
        self.wlb = dt("wlb", [DEPTH, 128, WL_COLS], BF16, kind="Internal").ap()

        total_bytes = 207 * 1024
        self.arena = nc.alloc_sbuf_tensor("arena", [128, total_bytes // 2], BF16)
        self._aoff = 0

        def carve(nbytes, dtype, shape):
            assert nbytes % 4 == 0
            o = self._aoff
            self._aoff += nbytes
            assert self._aoff <= total_bytes, ("SBUF overflow", self._aoff)
            ap = self.arena[:, o // 2:(o + nbytes) // 2]
            if dtype == F32:
                ap = ap.bitcast(F32)
            return self._shape(ap, shape)

        self.carve = carve
        K = 1024
        self.xT = carve(64 * K, F32, [8, N])
        self.hT = carve(32 * K, BF16, [8, N])
        self.identF = carve(512, F32, [128])
        self.cbf = carve(6 * 256, BF16, [6, 128])
        self.vecs = carve(NVEC * 4, F32, [NVEC])
        self.modc = carve(DEPTH * 48 * 4, F32, [DEPTH * 48])
        self.small = carve(256 * 4, F32, [256])
        self.ring = [carve(4 * K, BF16, [2048]) for _ in range(self.RING)]
        self.sq = [carve(1 * K, BF16, [512]) for _ in range(2)]
        self.lnt = carve(2 * K, F32, [512])
        self.rstd = [carve(2 * K, F32, [512]) for _ in range(2)]
        self.tmpf = [carve(2 * K, F32, [512]) for _ in range(2)]
        self.ov0 = self._aoff
        self.mixedT = carve(12 * K, BF16, [3, N])
        self.QT = carve(12 * K, BF16, [3, N])
        self.KT = carve(12 * K, BF16, [3, N])
        self.Vr = carve(12480, BF16, [6240])
        self.PT = [carve(1 * K, BF16, [512]) for _ in range(NPT)]
        self.ostage = [carve(2 * K, BF16, [1024]) for _ in range(2)]
        self.strip = carve(12800, BF16, [6400])
        self.att_end = self._aoff
        self._aoff = self.ov0
        self.gT = carve(24 * K, BF16, [6, N])
        self.acc = [carve(4 * K, F32, [2, 512]) for _ in range(4)]
        self.sg = [carve(2 * K, F32, [512]) for _ in range(4)]
        self.xstage = [carve(4 * K, F32, [1024]) for _ in range(4)]
        self._aoff = self.ov0
        self.oh_sb = carve(F_LEN * 4, F32, [F_LEN])
        self.f_sb = carve(F_LEN * 2, BF16, [F_LEN])
        self.t5aug_sb = carve(64, F32, [14])
        self.cst_sb = carve(128 * 7 * 4, F32, [7, 128])
        self.ra_f = carve(15 * 64 * 4, F32, [15, 64])
        self.ra_b = [carve(960 * 2, BF16, [960]) for _ in range(2)]
        self.wada_sb = [carve(4 * K, F32, [8, 128]) for _ in range(2)]
        self.modT = carve(DEPTH * 48 * ns * 4, F32, [ns, DEPTH * 48])
        self.sc = carve(8 * ns * 4, F32, [8, ns])
        self._aoff = max(self.att_end, self._aoff)
        self.sbuf_used = self._aoff
        self.psum = nc.alloc_psum_tensor("ps", [128, 8, 512], F32)

    @staticmethod
    def _shape(ap, shape):
        if len(shape) == 1:
            return ap
        names = "abcdef"[:len(shape)]
        s = "p (" + " ".join(names) + ") -> p " + " ".join(names)
        kw = {names[i]: shape[i] for i in range(len(shape) - 1)}
        return ap.rearrange(s, **kw)

    class Bank:
        def __init__(self, b, idx):
            self.idx = idx
            self.f32 = b.psum[:, idx, :]
            self.bf = b.psum[:, idx, :].bitcast(BF16)
            self.tok = b.S.tok("ps", idx)
            self.fresh = True

        def first(self):
            f = self.fresh
            self.fresh = False
            return f

    def banks_init(self):
        self.banks = [Builder.Bank(self, i) for i in range(8)]
        self.pools = {"S": [0, 1, 2, 3], "ACC": [4, 5, 6], "AUX": [7], "ALL7": [0, 1, 2, 3, 4, 5, 6],
                      "ALL8": [0, 1, 2, 3, 4, 5, 6, 7]}

    def bank(self, pool):
        lst = self.pools[pool]
        i = self._banks_rr.get(pool, 0)
        self._banks_rr[pool] = i + 1
        b = self.banks[lst[i % len(lst)]]
        b.fresh = True
        return b

    def MM(self, bank, out, lhsT, rhs, **kw):
        st = bank.first()
        return lambda h: h.matmul(out, lhsT, rhs, start=st, stop=True, skip_group_check=True, **kw)

    def wnext(self, l, key):
        if self.S.dry:
            self.wq.append((l, key))
            return self.ring[0], self.S.tok("ring", 0)
        i = self.wi
        assert self.wq[i] == (l, key), (self.wq[i], l, key)
        self.wi += 1
        while self.w_issued < min(len(self.wq), i + self.RING):
            self._wissue(self.w_issued)
            self.w_issued += 1
        s = i % self.RING
        return self.ring[s], self.S.tok("ring", s)

    def _wissue(self, i):
        l, key = self.wq[i]
        s = i % self.RING
        off, n = PLAN_OFF[key]
        dst = self.ring[s][:, 0:n]
        src = self.wlb[l, :, off:off + n]
        t = self.S.tok("ring", s)
        self.S.dma("sp", lambda h, sem, dst=dst, src=src: h.dma_start(out=dst, in_=src).then_inc(sem, 16),
                   1, writes=[t])

    def body(self):
        S = self.S
        S.dma("sp", lambda h, sem: h.dma_start(out=self.modc, in_=self.modD[self.loop_i]).then_inc(sem, 16), 1,
              writes=[S.tok("modc")])
        for en in ("dve", "act", "pool", "pe"):
            S.op(en, lambda h: h.nop(), reads=[S.tok("modc")])
        self.load_x(None)
        for l in self.layers:
            self.layer(None, l)
        self.store_x(None)

    def cb(self, i):
        return self.cbf[:, i, :]

    def vec(self, name, i=0, n=1):
        o, _ = VC[name]
        return self.vecs[:, o + i:o + i + n]

    def prologue(self):
        S = self.S
        ns = self.nseq
        tP = S.tok("pro")
        ld = lambda dst, src: (lambda h, sem: h.dma_start(out=dst, in_=src).then_inc(sem, 16))
        S.dma("sp", ld(self.vecs, self.vecs_d), 1, writes=[S.tok("vecs")])
        S.dma("sp", ld(self.cst_sb, self.cst_d.rearrange("p (a b) -> p a b", a=7)), 1, writes=[S.tok("cst")])
        S.dma("sp", ld(self.sc, self.cT_d.rearrange("p (a b) -> p a b", a=8)), 1, writes=[S.tok("sc")])
        S.dma("sp", ld(self.t5aug_sb[0:33, 0:14], self.t5aug_d), 1, writes=[S.tok("t5aug")])
        S.dma("sp", ld(self.oh_sb[0:33, :], self.oh_d), 1, writes=[S.tok("oh")])
        S.op("dve", lambda h: h.tensor_copy(out=self.identF, in_=self.cst_sb[:, 0, :]),
             reads=[S.tok("cst")], writes=[S.tok("c0")])
        S.op("dve", lambda h: h.tensor_copy(out=self.cbf, in_=self.cst_sb[:, 1:7, :]),
             reads=[S.tok("cst")], writes=[S.tok("c1")])
        S.op("act", lambda h: h.activation(out=self.sc, in_=self.sc, func=AF.Silu),
             reads=[S.tok("sc")], writes=[S.tok("sc")])
        for l in self.layers:
            for j in range(48):
                i = l * 48 + j
                wb = self.wada_sb[i % 2]
                tw = S.tok("wada", i % 2)
                S.dma("sp", ld(wb, self.wada_d[i].rearrange("p (a b) -> p a b", a=8)), 1, writes=[tw])
                bk = self.bank("ALL7")
                fns = [self.MM(bk, bk.f32[:, 0:ns], wb[:, kc, :], self.sc[:, kc, :]) for kc in range(8)]
                S.op("pe", seq(*fns), reads=[tw, S.tok("sc")], writes=[bk.tok])
                S.op("dve", lambda h, bk=bk, i=i: h.tensor_scalar(
                    out=self.modT[:, :, i], in0=bk.f32[:, 0:ns], scalar1=self.vec("b_ada", i), scalar2=None,
                    op0=ALU.add), reads=[bk.tok, S.tok("vecs")], writes=[S.tok("modT")])
        for c0 in range(0, F_LEN, 512):
            cn = min(512, F_LEN - c0)
            bk = self.bank("ALL7")
            S.op("pe", self.MM(bk, bk.f32[0:14, 0:cn], self.t5aug_sb[0:33, 0:14], self.oh_sb[0:33, c0:c0 + cn]),
                 reads=[S.tok("t5aug"), S.tok("oh")], writes=[bk.tok])
            S.op("dve", lambda h, bk=bk, c0=c0, cn=cn: h.tensor_copy(out=self.f_sb[0:14, c0:c0 + cn], in_=bk.f32[0:14, 0:cn]),
                 reads=[bk.tok], writes=[S.tok("fsb")])
        S.dma("sp", ld(self.Fd, self.f_sb[0:14, :]), 1, reads=[S.tok("fsb")], writes=[S.tok("Fd")])
        if "A" in self.mixers and not DBG.get("A_nostrip"):
            maskA = self.cst_sb[0:64, 0, :]
            for l in self.layers:
                for hh in range(4):
                    i = l * 4 + hh
                    pb = 64 * (hh % 2)

                    def ldra(h, sem, i=i, pb=pb):
                        for dr in range(15):
                            src = bass.AP(self.rpb_d.tensor, (i * 15 + dr) * 127, [[1, 64], [1, 64]])
                            h.dma_start(out=self.ra_f[pb:pb + 64, dr, :], in_=src).then_inc(sem, 16)
                    S.dma("sp", ldra, 15, writes=[S.tok("raf")])
                    rb = self.ra_b[i % 2]
                    S.op("dve", lambda h, rb=rb, pb=pb: h.tensor_tensor(
                        out=rb[pb:pb + 64, :].rearrange("p (a b) -> p a b", a=15), in0=self.ra_f[pb:pb + 64, :, :],
                        in1=self.cst_sb[pb:pb + 64, 6, 64:128].unsqueeze(1).broadcast_to([64, 15, 64]), op=ALU.add),
                        reads=[S.tok("raf"), S.tok("cst")], writes=[S.tok("rab", i % 2)])
                    S.dma("sp", ld(self.RAd[l, pb:pb + 64, (hh // 2) * 960:(hh // 2 + 1) * 960], rb[pb:pb + 64, :]), 1,
                          reads=[S.tok("rab", i % 2)], writes=[S.tok("RAd", i % 2)])
        sm = self.small
        o, _ = VC["qkg"]
        for l in range(DEPTH):
            for m in range(4):
                dd = 32.0 if m == 3 else 64.0
                c = (l * 4 + m) * 2
                S.op("dve", lambda h, c=c: h.tensor_copy(out=sm[:, c:c + 1], in_=self.vecs[:, o + c:o + c + 1]),
                     reads=[S.tok("vecs")], writes=[tP])
                S.op("dve", lambda h, c=c, dd=dd: h.tensor_scalar(
                    out=sm[:, c + 1:c + 2], in0=self.vecs[:, o + c + 1:o + c + 2], scalar1=math.sqrt(dd),
                    scalar2=None, op0=ALU.mult), reads=[S.tok("vecs")], writes=[tP])
        for l in range(DEPTH):
            for b in range(4):
                kc_ = o + (l * 4 + 3) * 2 + 1
                S.op("dve", lambda h, l=l, b=b, kc_=kc_: h.tensor_scalar(
                    out=sm[:, 160 + l * 4 + b:161 + l * 4 + b], in0=self.cst_sb[:, 5, 32 * b:32 * b + 1],
                    scalar1=self.vecs[:, kc_:kc_ + 1], scalar2=math.sqrt(32.0), op0=ALU.mult, op1=ALU.mult),
                    reads=[S.tok("vecs"), S.tok("cst")], writes=[tP])
        S.op("act", lambda h: h.activation(out=sm[:, 32:48], in_=self.vec("sink", 0, 16), func=AF.Exp),
             reads=[S.tok("vecs")], writes=[tP])
        lamv = self.vec("lam", 0, DEPTH * 128).rearrange("p (l f e) -> p l f e", l=DEPTH, f=4)
        S.op("dve", lambda h: h.tensor_tensor(out=self.tmpf[0][:, 0:DEPTH * 32].rearrange("p (l e) -> p l e", l=DEPTH),
                                              in0=lamv[:, :, 0, :], in1=lamv[:, :, 1, :], op=ALU.mult),
             reads=[S.tok("vecs")], writes=[S.tok("tmpf", 0)])
        S.op("dve", lambda h: h.tensor_tensor(out=self.tmpf[1][:, 0:DEPTH * 32].rearrange("p (l e) -> p l e", l=DEPTH),
                                              in0=lamv[:, :, 2, :], in1=lamv[:, :, 3, :], op=ALU.mult),
             reads=[S.tok("vecs")], writes=[S.tok("tmpf", 1)])
        S.op("dve", lambda h: h.tensor_reduce(out=sm[:, 52:56],
                                              in_=self.tmpf[0][:, 0:DEPTH * 32].rearrange("p (l e) -> p l e", l=DEPTH),
                                              axis=AX.X, op=ALU.add), reads=[S.tok("tmpf", 0)], writes=[tP])
        S.op("dve", lambda h: h.tensor_reduce(out=sm[:, 56:60],
                                              in_=self.tmpf[1][:, 0:DEPTH * 32].rearrange("p (l e) -> p l e", l=DEPTH),
                                              axis=AX.X, op=ALU.add), reads=[S.tok("tmpf", 1)], writes=[tP])
        S.op("act", lambda h: h.activation(out=sm[:, 52:60], in_=sm[:, 52:60], func=AF.Exp), reads=[tP], writes=[tP])
        S.op("dve", lambda h: h.tensor_tensor(out=sm[:, 48:52], in0=sm[:, 56:60], in1=sm[:, 52:56], op=ALU.subtract),
             reads=[tP], writes=[tP])
        S.op("dve", lambda h: h.tensor_tensor(out=sm[:, 48:52], in0=sm[:, 48:52], in1=self.vec("linit", 0, DEPTH),
                                              op=ALU.subtract), reads=[tP, S.tok("vecs")], writes=[tP])
        S.op("dve", lambda h: h.tensor_scalar(out=sm[:, 64:128], in0=self.vec("gnorm", 0, 64), scalar1=32.0,
                                              scalar2=None, op0=ALU.mult), reads=[S.tok("vecs")], writes=[tP])
        for l in self.layers:
            pieces = []
            for c0 in range(0, WL_COLS, 8192):
                cn = min(8192, WL_COLS - c0)
                pieces.append((self.wlb[l, :, c0:c0 + cn], self.wl_d[l, :, c0:c0 + cn]))

            def cast(h, sem, pieces=pieces):
                for dst, src in pieces:
                    h.dma_start(out=dst, in_=src).then_inc(sem, 16)
            S.dma("pool", cast, len(pieces), writes=[S.tok("wlb", l)])
        for s_ in range(ns):
            S.dma("sp", ld(self.modD[s_], self.modT[:, s_, :]), 1, reads=[S.tok("modT")], writes=[S.tok("modD", s_)])
        S.barrier()

    def qkg(self, l, m, which):
        c = (l * 4 + m) * 2 + which
        return self.small[:, c:c + 1]

    def mod(self, l, i, s):
        return self.modc[:, l * 48 + i * 8:l * 48 + i * 8 + 8]

    def ov_fence(self):
        S = self.S
        if S.dry:
            return
        toks = [t for k, t in S.toks.items() if k[0] in ("QT", "KT", "V", "mixedT", "PT", "ost", "strip", "gT", "acc",
                                                          "sg", "xst")]
        S.fence(toks)

    def load_x(self, s):
        S = self.S
        self.ov_fence()
        for t in range(16):
            st = self.xstage[t % 4]
            ts = S.tok("xst", t % 4)
            S.dma("sp", lambda h, sem, st=st, t=t: h.dma_start(
                out=st, in_=self.x_d[self.loop_i, t * 128:(t + 1) * 128, :]).then_inc(sem, 16), 1, writes=[ts])
            for half in range(2):
                bk = self.bank("ALL7")
                fns = [(lambda h, bk=bk, st=st, c=c, half=half: h.transpose(
                    bk.f32[:, c * 128:(c + 1) * 128], st[:, (half * 4 + c) * 128:(half * 4 + c + 1) * 128],
                    self.identF)) for c in range(4)]
                S.op("pe", seq(*fns), reads=[ts], writes=[bk.tok])
                eng = "dve" if half == 0 else "act"
                dst = self.xT[:, half * 4:half * 4 + 4, t * 128:(t + 1) * 128]
                srcp = bk.f32.rearrange("p (a b) -> p a b", a=4)
                if eng == "dve":
                    S.op("dve", lambda h, dst=dst, srcp=srcp: h.tensor_copy(out=dst, in_=srcp),
                         reads=[bk.tok], writes=[S.tok("xT", t // 4)])
                else:
                    S.op("act", lambda h, dst=dst, srcp=srcp: h.activation(out=dst, in_=srcp, func=AF.Copy),
                         reads=[bk.tok], writes=[S.tok("xT", t // 4)])

    def store_x(self, s):
        S = self.S
        self.ov_fence()
        for t in range(16):
            st = self.xstage[t % 4]
            ts = S.tok("xst", t % 4)
            for half in range(2):
                bk = self.bank("ALL7")
                fns = [(lambda h, bk=bk, c=c, half=half, t=t: h.transpose(
                    bk.f32[:, c * 128:(c + 1) * 128], self.xT[:, half * 4 + c, t * 128:(t + 1) * 128],
                    self.identF)) for c in range(4)]
                S.op("pe", seq(*fns), reads=[S.tok("xT", t // 4)], writes=[bk.tok])
                dst = st[:, half * 512:(half + 1) * 512]
                if half == 0:
                    S.op("dve", lambda h, dst=dst, bk=bk: h.tensor_copy(out=dst, in_=bk.f32), reads=[bk.tok], writes=[ts])
                else:
                    S.op("act", lambda h, dst=dst, bk=bk: h.activation(out=dst, in_=bk.f32, func=AF.Copy),
                         reads=[bk.tok], writes=[ts])
            S.dma("sp", lambda h, sem, st=st, t=t: h.dma_start(
                out=self.y_d[self.loop_i, t * 128:(t + 1) * 128, :], in_=st).then_inc(sem, 16), 1,
                  reads=[ts], writes=[S.tok("ydram", t % 4)], is_out=True)

    def norm(self, l, s, which):
        S = self.S
        sm = self.small
        ga = sm[:, 128 + which * 8:128 + which * 8 + 8]
        tg = S.tok("ga", which)
        gn = sm[:, 64 + (l * 2 + which) * 8:64 + (l * 2 + which) * 8 + 8]
        scale = self.mod(l, 1 + 3 * which, s)
        shift = self.mod(l, 0 + 3 * which, s)
        S.op("dve", lambda h: h.scalar_tensor_tensor(out=ga, in0=scale, scalar=1.0, in1=gn, op0=ALU.add, op1=ALU.mult),
             writes=[tg])
        ones = self.cb(2)
        for tb in range(4):
            blk = slice(tb * 512, (tb + 1) * 512)
            bk = self.bank("AUX")
            for c in range(8):
                sq = self.sq[c % 2]
                tq = S.tok("sq", c % 2)
                S.op("act", lambda h, sq=sq, c=c, blk=blk: h.activation(out=sq, in_=self.xT[:, c, blk], func=AF.Square),
                     reads=[S.tok("xT", tb)], writes=[tq])
                S.op("pe", self.MM(bk, bk.f32, ones, sq), reads=[tq], writes=[bk.tok])
            rs = self.rstd[tb % 2]
            tr = S.tok("rstd", tb % 2)
            S.op("act", lambda h, bk=bk: h.activation(out=self.lnt, in_=bk.f32, func=AF.Ln, bias=self.epsD, scale=1.0),
                 reads=[bk.tok], writes=[S.tok("lnt")])
            S.op("act", lambda h, rs=rs: h.activation(out=rs, in_=self.lnt, func=AF.Exp, scale=-0.5),
                 reads=[S.tok("lnt")], writes=[tr])
            for c in range(8):
                tf = self.tmpf[c % 2]
                tt = S.tok("tmpf", c % 2)
                S.op("dve", lambda h, tf=tf, c=c, blk=blk, rs=rs: h.tensor_tensor(
                    out=tf, in0=self.xT[:, c, blk], in1=rs, op=ALU.mult),
                    reads=[S.tok("xT", tb), tr], writes=[tt])
                if c % 2 == 0:
                    S.op("act", lambda h, tf=tf, c=c, blk=blk: h.activation(
                        out=self.hT[:, c, blk], in_=tf, func=AF.Identity, bias=shift[:, c:c + 1], scale=ga[:, c:c + 1]),
                        reads=[tt, tg], writes=[S.tok("hT", tb)])
                else:
                    S.op("dve", lambda h, tf=tf, c=c, blk=blk: h.tensor_scalar(
                        out=self.hT[:, c, blk], in0=tf, scalar1=ga[:, c:c + 1], scalar2=shift[:, c:c + 1],
                        op0=ALU.mult, op1=ALU.add), reads=[tt, tg], writes=[S.tok("hT", tb)])

    def proj_qk(self, wslot, wtok, wcol0, dst, dtok_name, dchunk, gvec, bd, dsz, multi=None):
        S = self.S
        w = wslot[:, 0:8 * 0 + 2048]
        for tb in range(4):
            blk = slice(tb * 512, (tb + 1) * 512)
            bk = self.bank("S")
            fns = [self.MM(bk, bk.f32, self._w(wslot, kc, wcol0, 128), self.hT[:, kc, blk]) for kc in range(8)]
            S.op("pe", seq(*fns), reads=[wtok, S.tok("hT", tb)], writes=[bk.tok])
            sq = self.sq[tb % 2]
            tq = S.tok("sq", tb % 2)
            S.op("act", lambda h, sq=sq, bk=bk: h.activation(out=sq, in_=bk.f32, func=AF.Square),
                 reads=[bk.tok], writes=[tq])
            b2 = self.bank("AUX")
            S.op("pe", self.MM(b2, b2.f32, bd, sq), reads=[tq], writes=[b2.tok])
            rs = self.rstd[tb % 2]
            tr = S.tok("rstd", tb % 2)
            epsb = self.eps64 if dsz == 64 else self.eps32
            S.op("act", lambda h, b2=b2, epsb=epsb: h.activation(out=self.lnt, in_=b2.f32, func=AF.Ln, bias=epsb, scale=1.0),
                 reads=[b2.tok], writes=[S.tok("lnt")])
            S.op("act", lambda h, rs=rs: h.activation(out=rs, in_=self.lnt, func=AF.Exp, scale=-0.5),
                 reads=[S.tok("lnt")], writes=[tr])
            if multi is None:
                S.op("dve", lambda h, bk=bk, rs=rs, blk=blk: h.scalar_tensor_tensor(
                    out=dst[:, dchunk, blk], in0=bk.f32, scalar=gvec, in1=rs, op0=ALU.mult, op1=ALU.mult),
                    reads=[bk.tok, tr], writes=[S.tok(dtok_name, dchunk, tb)])
            else:
                for (dv, tn, ti, gcol) in multi:
                    S.op("dve", lambda h, bk=bk, rs=rs, blk=blk, dv=dv, gcol=gcol: h.scalar_tensor_tensor(
                        out=dv[:, blk], in0=bk.f32, scalar=gcol, in1=rs, op0=ALU.mult, op1=ALU.mult),
                        reads=[bk.tok, tr], writes=[S.tok(tn, ti, tb)])

    def _w(self, wslot, kc, col0, ncols, wtot=None):
        return wslot[:, 0:8 * self._wtot].rearrange("p (a b) -> p a b", a=8)[:, kc, col0:col0 + ncols]

    def proj_v(self, wslot, wtok, ncols, nh, V, hoff, rows64=False):
        S = self.S
        if not rows64:
            for t in range(16):
                bk = self.bank("S")
                fns = [self.MM(bk, bk.f32[:, 0:ncols], self.hT[:, kc, t * 128:(t + 1) * 128],
                               self._w(wslot, kc, 0, ncols)) for kc in range(8)]
                S.op("pe", seq(*fns), reads=[wtok, S.tok("hT", t // 4)], writes=[bk.tok])
                S.op("dve", lambda h, bk=bk, t=t: h.tensor_copy(
                    out=V[:, t, hoff:hoff + nh, 0:64], in_=bk.f32[:, 0:ncols].rearrange("p (a b) -> p a b", a=nh)),
                    reads=[bk.tok], writes=[S.tok("V", t // 4)])
        else:
            for r2 in range(16):
                bk = self.bank("S")
                fns = []
                for rr in range(2):
                    r = 2 * r2 + rr
                    fns += [self.MM(bk, bk.f32[0:64, rr * ncols:(rr + 1) * ncols], self.hT[:, kc, r * 64:(r + 1) * 64],
                                    self._w(wslot, kc, 0, ncols)) for kc in range(8)]
                S.op("pe", seq(*fns), reads=[wtok, S.tok("hT", r2 // 4)], writes=[bk.tok])
                S.op("dve", lambda h, bk=bk, r2=r2: h.tensor_copy(
                    out=V[0:64, 2 * r2:2 * r2 + 2, :, 0:64],
                    in_=bk.f32[0:64, 0:2 * ncols].rearrange("p (r a b) -> p r a b", r=2, a=nh)),
                    reads=[bk.tok], writes=[S.tok("V", r2 // 4)])

    def ones_col(self, V, nparts=128):
        S = self.S
        vt = [S.tok("V", i) for i in range(4)]
        S.op("pool", lambda h: h.memset(V[0:nparts, :, :, 64:65], 1.0), writes=vt)

    def wout_pass(self, l, s, slots, nks, kbase0=0):
        S = self.S
        gate = self.mod(l, 2, s)
        for m in range(8):
            for tb in range(4):
                blk = slice(tb * 512, (tb + 1) * 512)
                bk = self.bank("ALL7")
                fns = []
                reads = []
                kbase = kbase0
                for (slot, tk, nk) in slots:
                    wv = slot[:, 0:nk * 1024].rearrange("p (a b) -> p a b", a=nk)
                    for kc in range(nk):
                        fns.append(self.MM(bk, bk.f32, wv[:, kc, m * 128:(m + 1) * 128], self.mixedT[:, kbase + kc, blk]))
                        reads.append(S.tok("mixedT", kbase + kc, tb))
                    reads.append(tk)
                    kbase += nk
                S.op("pe", seq(*fns), reads=reads, writes=[bk.tok])
                S.op("dve", lambda h, bk=bk, m=m, blk=blk: h.scalar_tensor_tensor(
                    out=self.xT[:, m, blk], in0=bk.f32, scalar=gate[:, m:m + 1], in1=self.xT[:, m, blk],
                    op0=ALU.mult, op1=ALU.add), reads=[bk.tok], writes=[S.tok("xT", tb)])

    def transposes(self, src_fn, ntile_cols, nchunks, dst_cols, tb, src_tok, kparts=128, c0=0):
        S = self.S
        bk = self.bank("AUX")
        ident = self.cb(0)
        fns = [(lambda h, c=c: h.transpose(bk.bf[:, c * ntile_cols:(c + 1) * ntile_cols], src_fn(c),
                                           ident[0:kparts, 0:kparts])) for c in range(nchunks)]
        S.op("pe", seq(*fns), reads=[src_tok], writes=[bk.tok])
        S.op("act", lambda h: h.activation(
            out=self.mixedT[:, c0:c0 + nchunks, dst_cols],
            in_=bk.bf[:, 0:nchunks * ntile_cols].rearrange("p (a b) -> p a b", a=nchunks), func=AF.Copy),
            reads=[bk.tok], writes=[S.tok("mixedT", c0 + c, tb) for c in range(nchunks)])

    def load_strip(self, names):
        S = self.S
        ts = S.tok("strip")
        views = {}
        fl = []
        off = 0
        for nm in names:
            band, hb, nh = STRIP_INFO[nm]
            W = STRIP_W[nm]
            v = self.strip[:, off:off + nh * W].rearrange("p (a b) -> p a b", a=nh)
            views[nm] = v
            for hh in range(nh):
                src = bass.AP(self.Fd.tensor, (hb + hh) * F_LEN + STRIP_OFF[nm], [[1, 128], [1, W]])
                fl.append((v[:, hh, :], src))
            off += nh * W

        def f(h, sem):
            for dst, src in fl:
                h.dma_start(out=dst, in_=src).then_inc(sem, 16)
        S.dma("sp", f, len(fl), writes=[ts])
        return views

    def banded_heads(self, pipe, qt, heads, acc, post):
        S = self.S
        J = self.cb(1)
        steps = []
        for hd in heads:
            band = hd["band"]
            kts = [kt for kt in range(qt - band, qt + band + 1) if 0 <= kt < 16]
            for g0 in range(0, len(kts), 4):
                steps.append((hd, kts[g0:g0 + 4]))
        for si, (hd, grp) in enumerate(steps):
            st = {}

            def s1(hd=hd, grp=grp, st=st):
                band = hd["band"]
                bk = self.bank("S")
                st["bk"] = bk
                fns = []
                fnj = []
                reads = [S.tok("QT", hd["qc"], qt // 4), S.tok("strip")]
                for i, kt in enumerate(grp):
                    o = bk.f32[:, i * 128:(i + 1) * 128]
                    pb = hd["pb"]
                    fns.append(self.MM(bk, o, self.KT[pb:pb + 64, hd["kc"], kt * 128:(kt + 1) * 128],
                                       self.QT[pb:pb + 64, hd["qc"], qt * 128:(qt + 1) * 128]))
                    tk = S.tok("KT", hd["kc"], kt // 4)
                    if tk not in reads:
                        reads.append(tk)
                for i, kt in enumerate(grp):
                    o = bk.f32[:, i * 128:(i + 1) * 128]
                    off = 128 * (band - (kt - qt))
                    fnj.append(self.MM(bk, o, J, hd["strip"][:, off:off + 128]))
                S.op("pe", seq(*(fns + fnj)), reads=reads, writes=[bk.tok])

            def s23(hd=hd, grp=grp, st=st):
                bk = st["bk"]
                n = len(grp) * 128
                pt = self.PT[self._pti % NPT]
                tp = S.tok("PT", self._pti % NPT)
                self._pti += 1
                S.op("act", lambda h, bk=bk, pt=pt, n=n: h.activation(out=pt[:, 0:n], in_=bk.f32[:, 0:n], func=AF.Exp),
                     reads=[bk.tok], writes=[tp])
                fns = []
                reads = [tp]
                for i, kt in enumerate(grp):
                    fns.append(self.MM(acc, acc.f32[:, hd["col"] * 65:hd["col"] * 65 + 65], pt[:, i * 128:(i + 1) * 128],
                                       hd["V"][:, kt, hd["vh"], :]))
                    tv = S.tok("V", kt // 4)
                    if tv not in reads:
                        reads.append(tv)
                S.op("pe", seq(*fns), reads=reads, writes=[acc.tok])
            pipe.push(s1, s23, post if si == len(steps) - 1 else None)

    def layer(self, s, l):
        S = self.S
        self._pti = 0
        self.ov_fence()
        self.norm(l, s, 0)
        if "A" in self.mixers:
            self.mixer_A(s, l)
        if "B" in self.mixers:
            self.mixer_B(s, l)
        if "C" in self.mixers:
            self.mixer_C(s, l)
        if "D" in self.mixers:
            self.mixer_D(s, l)
        if self.ffn:
            self.ov_fence()
            self.norm(l, s, 1)
            self.ffn_block(s, l)

    def mixer_A(self, s, l):
        S = self.S
        bd64 = self.cb(3)
        J64 = self.cb(5)
        V = self.Vr[:, 0:32 * 2 * 65].rearrange("p (r a b) -> p r a b", r=32, a=2)
        ts = S.tok("strip")
        stripA = self.strip[:, 0:1920].rearrange("p (a b) -> p a b", a=2)
        src = self.RAd[l, :, :]
        if not DBG.get("A_nostrip"):
            S.dma("sp", lambda h, sem: h.dma_start(out=self.strip[:, 0:1920], in_=src).then_inc(sem, 16), 1, writes=[ts])
        for p in range(2):
            self._wtot = 128
            wq, tq = self.wnext(l, ("A", p, "q"))
            self.proj_qk(wq, tq, 0, self.QT, "QT", 0, self.qkg(l, 0, 0), bd64, 64)
            wk, tk = self.wnext(l, ("A", p, "k"))
            self.proj_qk(wk, tk, 0, self.KT, "KT", 0, self.qkg(l, 0, 1), bd64, 64)
            wv, tv = self.wnext(l, ("A", p, "v"))
            self.ones_col(V, 64)
            self.proj_v(wv, tv, 128, 2, V, 0, rows64=True)
            wo, to = self.wnext(l, ("A", p, "o"))
            pipe = Pipe(PDEPTH)
            ident = self.cb(0)
            for r in range(32):
                rs_ = min(max(r - 4, 0), 24)
                dr0 = rs_ - r + 7
                ost = self.ostage[(r // 4) % 2]
                tos = S.tok("ost", (r // 4) % 2)
                acc = self.bank("ACC")

                def post_row(r=r, acc=acc, ost=ost, tos=tos):
                    rd = self.tmpf[r % 2]
                    trd = S.tok("tmpf", r % 2)
                    accv = acc.f32[0:64, 0:130].rearrange("p (a b) -> p a b", a=2)
                    S.op("dve", lambda h: h.reciprocal(out=rd[0:64, 0:2], in_=accv[:, :, 64]),
                         reads=[acc.tok], writes=[trd])
                    S.op("dve", lambda h: h.tensor_tensor(
                        out=ost[0:64, (r % 4) * 128:(r % 4) * 128 + 128].rearrange("p (a b) -> p a b", a=2),
                        in0=accv[:, :, 0:64], in1=rd[0:64, 0:2].unsqueeze(2).broadcast_to([64, 2, 64]), op=ALU.mult),
                        reads=[acc.tok, trd], writes=[tos])
                    if r % 4 == 3:
                        r0 = r - 3
                        bk = self.bank("AUX")
                        fns = [(lambda h, c=c: h.transpose(bk.bf[:, c * 64:(c + 1) * 64], ost[0:64, c * 128:(c + 1) * 128],
                                                           ident[0:64, 0:64])) for c in range(4)]
                        S.op("pe", seq(*fns), reads=[tos], writes=[bk.tok])
                        S.op("act", lambda h: h.activation(
                            out=self.mixedT[:, 0, r0 * 64:r0 * 64 + 256], in_=bk.bf[:, 0:256], func=AF.Copy),
                            reads=[bk.tok], writes=[S.tok("mixedT", 0, r0 // 8)])

                for hh in range(2):
                    pb = 64 * hh
                    st = {}

                    def s1(r=r, rs_=rs_, dr0=dr0, pb=pb, st=st, p=p):
                        bk = self.bank("S")
                        st["bk"] = bk
                        fns = []
                        reads = [S.tok("QT", 0, r // 8), ts]
                        for i in range(8):
                            kr = rs_ + i
                            fns.append(self.MM(bk, bk.f32[0:64, i * 64:(i + 1) * 64],
                                               self.KT[pb:pb + 64, 0, kr * 64:(kr + 1) * 64],
                                               self.QT[pb:pb + 64, 0, r * 64:(r + 1) * 64]))
                            tk_ = S.tok("KT", 0, kr // 8)
                            if tk_ not in reads:
                                reads.append(tk_)
                        fns.append(self.MM(bk, bk.f32[0:64, :], J64[pb:pb + 64, 0:64],
                                           stripA[pb:pb + 64, p, dr0 * 64:dr0 * 64 + 512]))
                        S.op("pe", seq(*fns), reads=reads, writes=[bk.tok])

                    def s23(rs_=rs_, hh=hh, st=st, acc=acc):
                        bk = st["bk"]
                        pt = self.PT[self._pti % NPT]
                        tp = S.tok("PT", self._pti % NPT)
                        self._pti += 1
                        S.op("act", lambda h: h.activation(out=pt[0:64, :], in_=bk.f32[0:64, :], func=AF.Exp),
                             reads=[bk.tok], writes=[tp])
                        fns = []
                        reads = [tp]
                        for i in range(8):
                            kr = rs_ + i
                            fns.append(self.MM(acc, acc.f32[0:64, hh * 65:hh * 65 + 65], pt[0:64, i * 64:(i + 1) * 64],
                                               V[0:64, kr, hh, :]))
                            tv_ = S.tok("V", kr // 8)
                            if tv_ not in reads:
                                reads.append(tv_)
                        S.op("pe", seq(*fns), reads=reads, writes=[acc.tok])
                    pipe.push(s1, s23, post_row if hh == 1 else None)
            pipe.flush()
            self.wout_pass(l, s, [(wo, to, 1)], None)

    def mixer_B(self, s, l):
        S = self.S
        bd64 = self.cb(3)
        V = self.Vr[:, 0:16 * 2 * 65].rearrange("p (t a b) -> p t a b", t=16, a=2)
        sv = self.load_strip(["B"])["B"]
        self._wtot = 256
        wq, tq = self.wnext(l, ("B", "q"))
        for c in range(2):
            self.proj_qk(wq, tq, c * 128, self.QT, "QT", c, self.qkg(l, 1, 0), bd64, 64)
        wk, tk = self.wnext(l, ("B", "k"))
        for c in range(2):
            self.proj_qk(wk, tk, c * 128, self.KT, "KT", c, self.qkg(l, 1, 1), bd64, 64)
        self._wtot = 128
        wv, tv = self.wnext(l, ("B", "v"))
        self.ones_col(V)
        self.proj_v(wv, tv, 128, 2, V, 0)
        wo, to = self.wnext(l, ("B", "o"))
        esink = self.small[:, 32 + l * 4:32 + l * 4 + 4]
        pipe = Pipe(PDEPTH)
        for qt in range(16):
            acc = self.bank("ACC")
            heads = [dict(qc=h // 2, pb=64 * (h % 2), kc=h // 2, vh=h // 2, strip=sv[:, h, :], band=1, col=h, V=V)
                     for h in range(4)]

            def post(qt=qt, acc=acc):
                accv = acc.f32[:, 0:260].rearrange("p (a b) -> p a b", a=4)
                rd = self.tmpf[qt % 2]
                trd = S.tok("tmpf", qt % 2)
                S.op("dve", lambda h, rd=rd, accv=accv: h.tensor_tensor(out=rd[:, 0:4], in0=accv[:, :, 64], in1=esink,
                                                                       op=ALU.add), reads=[acc.tok], writes=[trd])
                S.op("dve", lambda h, rd=rd: h.reciprocal(out=rd[:, 4:8], in_=rd[:, 0:4]), reads=[trd], writes=[trd])
                ost = self.ostage[qt % 2]
                tos = S.tok("ost", qt % 2)
                S.op("dve", lambda h, rd=rd, accv=accv, ost=ost: h.tensor_tensor(
                    out=ost[:, 0:256].rearrange("p (a b) -> p a b", a=4), in0=accv[:, :, 0:64],
                    in1=rd[:, 4:8].unsqueeze(2).broadcast_to([128, 4, 64]), op=ALU.mult),
                    reads=[acc.tok, trd], writes=[tos])
                self.transposes(lambda c, ost=ost: ost[:, c * 128:(c + 1) * 128], 128, 2,
                                slice(qt * 128, (qt + 1) * 128), qt // 4, tos)
            self.banded_heads(pipe, qt, heads, acc, post)
        pipe.flush()
        self.wout_pass(l, s, [(wo, to, 2)], None)

    def mixer_C(self, s, l):
        S = self.S
        bd64 = self.cb(3)
        V = self.Vr[:, 0:16 * 6 * 65].rearrange("p (t a b) -> p t a b", t=16, a=6)
        svs = self.load_strip(["C0", "C1", "C2"])
        self._wtot = 256
        w, t = self.wnext(l, ("C", "q0"))
        for c in range(2):
            self.proj_qk(w, t, c * 128, self.QT, "QT", c, self.qkg(l, 2, 0), bd64, 64)
        self._wtot = 128
        w, t = self.wnext(l, ("C", "q1"))
        self.proj_qk(w, t, 0, self.QT, "QT", 2, self.qkg(l, 2, 0), bd64, 64)
        self._wtot = 256
        w, t = self.wnext(l, ("C", "k0"))
        for c in range(2):
            self.proj_qk(w, t, c * 128, self.KT, "KT", c, self.qkg(l, 2, 1), bd64, 64)
        self._wtot = 128
        w, t = self.wnext(l, ("C", "k1"))
        self.proj_qk(w, t, 0, self.KT, "KT", 2, self.qkg(l, 2, 1), bd64, 64)
        self.ones_col(V)
        self._wtot = 192
        w, t = self.wnext(l, ("C", "v0"))
        self.proj_v(w, t, 192, 3, V, 0)
        w, t = self.wnext(l, ("C", "v1"))
        self.proj_v(w, t, 192, 3, V, 3)
        bands = (1, 2, 8)
        names = ("C0", "C1", "C2")
        pipe = Pipe(PDEPTH)
        for qt in range(16):
            acc = self.bank("ACC")
            heads = []
            for g in range(3):
                for j in range(2):
                    hh = 2 * g + j
                    heads.append(dict(qc=g, pb=64 * j, kc=g, vh=hh, strip=svs[names[g]][:, j, :], band=bands[g],
                                      col=hh, V=V))

            def post(qt=qt, acc=acc):
                accv = acc.f32[:, 0:390].rearrange("p (g j b) -> p g j b", g=3, j=2)
                rd = self.tmpf[qt % 2]
                trd = S.tok("tmpf", qt % 2)
                S.op("dve", lambda h, rd=rd, accv=accv: h.tensor_copy(
                    out=rd[:, 8:14].rearrange("p (g j) -> p g j", g=3), in_=accv[:, :, :, 64]),
                    reads=[acc.tok], writes=[trd])
                S.op("dve", lambda h, rd=rd: h.tensor_tensor(out=rd[:, 0:2], in0=rd[:, 8:10], in1=rd[:, 10:12], op=ALU.add),
                     reads=[trd], writes=[trd])
                S.op("dve", lambda h, rd=rd: h.tensor_tensor(out=rd[:, 0:2], in0=rd[:, 0:2], in1=rd[:, 12:14], op=ALU.add),
                     reads=[trd], writes=[trd])
                S.op("dve", lambda h, rd=rd: h.reciprocal(out=rd[:, 4:6], in_=rd[:, 0:2]), reads=[trd], writes=[trd])
                ost = self.ostage[qt % 2]
                tos = S.tok("ost", qt % 2)
                for g in range(3):
                    S.op("dve", lambda h, rd=rd, accv=accv, ost=ost, g=g: h.tensor_tensor(
                        out=ost[:, g * 128:(g + 1) * 128].rearrange("p (a b) -> p a b", a=2), in0=accv[:, g, :, 0:64],
                        in1=rd[:, 4:6].unsqueeze(2).broadcast_to([128, 2, 64]), op=ALU.mult),
                        reads=[acc.tok, trd], writes=[tos])
                self.transposes(lambda c, ost=ost: ost[:, c * 128:(c + 1) * 128], 128, 3,
                                slice(qt * 128, (qt + 1) * 128), qt // 4, tos)
            self.banded_heads(pipe, qt, heads, acc, post)
        pipe.flush()
        wo0, to0 = self.wnext(l, ("C", "o0"))
        self.wout_pass(l, s, [(wo0, to0, 2)], None)
        wo1, to1 = self.wnext(l, ("C", "o1"))
        self.wout_pass(l, s, [(wo1, to1, 1)], None, kbase0=2)

    def mixer_D(self, s, l):
        S = self.S
        bd32 = self.cb(4)
        J = self.cb(1)
        V = self.Vr[:, 0:16 * 4 * 65].rearrange("p (t a b) -> p t a b", t=16, a=4)
        sv = self.load_strip(["D"])["D"]
        nlam = self.small[:, 48 + l:49 + l]
        sub = self.vec("subln", l * 64, 64)
        om = 1.0 - lambda_init(l)
        ktm = [(self.KT[:, 0, :], "KT", 0), (self.KT[:, 1, :], "KT", 1), (self.KT[:, 2, :], "KT", 2),
               (self.QT[:, 2, :], "QT", 2)]
        for c in range(2):
            self._wtot = 128
            w, t = self.wnext(l, ("D", "q%d" % c))
            self.proj_qk(w, t, 0, self.QT, "QT", c, self.qkg(l, 3, 0), bd32, 32)
            w, t = self.wnext(l, ("D", "k%d" % c))
            multi = [(ktm[b][0], ktm[b][1], ktm[b][2], self.small[:, 160 + l * 4 + b:161 + l * 4 + b]) for b in range(4)]
            self.proj_qk(w, t, 0, None, None, None, None, bd32, 32, multi=multi)
            if c == 0:
                self.ones_col(V)
                self._wtot = 256
                w, t = self.wnext(l, ("D", "v"))
                self.proj_v(w, t, 256, 4, V, 0)
            pipe = Pipe(PDEPTH)
            for qb in range(4):
                ost = self.ostage[(2 * c + qb) % 2]
                tos = S.tok("ost", (2 * c + qb) % 2)
                for hh in (2 * c, 2 * c + 1):
                    accs = [self.bank("ACC"), self.bank("ACC")]

                    def post_head(qb=qb, hh=hh, accs=accs, ost=ost, tos=tos):
                        a0 = accs[0].f32[:, 0:260].rearrange("p (a b) -> p a b", a=4)
                        a1 = accs[1].f32[:, 0:260].rearrange("p (a b) -> p a b", a=4)
                        rd = self.rstd[hh % 2]
                        trd = S.tok("rstd", hh % 2)
                        t0 = self.tmpf[0]
                        t1 = self.tmpf[1]
                        t0v = t0[:, 0:256].rearrange("p (a b) -> p a b", a=4)
                        t1v = t1[:, 0:256].rearrange("p (a b) -> p a b", a=4)
                        S.op("dve", lambda h: h.reciprocal(out=rd[:, 0:4], in_=a0[:, :, 64]),
                             reads=[accs[0].tok], writes=[trd])
                        S.op("dve", lambda h: h.reciprocal(out=rd[:, 4:8], in_=a1[:, :, 64]),
                             reads=[accs[1].tok, trd], writes=[trd])
                        S.op("dve", lambda h: h.tensor_scalar(out=rd[:, 4:8], in0=rd[:, 4:8], scalar1=nlam, scalar2=None,
                                                              op0=ALU.mult), reads=[trd], writes=[trd])
                        S.op("dve", lambda h: h.tensor_tensor(
                            out=t0v, in0=a0[:, :, 0:64], in1=rd[:, 0:4].unsqueeze(2).broadcast_to([128, 4, 64]), op=ALU.mult),
                            reads=[accs[0].tok, trd], writes=[S.tok("tmpf", 0)])
                        S.op("dve", lambda h: h.tensor_tensor(
                            out=t1v, in0=a1[:, :, 0:64], in1=rd[:, 4:8].unsqueeze(2).broadcast_to([128, 4, 64]), op=ALU.mult),
                            reads=[accs[1].tok, trd], writes=[S.tok("tmpf", 1)])
                        S.op("dve", lambda h: h.tensor_tensor(out=t0[:, 0:256], in0=t0[:, 0:256], in1=t1[:, 0:256], op=ALU.add),
                             reads=[S.tok("tmpf", 1)], writes=[S.tok("tmpf", 0)])
                        S.op("act", lambda h: h.activation(out=t1[:, 0:256], in_=t0[:, 0:256], func=AF.Square),
                             reads=[S.tok("tmpf", 0)], writes=[S.tok("tmpf", 1)])
                        S.op("dve", lambda h: h.tensor_reduce(out=rd[:, 8:12], in_=t1v, axis=AX.X, op=ALU.add),
                             reads=[S.tok("tmpf", 1)], writes=[trd])
                        S.op("act", lambda h: h.activation(out=rd[:, 8:12], in_=rd[:, 8:12], func=AF.Ln, bias=self.eps64,
                                                           scale=1.0), reads=[trd], writes=[trd])
                        S.op("act", lambda h: h.activation(out=rd[:, 8:12], in_=rd[:, 8:12], func=AF.Exp, scale=-0.5),
                             reads=[trd], writes=[trd])
                        S.op("dve", lambda h: h.tensor_tensor(
                            out=t0v, in0=t0v, in1=rd[:, 8:12].unsqueeze(2).broadcast_to([128, 4, 64]), op=ALU.mult),
                            reads=[trd], writes=[S.tok("tmpf", 0)])
                        S.op("dve", lambda h: h.scalar_tensor_tensor(
                            out=ost[:, 0:1024].rearrange("p (q a b) -> p q a b", q=4, a=4)[:, :, hh, :], in0=t0v,
                            scalar=8.0 * om, in1=sub.unsqueeze(1).broadcast_to([128, 4, 64]), op0=ALU.mult, op1=ALU.mult),
                            reads=[S.tok("tmpf", 0)], writes=[tos])
                        if hh % 2 == 1:
                            cc = hh // 2
                            for jq in range(4):
                                qt = 4 * qb + jq
                                self.transposes(lambda c_, jq=jq, cc=cc: ost[:, jq * 256 + cc * 128:jq * 256 + (cc + 1) * 128],
                                                128, 1, slice(qt * 128, (qt + 1) * 128), qb, tos, c0=cc)

                    for i in range(2):
                        pb = 32 * (2 * (hh % 2) + i)
                        acc = accs[i]
                        for kt in range(16):
                            d0 = kt - 4 * qb
                            mixed = -1 <= d0 <= 4
                            st = {}

                            def s1(pb=pb, kt=kt, d0=d0, mixed=mixed, st=st, c=c, hh=hh, qb=qb):
                                bk = self.bank("S")
                                st["bk"] = bk
                                kb = ktm[pb // 32]
                                fns = [self.MM(bk, bk.f32, kb[0][:, kt * 128:(kt + 1) * 128],
                                               self.QT[:, c, qb * 512:(qb + 1) * 512])]
                                if mixed:
                                    off = 128 * (4 - d0)
                                    fns.append(self.MM(bk, bk.f32, J, sv[:, hh, off:off + 512]))
                                S.op("pe", seq(*fns), reads=[S.tok("QT", c, qb), S.tok(kb[1], kb[2], kt // 4), S.tok("strip")],
                                     writes=[bk.tok])

                            def s23(kt=kt, d0=d0, mixed=mixed, st=st, hh=hh, acc=acc):
                                bk = st["bk"]
                                pt = self.PT[self._pti % NPT]
                                tp = S.tok("PT", self._pti % NPT)
                                self._pti += 1
                                if mixed:
                                    S.op("act", lambda h: h.activation(out=pt, in_=bk.f32, func=AF.Exp),
                                         reads=[bk.tok], writes=[tp])
                                else:
                                    cb_ = self.vec("t5c", (0 if d0 < 0 else 4) + hh)
                                    S.op("act", lambda h: h.activation(out=pt, in_=bk.f32, func=AF.Exp, bias=cb_, scale=1.0),
                                         reads=[bk.tok], writes=[tp])
                                fns = [self.MM(acc, acc.f32[:, jq * 65:jq * 65 + 65], pt[:, jq * 128:(jq + 1) * 128],
                                               V[:, kt, hh, :]) for jq in range(4)]
                                S.op("pe", seq(*fns), reads=[tp, S.tok("V", kt // 4)], writes=[acc.tok])
                            pipe.push(s1, s23, post_head if (i == 1 and kt == 15) else None)
            pipe.flush()
        wo, to = self.wnext(l, ("D", "o"))
        self.wout_pass(l, s, [(wo, to, 2)], None)

    def ffn_block(self, s, l):
        S = self.S
        gate = self.mod(l, 5, s)
        co, _ = VC["conv"]

        def cw(tap, jc):
            c = co + (l * 4 + tap) * 44 + jc
            return self.vecs[:, c:c + 1]
        self._wtot = 256
        blkc = 0
        for qi, (j0, nj) in enumerate(QUARTERS):
            for jj in range(nj):
                j = j0 + jj
                w, tw = self.wnext(l, ("U", j))
                for (o0, o1, i0, i1) in FFN_BLOCKS:
                    nin = i1 - i0
                    nout = o1 - o0
                    bks = []
                    for vg in range(2):
                        bk = self.bank("ALL8")
                        fns = [self.MM(bk, bk.f32[:, 0:nin], self._w(w, kc, vg * 128, 128), self.hT[:, kc, i0:i1])
                               for kc in range(8)]
                        rt = list({S.tok("hT", i0 // 512), S.tok("hT", (i1 - 1) // 512)})
                        S.op("pe", seq(*fns), reads=[tw] + rt, writes=[bk.tok])
                        bks.append(bk)
                    ab = self.acc[blkc % 4]
                    tas = [S.tok("acc", blkc % 4, 0), S.tok("acc", blkc % 4, 1)]
                    sgb = self.sg[blkc % 4]
                    tsg = S.tok("sg", blkc % 4)
                    blkc += 1
                    for vg in range(2):
                        ta = tas[vg]
                        bk = bks[vg]
                        jc = j + 22 * vg
                        a = ab[:, vg, :]
                        c1 = o0 - i0
                        S.op("act", lambda h, a=a, bk=bk, jc=jc, c1=c1, nout=nout: h.activation(
                            out=a[:, 0:nout], in_=bk.f32[:, c1:c1 + nout], func=AF.Identity, bias=cw(3, jc),
                            scale=cw(1, jc)), reads=[bk.tok], writes=[ta])
                        ta0 = max(o0, 1)
                        n0 = o1 - ta0
                        S.op("dve", lambda h, a=a, bk=bk, jc=jc, ta0=ta0, n0=n0, o0=o0, i0=i0: h.scalar_tensor_tensor(
                            out=a[:, ta0 - o0:ta0 - o0 + n0], in0=bk.f32[:, ta0 - 1 - i0:ta0 - 1 - i0 + n0],
                            scalar=cw(0, jc), in1=a[:, ta0 - o0:ta0 - o0 + n0], op0=ALU.mult, op1=ALU.add),
                            reads=[bk.tok, ta], writes=[ta])
                        e2 = min(o1, N - 1)
                        n2 = e2 - o0
                        S.op("dve", lambda h, a=a, bk=bk, jc=jc, n2=n2, o0=o0, i0=i0: h.scalar_tensor_tensor(
                            out=a[:, 0:n2], in0=bk.f32[:, o0 + 1 - i0:o0 + 1 - i0 + n2],
                            scalar=cw(2, jc), in1=a[:, 0:n2], op0=ALU.mult, op1=ALU.add),
                            reads=[bk.tok, ta], writes=[ta])
                    S.op("act", lambda h, ab=ab, sgb=sgb, nout=nout: h.activation(out=sgb[:, 0:nout], in_=ab[:, 1, 0:nout],
                                                                                func=AF.Silu),
                         reads=[tas[1]], writes=[tsg])
                    gt = list({S.tok("gT", jj, o0 // 512), S.tok("gT", jj, (o1 - 1) // 512)})
                    S.op("pool", lambda h, ab=ab, sgb=sgb, nout=nout, jj=jj, o0=o0, o1=o1: h.tensor_tensor(
                        out=self.gT[:, jj, o0:o1], in0=ab[:, 0, 0:nout], in1=sgb[:, 0:nout], op=ALU.mult),
                        reads=[tas[0], tsg], writes=gt)
            for mp in range(4):
                w, tw = self.wnext(l, ("Dn", qi, mp))
                wv = w[:, 0:2 * nj * 128].rearrange("p (m a b) -> p m a b", m=2, a=nj)
                for mm in range(2):
                    m = 2 * mp + mm
                    for tb in range(4):
                        blk = slice(tb * 512, (tb + 1) * 512)
                        bk = self.bank("ALL7")
                        fns = [self.MM(bk, bk.f32, wv[:, mm, jj, :], self.gT[:, jj, blk]) for jj in range(nj)]
                        S.op("pe", seq(*fns), reads=[tw] + [S.tok("gT", jj, tb) for jj in range(nj)], writes=[bk.tok])
                        S.op("dve", lambda h, bk=bk, m=m, blk=blk: h.scalar_tensor_tensor(
                            out=self.xT[:, m, blk], in0=bk.f32, scalar=gate[:, m:m + 1], in1=self.xT[:, m, blk],
                            op0=ALU.mult, op1=ALU.add), reads=[bk.tok], writes=[S.tok("xT", tb)])

    def build(self):
        nc = self.nc
        self.setup_mem()
        self.banks_init()
        self.epsD = self.small[:, 250:251]
        self.eps64 = self.small[:, 251:252]
        self.eps32 = self.small[:, 252:253]
        nc = self.nc
        handles = {"pe": nc.tensor, "act": nc.scalar, "dve": nc.vector, "pool": nc.gpsimd, "sp": nc.sync}
        battr = {"pe": "tensor", "act": "scalar", "dve": "vector", "pool": "gpsimd", "sp": "sync"}

        def replay(eng, h, semh):
            for it in eng.items:
                if it[0] == "w":
                    h.wait_ge(semh[it[1]], it[2])
                elif it[0] == "i":
                    it[1](h).then_inc(semh[it[2]], 1)
                else:
                    it[1](h, semh[it[2]])

        S1 = self.S
        S1.op("pool", lambda h: h.memset(self.small[:, 250:251], EPS * D), writes=[S1.tok("pro")])
        S1.op("pool", lambda h: h.memset(self.small[:, 251:252], EPS * 64), writes=[S1.tok("pro")])
        S1.op("pool", lambda h: h.memset(self.small[:, 252:253], EPS * 32), writes=[S1.tok("pro")])
        self.prologue()
        with ExitStack() as es:
            semh = {}
            for i, k in enumerate(S1.all_keys()):
                semh[k] = es.enter_context(nc.semaphore("p%d" % i))
            block = es.enter_context(nc.Block())
            for en in Sched.ENG:
                getattr(block, battr[en])(lambda h, en=en: replay(S1.E[en], h, semh))
        S2 = Sched()
        self.S = S2
        self.banks_init()
        S2.dry = True
        self._banks_rr = {}
        self.body()
        S2.dry = False
        self._banks_rr = {}
        self.wi = 0
        self.w_issued = 0
        self.body()
        S2._waits("sp", {k: v for k, v in S2.dma_cnt.items() if v})
        self.n_items = {en: len(S2.E[en].items) for en in Sched.ENG}
        with ExitStack() as es:
            semh = {}
            keys = S2.all_keys()
            for i, k in enumerate(keys):
                semh[k] = es.enter_context(nc.semaphore("b%d" % i))
            with nc.Fori(0, self.nseq) as li:
                self.loop_i = li
                for en in Sched.ENG:
                    replay(S2.E[en], handles[en], semh)
                nc.all_engine_barrier()
                for k in keys:
                    nc.sync.sem_clear(semh[k])
                nc.all_engine_barrier()
        return nc


def host_inputs(inp, seqs_x, seqs_c):
    f = np.float32
    w_ada = np.asarray(inp["w_ada"], f)
    wada = w_ada.reshape(DEPTH, 8, 128, 48, 128).transpose(0, 3, 2, 1, 4).reshape(DEPTH * 48, 128, 1024)
    wada = np.ascontiguousarray(wada)
    wl = np.empty((DEPTH, 128, WL_COLS), f)
    for l in range(DEPTH):
        ch = layer_chunks(np.asarray(inp["w_in"][l], f), np.asarray(inp["w_out"][l], f),
                          np.asarray(inp["w_up"][l], f), np.asarray(inp["w_down"][l], f))
        assert [k for k, _ in ch] == [k for k, _ in PLAN]
        o = 0
        for (k, a), (_, n) in zip(ch, PLAN):
            assert a.shape == (128, n), (k, a.shape, n)
            wl[l, :, o:o + n] = a
            o += n
    vecs = np.zeros((128, NVEC), f)

    def put(name, arr):
        o, n = VC[name]
        assert arr.shape == (128, n), (name, arr.shape, n)
        vecs[:, o:o + n] = arr
    b_ada = np.asarray(inp["b_ada"], f)
    put("b_ada", b_ada.reshape(DEPTH, 48, 128).transpose(2, 0, 1).reshape(128, DEPTH * 48))
    gn = np.stack([np.asarray(inp["norm_attn_g"], f), np.asarray(inp["norm_ffn_g"], f)], 1)
    put("gnorm", gn.reshape(DEPTH, 2, 8, 128).transpose(3, 0, 1, 2).reshape(128, DEPTH * 16))
    p = np.arange(128)
    qk = np.zeros((128, DEPTH, 4, 2), f)
    for m, nm in enumerate(["qkn_a", "qkn_b", "qkn_c", "qkn_d"]):
        g = np.asarray(inp[nm], f)
        d = g.shape[2]
        qk[:, :, m, :] = g[:, :, p % d].transpose(2, 0, 1)
    put("qkg", qk.reshape(128, DEPTH * 8))
    cw = np.concatenate([np.asarray(inp["conv_w"], f), np.asarray(inp["conv_b"], f)[:, None, :]], 1)
    put("conv", cw.reshape(DEPTH, 4, 44, 128).transpose(3, 0, 1, 2).reshape(128, DEPTH * 4 * 44))
    t5 = np.asarray(inp["t5_table"], f)
    put("t5c", np.broadcast_to(np.concatenate([t5[15, 10:14], t5[31, 10:14]])[None, :], (128, 8)))
    put("sink", np.broadcast_to(np.asarray(inp["sink_b"], f).reshape(1, 16), (128, 16)))
    put("linit", np.broadcast_to(np.array([lambda_init(l) for l in range(DEPTH)], f)[None, :], (128, DEPTH)))
    put("lam", np.broadcast_to(np.asarray(inp["lam_d"], f).reshape(1, -1), (128, DEPTH * 128)))
    put("subln", np.broadcast_to(np.asarray(inp["subln_d"], f).reshape(1, -1), (128, DEPTH * 64)))
    cst = np.zeros((128, 7, 128), f)
    cst[:, 0] = np.eye(128)
    cst[:, 1] = np.eye(128)
    cst[:, 2] = np.eye(128)[::-1]
    cst[:, 3] = 1.0
    cst[:, 4] = np.kron(np.eye(2), np.ones((64, 64)))
    cst[:, 5] = np.kron(np.eye(4), np.ones((32, 32)))
    pc = np.arange(64)
    kc_ = 63 - pc[:, None]
    c_ = pc[None, :]
    cs = np.clip(c_ - 8, 0, 48)
    okm = (kc_ >= cs) & (kc_ < cs + 16)
    for hb in (0, 64):
        cst[hb:hb + 64, 6, 0:64] = np.eye(64)[::-1]
        cst[hb:hb + 64, 6, 64:128] = np.where(okm, 0.0, MASKV)
    t5aug = np.concatenate([t5, np.full((1, 14), MASKV, f)], 0)
    oh = build_onehot()
    rpb = np.asarray(inp["rpb_a"], f)
    rp = np.zeros((DEPTH, 4, 15, 127), f)
    rp[..., 48:79] = rpb[..., ::-1]
    rp = rp.reshape(16, 15 * 127)
    maps = []
    for xs, cs_ in zip(seqs_x, seqs_c):
        ns = xs.shape[0]
        cT = np.ascontiguousarray(cs_.reshape(ns, 8, 128).transpose(2, 1, 0).reshape(128, 8 * ns))
        maps.append({"x": np.ascontiguousarray(xs), "cT": cT, "wada": wada, "wl": wl, "vecs": vecs,
                     "cst": cst.reshape(128, 7 * 128), "t5aug": t5aug, "oh": oh, "rpbrp": rp})
    return maps


_CACHE = {}


def get_program(nseq, layers=(0, 1, 2, 3), mixers=("A", "B", "C", "D"), ffn=True):
    key = (nseq, tuple(layers), tuple(mixers), ffn)
    if key not in _CACHE:
        b = Builder(nseq, list(layers), mixers, ffn)
        _CACHE[key] = b.build()
    return _CACHE[key]


def kernel(**inp):
    xp = np.asarray(inp["x_prompt"], np.float32)
    xs = np.asarray(inp["x_sample"], np.float32)
    cp = np.asarray(inp["c_prompt"], np.float32)
    cs = np.asarray(inp["c_sample"], np.float32)
    seqs_x, seqs_c = [], []
    for c in range(8):
        seqs_x.append(np.concatenate([xp[4 * c:4 * c + 4], xs[2 * c:2 * c + 2]], 0))
        seqs_c.append(np.concatenate([cp[4 * c:4 * c + 4], cs[2 * c:2 * c + 2]], 0))
    maps = host_inputs(inp, seqs_x, seqs_c)
    nc = get_program(6)
    res = run_bass_kernel_spmd(nc, maps, core_ids=list(range(8)))
    yp = np.empty_like(xp)
    ys = np.empty_like(xs)
    for c in range(8):
        y = res.results[c]["y"]
        yp[4 * c:4 * c + 4] = y[0:4]
        ys[2 * c:2 * c + 2] = y[4:6]
    return (yp, ys)
```

```python
import math
from contextlib import ExitStack
import numpy as np
import concourse.bass as bass
import concourse.mybir as mybir
from concourse.bass_utils import run_bass_kernel_spmd

F32 = mybir.dt.float32
BF16 = mybir.dt.bfloat16
AF = mybir.ActivationFunctionType
ALU = mybir.AluOpType
AX = mybir.AxisListType
DBG = {}

D = 1024
N = 2048
DEPTH = 4
DFF = 2816
NJ = 22
EPS = 1e-6
MASKV = -30000.0
NPT = 6
PDEPTH = 3
QUARTERS = [(0, 6), (6, 6), (12, 5), (17, 5)]
FFN_BLOCKS = []
for _b in range(5):
    _o0 = 510 * _b
    _o1 = min(_o0 + 510, N)
    FFN_BLOCKS.append((_o0, _o1, max(_o0 - 1, 0), min(_o1 + 1, N)))

A0, B0, C0, D0 = 0, 768, 1280, 2432


def t5_bucket_np(rel):
    rel = np.asarray(rel, dtype=np.int32)
    nb = 16
    max_exact = 8
    base = np.where(rel > 0, nb, 0)
    n = np.abs(rel)
    nf = np.maximum(n, 1).astype(np.float32)
    large = max_exact + (np.log(nf / np.float32(max_exact)) / np.float32(math.log(128 / max_exact))
                         * np.float32(nb - max_exact)).astype(np.int32)
    large = np.minimum(large, nb - 1)
    return base + np.where(n < max_exact, n, large)


STRIPS = [
    ("B", 1, lambda r: np.abs(r) <= 128, 0, 4),
    ("C0", 1, lambda r: np.abs(r) <= 64, 4, 2),
    ("C1", 2, lambda r: (np.abs(r) <= 256) & (r % 4 == 0), 6, 2),
    ("C2", 8, lambda r: (np.abs(r) <= 1024) & (r % 16 == 0), 8, 2),
    ("D", 4, lambda r: np.ones_like(r, dtype=bool), 10, 4),
]
STRIP_W = {}
STRIP_OFF = {}
_off = 0
for _n, _band, _v, _hb, _nh in STRIPS:
    STRIP_W[_n] = 128 * (2 * _band + 1)
    STRIP_OFF[_n] = _off
    _off += STRIP_W[_n] + 128
F_LEN = _off
STRIP_INFO = {n: (band, hb, nh) for n, band, v, hb, nh in STRIPS}


def build_onehot():
    oh = np.zeros((33, F_LEN), np.float32)
    for name, band, valid, hb, nh in STRIPS:
        L = STRIP_W[name] + 128
        y = np.arange(L)
        rel = 127 + 128 * band - y
        ok = valid(rel)
        bk = t5_bucket_np(rel)
        idx = np.where(ok, bk, 32)
        oh[idx, STRIP_OFF[name] + y] = 1.0
    return oh


def _kt(w, cols):
    sub = w[:, cols]
    n = sub.shape[1]
    return sub.reshape(8, 128, n).transpose(1, 0, 2).reshape(128, 8 * n)


def layer_chunks(w_in, w_out, w_up, w_down):
    ch = []
    r = np.arange
    for p in range(2):
        ch.append((("A", p, "q"), _kt(w_in, A0 + 0 + p * 128 + r(128))))
        ch.append((("A", p, "k"), _kt(w_in, A0 + 256 + p * 128 + r(128))))
        ch.append((("A", p, "v"), _kt(w_in, A0 + 512 + p * 128 + r(128))))
        ch.append((("A", p, "o"), w_out[p * 128:(p + 1) * 128, :].reshape(128, 1024)))
    ch.append((("B", "q"), _kt(w_in, B0 + r(256))))
    kc = np.concatenate([B0 + 256 + r(64), B0 + 256 + r(64), B0 + 320 + r(64), B0 + 320 + r(64)])
    ch.append((("B", "k"), _kt(w_in, kc)))
    ch.append((("B", "v"), _kt(w_in, B0 + 384 + r(128))))
    ch.append((("B", "o"), w_out[256:512, :].reshape(2, 128, 1024).transpose(1, 0, 2).reshape(128, 2048)))
    ch.append((("C", "q0"), _kt(w_in, C0 + r(256))))
    ch.append((("C", "q1"), _kt(w_in, C0 + 256 + r(128))))
    ch.append((("C", "k0"), _kt(w_in, C0 + 384 + r(256))))
    ch.append((("C", "k1"), _kt(w_in, C0 + 640 + r(128))))
    ch.append((("C", "v0"), _kt(w_in, C0 + 768 + r(192))))
    ch.append((("C", "v1"), _kt(w_in, C0 + 960 + r(192))))
    ch.append((("C", "o0"), w_out[512:768, :].reshape(2, 128, 1024).transpose(1, 0, 2).reshape(128, 2048)))
    ch.append((("C", "o1"), w_out[768:896, :].reshape(128, 1024)))
    ch.append((("D", "q0"), _kt(w_in, D0 + r(128))))
    ch.append((("D", "k0"), _kt(w_in, D0 + 256 + r(128))))
    ch.append((("D", "v"), _kt(w_in, D0 + 512 + r(256))))
    ch.append((("D", "q1"), _kt(w_in, D0 + 128 + r(128))))
    ch.append((("D", "k1"), _kt(w_in, D0 + 384 + r(128))))
    ch.append((("D", "o"), w_out[896:1152, :].reshape(2, 128, 1024).transpose(1, 0, 2).reshape(128, 2048)))
    for qi, (j0, nj) in enumerate(QUARTERS):
        for jj in range(nj):
            j = j0 + jj
            cols = np.concatenate([j * 128 + r(128), DFF + j * 128 + r(128)])
            ch.append((("U", j), _kt(w_up, cols)))
        wd = w_down[j0 * 128:(j0 + nj) * 128, :].reshape(nj, 128, 8, 128)
        for mp in range(4):
            blk = wd[:, :, 2 * mp:2 * mp + 2, :].transpose(1, 2, 0, 3).reshape(128, 2 * nj * 128)
            ch.append((("Dn", qi, mp), blk))
    return ch


def chunk_plan():
    z_in = np.zeros((1024, 1), np.float32)

    class _Z:
        def __init__(self, shape):
            self.shape = shape

    plan = []
    for p in range(2):
        plan += [(("A", p, "q"), 1024), (("A", p, "k"), 1024), (("A", p, "v"), 1024), (("A", p, "o"), 1024)]
    plan += [(("B", "q"), 2048), (("B", "k"), 2048), (("B", "v"), 1024), (("B", "o"), 2048)]
    plan += [(("C", "q0"), 2048), (("C", "q1"), 1024), (("C", "k0"), 2048), (("C", "k1"), 1024),
             (("C", "v0"), 1536), (("C", "v1"), 1536), (("C", "o0"), 2048), (("C", "o1"), 1024)]
    plan += [(("D", "q0"), 1024), (("D", "k0"), 1024), (("D", "v"), 2048), (("D", "q1"), 1024), (("D", "k1"), 1024),
             (("D", "o"), 2048)]
    for qi, (j0, nj) in enumerate(QUARTERS):
        for jj in range(nj):
            plan.append((("U", j0 + jj), 2048))
        for mp in range(4):
            plan.append((("Dn", qi, mp), 2 * nj * 128))
    return plan


PLAN = chunk_plan()
PLAN_OFF = {}
_o = 0
for _k, _n in PLAN:
    PLAN_OFF[_k] = (_o, _n)
    _o += _n
WL_COLS = _o

VC = {}
_c = 0


def _vc(name, n):
    global _c
    VC[name] = (_c, n)
    _c += n


_vc("b_ada", DEPTH * 48)
_vc("gnorm", DEPTH * 2 * 8)
_vc("qkg", DEPTH * 4 * 2)
_vc("conv", DEPTH * 4 * 44)
_vc("t5c", 8)
_vc("sink", DEPTH * 4)
_vc("linit", DEPTH)
_vc("lam", DEPTH * 4 * 32)
_vc("subln", DEPTH * 64)
NVEC = _c


def lambda_init(l):
    return 0.8 - 0.6 * math.exp(-0.3 * l)


class Tok:
    __slots__ = ("w", "r")

    def __init__(self):
        self.w = None
        self.r = {}


EPOCH = 16000


class _Eng:
    def __init__(self, name):
        self.name = name
        self.items = []
        self.count = 0
        self.epoch = 0
        self.finals = {}
        self.seen = {}

    def key(self):
        return ("e", self.name, self.epoch)

    def bump(self):
        if self.count >= EPOCH:
            self.finals[self.key()] = self.count
            self.epoch += 1
            self.count = 0
        self.count += 1
        return (self.key(), self.count)


class Sched:
    ENG = ("pe", "act", "dve", "pool", "sp")

    def __init__(self):
        self.E = {n: _Eng(n) for n in self.ENG}
        self.toks = {}
        self.dry = False
        self.dma_key = {}
        self.dma_cnt = {}
        self.out_keys = set()

    def tok(self, *key):
        t = self.toks.get(key)
        if t is None:
            t = self.toks[key] = Tok()
        return t

    def _deps(self, reads, writes):
        deps = {}
        for t in reads:
            if t.w is not None and deps.get(t.w[0], 0) < t.w[1]:
                deps[t.w[0]] = t.w[1]
        for t in writes:
            if t.w is not None and deps.get(t.w[0], 0) < t.w[1]:
                deps[t.w[0]] = t.w[1]
            for k, v in t.r.items():
                if deps.get(k, 0) < v:
                    deps[k] = v
        return deps

    def _waits(self, en, deps):
        eng = self.E[en]
        for k, v in deps.items():
            if en == "pe" and k[0] == "e" and k[1] == "pe":
                continue
            if eng.seen.get(k, 0) >= v:
                continue
            eng.seen[k] = v
            eng.items.append(("w", k, v))

    def op(self, en, fn, reads=(), writes=()):
        if self.dry:
            return
        eng = self.E[en]
        self._waits(en, self._deps(reads, writes))
        me = eng.bump()
        key = me[0]
        eng.items.append(("i", fn, key))
        for t in writes:
            t.w = me
            t.r = {}
        for t in reads:
            if t.r.get(key, 0) < me[1]:
                t.r[key] = me[1]

    def dma(self, qn, fn, n, reads=(), writes=(), owner=None, is_out=False):
        if self.dry:
            return
        eng = self.E[qn]
        self._waits(qn, self._deps(reads, writes))
        own = owner if owner is not None else writes[0]
        key = self.dma_key.get(id(own))
        if key is None:
            key = ("d", len(self.dma_key))
            self.dma_key[id(own)] = key
            self.dma_cnt[key] = 0
        self.dma_cnt[key] += 16 * n
        cnt = self.dma_cnt[key]
        if is_out:
            self.out_keys.add(key)
        eng.items.append(("d", fn, key))
        me = (key, cnt)
        for t in writes:
            t.w = me
            t.r = {}
        for t in reads:
            if t.r.get(key, 0) < cnt:
                t.r[key] = cnt

    def fence(self, toks):
        if self.dry:
            return
        deps = self._deps((), toks)
        for t in toks:
            t.w = None
            t.r = dict(deps)

    def barrier(self):
        if self.dry:
            return
        allv = {}
        for n, e in self.E.items():
            for k, v in e.finals.items():
                allv[k] = v
            if e.count:
                allv[e.key()] = e.count
        for k, v in self.dma_cnt.items():
            if v:
                allv[k] = v
        for n in self.ENG:
            self._waits(n, dict(allv))

    def final_wait(self):
        deps = {k: self.dma_cnt[k] for k in self.out_keys}
        self._waits("sp", deps)

    def all_keys(self):
        keys = []
        for n in self.ENG:
            e = self.E[n]
            keys += [("e", n, ep) for ep in range(e.epoch + 1)]
        keys += list(self.dma_cnt.keys())
        return keys


class Pipe:
    def __init__(self, depth=2, pdelay=2):
        self.depth = depth
        self.pdelay = pdelay
        self.q = []
        self.pq = []

    def push(self, s1, s23, post=None):
        s1()
        self.q.append((s23, post))
        while len(self.q) > self.depth:
            self._pop()

    def _pop(self):
        s23, post = self.q.pop(0)
        s23()
        self.pq = [(c - 1, p) for c, p in self.pq]
        while self.pq and self.pq[0][0] <= 0:
            self.pq.pop(0)[1]()
        if post is not None:
            self.pq.append((self.pdelay, post))

    def flush(self):
        while self.q:
            self._pop()
        while self.pq:
            self.pq.pop(0)[1]()


def seq(*fns):
    def f(h):
        r = None
        for g in fns:
            r = g(h)
        return r
    return f


class Builder:
    def __init__(self, nseq, layers, mixers=("A", "B", "C", "D"), ffn=True):
        self.nseq = nseq
        self.layers = layers
        self.mixers = mixers
        self.ffn = ffn
        self.S = Sched()
        self.nc = bass.Bass("TRN2", target_bir_lowering=False)
        self.wq = []
        self.wi = 0
        self.w_issued = 0
        self.RING = 3
        self._banks_rr = {}

    def setup_mem(self):
        nc = self.nc
        ns = self.nseq
        dt = nc.dram_tensor
        self.x_d = dt("x", [ns, N, D], F32, kind="ExternalInput").ap()
        self.y_d = dt("y", [ns, N, D], F32, kind="ExternalOutput").ap()
        self.cT_d = dt("cT", [128, 8 * ns], F32, kind="ExternalInput").ap()
        self.wada_d = dt("wada", [DEPTH * 48, 128, 1024], F32, kind="ExternalInput").ap()
        self.wl_d = dt("wl", [DEPTH, 128, WL_COLS], F32, kind="ExternalInput").ap()
        self.vecs_d = dt("vecs", [128, NVEC], F32, kind="ExternalInput").ap()
        self.cst_d = dt("cst", [128, 128 * 7], F32, kind="ExternalInput").ap()
        self.t5aug_d = dt("t5aug", [33, 14], F32, kind="ExternalInput").ap()
        self.oh_d = dt("oh", [33, F_LEN], F32, kind="ExternalInput").ap()
        self.rpb_d = dt("rpbrp", [16, 15 * 127], F32, kind="ExternalInput").ap()
        self.Fd = dt("Fd", [14, F_LEN], BF16, kind="Internal").ap()
        self.RAd = dt("RAd", [DEPTH, 128, 2 * 960], BF16, kind="Internal").ap()
        self.modD = dt("modD", [ns, 128, DEPTH * 48], F32, kind="Internal").ap()
        self.wlb = dt("wlb", [DEPTH, 128, WL_COLS], BF16, kind="Internal").ap()

        total_bytes = 207 * 1024
        self.arena = nc.alloc_sbuf_tensor("arena", [128, total_bytes // 2], BF16)
        self._aoff = 0

        def carve(nbytes, dtype, shape):
            assert nbytes % 4 == 0
            o = self._aoff
            self._aoff += nbytes
            assert self._aoff <= total_bytes, ("SBUF overflow", self._aoff)
            ap = self.arena[:, o // 2:(o + nbytes) // 2]
            if dtype == F32:
                ap = ap.bitcast(F32)
            return self._shape(ap, shape)

        self.carve = carve
        K = 1024
        self.xT = carve(64 * K, F32, [8, N])
        self.hT = carve(32 * K, BF16, [8, N])
        self.identF = carve(512, F32, [128])
        self.cbf = carve(6 * 256, BF16, [6, 128])
        self.vecs = carve(NVEC * 4, F32, [NVEC])
        self.modc = carve(DEPTH * 48 * 4, F32, [DEPTH * 48])
        self.small = carve(256 * 4, F32, [256])
        self.ring = [carve(4 * K, BF16, [2048]) for _ in range(self.RING)]
        self.sq = [carve(1 * K, BF16, [512]) for _ in range(2)]
        self.lnt = carve(2 * K, F32, [512])
        self.rstd = [carve(2 * K, F32, [512]) for _ in range(2)]
        self.tmpf = [carve(2 * K, F32, [512]) for _ in range(2)]
        self.ov0 = self._aoff
        self.mixedT = carve(12 * K, BF16, [3, N])
        self.QT = carve(12 * K, BF16, [3, N])
        self.KT = carve(12 * K, BF16, [3, N])
        self.Vr = carve(12480, BF16, [6240])
        self.PT = [carve(1 * K, BF16, [512]) for _ in range(NPT)]
        self.ostage = [carve(2 * K, BF16, [1024]) for _ in range(2)]
        self.strip = carve(12800, BF16, [6400])
        self.att_end = self._aoff
        self._aoff = self.ov0
        self.gT = carve(24 * K, BF16, [6, N])
        self.acc = [carve(4 * K, F32, [2, 512]) for _ in range(4)]
        self.sg = [carve(2 * K, F32, [512]) for _ in range(4)]
        self.xstage = [carve(4 * K, F32, [1024]) for _ in range(4)]
        self._aoff = self.ov0
        self.oh_sb = carve(F_LEN * 4, F32, [F_LEN])
        self.f_sb = carve(F_LEN * 2, BF16, [F_LEN])
        self.t5aug_sb = carve(64, F32, [14])
        self.cst_sb = carve(128 * 7 * 4, F32, [7, 128])
        self.ra_f = carve(15 * 64 * 4, F32, [15, 64])
        self.ra_b = [carve(960 * 2, BF16, [960]) for _ in range(2)]
        self.wada_sb = [carve(4 * K, F32, [8, 128]) for _ in range(2)]
        self.modT = carve(DEPTH * 48 * ns * 4, F32, [ns, DEPTH * 48])
        self.sc = carve(8 * ns * 4, F32, [8, ns])
        self._aoff = max(self.att_end, self._aoff)
        self.sbuf_used = self._aoff
        self.psum = nc.alloc_psum_tensor("ps", [128, 8, 512], F32)

    @staticmethod
    def _shape(ap, shape):
        if len(shape) == 1:
            return ap
        names = "abcdef"[:len(shape)]
        s = "p (" + " ".join(names) + ") -> p " + " ".join(names)
        kw = {names[i]: shape[i] for i in range(len(shape) - 1)}
        return ap.rearrange(s, **kw)

    class Bank:
        def __init__(self, b, idx):
            self.idx = idx
            self.f32 = b.psum[:, idx, :]
            self.bf = b.psum[:, idx, :].bitcast(BF16)
            self.tok = b.S.tok("ps", idx)
            self.fresh = True

        def first(self):
            f = self.fresh
            self.fresh = False
            return f

    def banks_init(self):
        self.banks = [Builder.Bank(self, i) for i in range(8)]
        self.pools = {"S": [0, 1, 2, 3], "ACC": [4, 5, 6], "AUX": [7], "ALL7": [0, 1, 2, 3, 4, 5, 6],
                      "ALL8": [0, 1, 2, 3, 4, 5, 6, 7]}

    def bank(self, pool):
        lst = self.pools[pool]
        i = self._banks_rr.get(pool, 0)
        self._banks_rr[pool] = i + 1
        b = self.banks[lst[i % len(lst)]]
        b.fresh = True
        return b

    def MM(self, bank, out, lhsT, rhs, **kw):
        st = bank.first()
        return lambda h: h.matmul(out, lhsT, rhs, start=st, stop=True, skip_group_check=True, **kw)

    def wnext(self, l, key):
        if self.S.dry:
            self.wq.append((l, key))
            return self.ring[0], self.S.tok("ring", 0)
        i = self.wi
        assert self.wq[i] == (l, key), (self.wq[i], l, key)
        self.wi += 1
        while self.w_issued < min(len(self.wq), i + self.RING):
            self._wissue(self.w_issued)
            self.w_issued += 1
        s = i % self.RING
        return self.ring[s], self.S.tok("ring", s)

    def _wissue(self, i):
        l, key = self.wq[i]
        s = i % self.RING
        off, n = PLAN_OFF[key]
        dst = self.ring[s][:, 0:n]
        src = self.wlb[l, :, off:off + n]
        t = self.S.tok("ring", s)
        self.S.dma("sp", lambda h, sem, dst=dst, src=src: h.dma_start(out=dst, in_=src).then_inc(sem, 16),
                   1, writes=[t])

    def body(self):
        S = self.S
        S.dma("sp", lambda h, sem: h.dma_start(out=self.modc, in_=self.modD[self.loop_i]).then_inc(sem, 16), 1,
              writes=[S.tok("modc")])
        for en in ("dve", "act", "pool", "pe"):
            S.op(en, lambda h: h.nop(), reads=[S.tok("modc")])
        self.load_x(None)
        for l in self.layers:
            self.layer(None, l)
        self.store_x(None)

    def cb(self, i):
        return self.cbf[:, i, :]

    def vec(self, name, i=0, n=1):
        o, _ = VC[name]
        return self.vecs[:, o + i:o + i + n]

    def prologue(self):
        S = self.S
        ns = self.nseq
        tP = S.tok("pro")
        ld = lambda dst, src: (lambda h, sem: h.dma_start(out=dst, in_=src).then_inc(sem, 16))
        S.dma("sp", ld(self.vecs, self.vecs_d), 1, writes=[S.tok("vecs")])
        S.dma("sp", ld(self.cst_sb, self.cst_d.rearrange("p (a b) -> p a b", a=7)), 1, writes=[S.tok("cst")])
        S.dma("sp", ld(self.sc, self.cT_d.rearrange("p (a b) -> p a b", a=8)), 1, writes=[S.tok("sc")])
        S.dma("sp", ld(self.t5aug_sb[0:33, 0:14], self.t5aug_d), 1, writes=[S.tok("t5aug")])
        S.dma("sp", ld(self.oh_sb[0:33, :], self.oh_d), 1, writes=[S.tok("oh")])
        S.op("dve", lambda h: h.tensor_copy(out=self.identF, in_=self.cst_sb[:, 0, :]),
             reads=[S.tok("cst")], writes=[S.tok("c0")])
        S.op("dve", lambda h: h.tensor_copy(out=self.cbf, in_=self.cst_sb[:, 1:7, :]),
             reads=[S.tok("cst")], writes=[S.tok("c1")])
        S.op("act", lambda h: h.activation(out=self.sc, in_=self.sc, func=AF.Silu),
             reads=[S.tok("sc")], writes=[S.tok("sc")])
        for l in self.layers:
            for j in range(48):
                i = l * 48 + j
                wb = self.wada_sb[i % 2]
                tw = S.tok("wada", i % 2)
                S.dma("sp", ld(wb, self.wada_d[i].rearrange("p (a b) -> p a b", a=8)), 1, writes=[tw])
                bk = self.bank("ALL7")
                fns = [self.MM(bk, bk.f32[:, 0:ns], wb[:, kc, :], self.sc[:, kc, :]) for kc in range(8)]
                S.op("pe", seq(*fns), reads=[tw, S.tok("sc")], writes=[bk.tok])
                S.op("dve", lambda h, bk=bk, i=i: h.tensor_scalar(
                    out=self.modT[:, :, i], in0=bk.f32[:, 0:ns], scalar1=self.vec("b_ada", i), scalar2=None,
                    op0=ALU.add), reads=[bk.tok, S.tok("vecs")], writes=[S.tok("modT")])
        for c0 in range(0, F_LEN, 512):
            cn = min(512, F_LEN - c0)
            bk = self.bank("ALL7")
            S.op("pe", self.MM(bk, bk.f32[0:14, 0:cn], self.t5aug_sb[0:33, 0:14], self.oh_sb[0:33, c0:c0 + cn]),
                 reads=[S.tok("t5aug"), S.tok("oh")], writes=[bk.tok])
            S.op("dve", lambda h, bk=bk, c0=c0, cn=cn: h.tensor_copy(out=self.f_sb[0:14, c0:c0 + cn], in_=bk.f32[0:14, 0:cn]),
                 reads=[bk.tok], writes=[S.tok("fsb")])
        S.dma("sp", ld(self.Fd, self.f_sb[0:14, :]), 1, reads=[S.tok("fsb")], writes=[S.tok("Fd")])
        if "A" in self.mixers and not DBG.get("A_nostrip"):
            maskA = self.cst_sb[0:64, 0, :]
            for l in self.layers:
                for hh in range(4):
                    i = l * 4 + hh
                    pb = 64 * (hh % 2)

                    def ldra(h, sem, i=i, pb=pb):
                        for dr in range(15):
                            src = bass.AP(self.rpb_d.tensor, (i * 15 + dr) * 127, [[1, 64], [1, 64]])
                            h.dma_start(out=self.ra_f[pb:pb + 64, dr, :], in_=src).then_inc(sem, 16)
                    S.dma("sp", ldra, 15, writes=[S.tok("raf")])
                    rb = self.ra_b[i % 2]
                    S.op("dve", lambda h, rb=rb, pb=pb: h.tensor_tensor(
                        out=rb[pb:pb + 64, :].rearrange("p (a b) -> p a b", a=15), in0=self.ra_f[pb:pb + 64, :, :],
                        in1=self.cst_sb[pb:pb + 64, 6, 64:128].unsqueeze(1).broadcast_to([64, 15, 64]), op=ALU.add),
                        reads=[S.tok("raf"), S.tok("cst")], writes=[S.tok("rab", i % 2)])
                    S.dma("sp", ld(self.RAd[l, pb:pb + 64, (hh // 2) * 960:(hh // 2 + 1) * 960], rb[pb:pb + 64, :]), 1,
                          reads=[S.tok("rab", i % 2)], writes=[S.tok("RAd", i % 2)])
        sm = self.small
        o, _ = VC["qkg"]
        for l in range(DEPTH):
            for m in range(4):
                dd = 32.0 if m == 3 else 64.0
                c = (l * 4 + m) * 2
                S.op("dve", lambda h, c=c: h.tensor_copy(out=sm[:, c:c + 1], in_=self.vecs[:, o + c:o + c + 1]),
                     reads=[S.tok("vecs")], writes=[tP])
                S.op("dve", lambda h, c=c, dd=dd: h.tensor_scalar(
                    out=sm[:, c + 1:c + 2], in0=self.vecs[:, o + c + 1:o + c + 2], scalar1=math.sqrt(dd),
                    scalar2=None, op0=ALU.mult), reads=[S.tok("vecs")], writes=[tP])
        for l in range(DEPTH):
            for b in range(4):
                kc_ = o + (l * 4 + 3) * 2 + 1
                S.op("dve", lambda h, l=l, b=b, kc_=kc_: h.tensor_scalar(
                    out=sm[:, 160 + l * 4 + b:161 + l * 4 + b], in0=self.cst_sb[:, 5, 32 * b:32 * b + 1],
                    scalar1=self.vecs[:, kc_:kc_ + 1], scalar2=math.sqrt(32.0), op0=ALU.mult, op1=ALU.mult),
                    reads=[S.tok("vecs"), S.tok("cst")], writes=[tP])
        S.op("act", lambda h: h.activation(out=sm[:, 32:48], in_=self.vec("sink", 0, 16), func=AF.Exp),
             reads=[S.tok("vecs")], writes=[tP])
        lamv = self.vec("lam", 0, DEPTH * 128).rearrange("p (l f e) -> p l f e", l=DEPTH, f=4)
        S.op("dve", lambda h: h.tensor_tensor(out=self.tmpf[0][:, 0:DEPTH * 32].rearrange("p (l e) -> p l e", l=DEPTH),
                                              in0=lamv[:, :, 0, :], in1=lamv[:, :, 1, :], op=ALU.mult),
             reads=[S.tok("vecs")], writes=[S.tok("tmpf", 0)])
        S.op("dve", lambda h: h.tensor_tensor(out=self.tmpf[1][:, 0:DEPTH * 32].rearrange("p (l e) -> p l e", l=DEPTH),
                                              in0=lamv[:, :, 2, :], in1=lamv[:, :, 3, :], op=ALU.mult),
             reads=[S.tok("vecs")], writes=[S.tok("tmpf", 1)])
        S.op("dve", lambda h: h.tensor_reduce(out=sm[:, 52:56],
                                              in_=self.tmpf[0][:, 0:DEPTH * 32].rearrange("p (l e) -> p l e", l=DEPTH),
                                              axis=AX.X, op=ALU.add), reads=[S.tok("tmpf", 0)], writes=[tP])
        S.op("dve", lambda h: h.tensor_reduce(out=sm[:, 56:60],
                                              in_=self.tmpf[1][:, 0:DEPTH * 32].rearrange("p (l e) -> p l e", l=DEPTH),
                                              axis=AX.X, op=ALU.add), reads=[S.tok("tmpf", 1)], writes=[tP])
        S.op("act", lambda h: h.activation(out=sm[:, 52:60], in_=sm[:, 52:60], func=AF.Exp), reads=[tP], writes=[tP])
        S.op("dve", lambda h: h.tensor_tensor(out=sm[:, 48:52], in0=sm[:, 56:60], in1=sm[:, 52:56], op=ALU.subtract),
             reads=[tP], writes=[tP])
        S.op("dve", lambda h: h.tensor_tensor(out=sm[:, 48:52], in0=sm[:, 48:52], in1=self.vec("linit", 0, DEPTH),
                                              op=ALU.subtract), reads=[tP, S.tok("vecs")], writes=[tP])
        S.op("dve", lambda h: h.tensor_scalar(out=sm[:, 64:128], in0=self.vec("gnorm", 0, 64), scalar1=32.0,
                                              scalar2=None, op0=ALU.mult), reads=[S.tok("vecs")], writes=[tP])
        for l in self.layers:
            pieces = []
            for c0 in range(0, WL_COLS, 8192):
                cn = min(8192, WL_COLS - c0)
                pieces.append((self.wlb[l, :, c0:c0 + cn], self.wl_d[l, :, c0:c0 + cn]))

            def cast(h, sem, pieces=pieces):
                for dst, src in pieces:
                    h.dma_start(out=dst, in_=src).then_inc(sem, 16)
            S.dma("pool", cast, len(pieces), writes=[S.tok("wlb", l)])
        for s_ in range(ns):
            S.dma("sp", ld(self.modD[s_], self.modT[:, s_, :]), 1, reads=[S.tok("modT")], writes=[S.tok("modD", s_)])
        S.barrier()

    def qkg(self, l, m, which):
        c = (l * 4 + m) * 2 + which
        return self.small[:, c:c + 1]

    def mod(self, l, i, s):
        return self.modc[:, l * 48 + i * 8:l * 48 + i * 8 + 8]

    def ov_fence(self):
        S = self.S
        if S.dry:
            return
        toks = [t for k, t in S.toks.items() if k[0] in ("QT", "KT", "V", "mixedT", "PT", "ost", "strip", "gT", "acc",
                                                          "sg", "xst")]
        S.fence(toks)

    def load_x(self, s):
        S = self.S
        self.ov_fence()
        for t in range(16):
            st = self.xstage[t % 4]
            ts = S.tok("xst", t % 4)
            S.dma("sp", lambda h, sem, st=st, t=t: h.dma_start(
                out=st, in_=self.x_d[self.loop_i, t * 128:(t + 1) * 128, :]).then_inc(sem, 16), 1, writes=[ts])
            for half in range(2):
                bk = self.bank("ALL7")
                fns = [(lambda h, bk=bk, st=st, c=c, half=half: h.transpose(
                    bk.f32[:, c * 128:(c + 1) * 128], st[:, (half * 4 + c) * 128:(half * 4 + c + 1) * 128],
                    self.identF)) for c in range(4)]
                S.op("pe", seq(*fns), reads=[ts], writes=[bk.tok])
                eng = "dve" if half == 0 else "act"
                dst = self.xT[:, half * 4:half * 4 + 4, t * 128:(t + 1) * 128]
                srcp = bk.f32.rearrange("p (a b) -> p a b", a=4)
                if eng == "dve":
                    S.op("dve", lambda h, dst=dst, srcp=srcp: h.tensor_copy(out=dst, in_=srcp),
                         reads=[bk.tok], writes=[S.tok("xT", t // 4)])
                else:
                    S.op("act", lambda h, dst=dst, srcp=srcp: h.activation(out=dst, in_=srcp, func=AF.Copy),
                         reads=[bk.tok], writes=[S.tok("xT", t // 4)])

    def store_x(self, s):
        S = self.S
        self.ov_fence()
        for t in range(16):
            st = self.xstage[t % 4]
            ts = S.tok("xst", t % 4)
            for half in range(2):
                bk = self.bank("ALL7")
                fns = [(lambda h, bk=bk, c=c, half=half, t=t: h.transpose(
                    bk.f32[:, c * 128:(c + 1) * 128], self.xT[:, half * 4 + c, t * 128:(t + 1) * 128],
                    self.identF)) for c in range(4)]
                S.op("pe", seq(*fns), reads=[S.tok("xT", t // 4)], writes=[bk.tok])
                dst = st[:, half * 512:(half + 1) * 512]
                if half == 0:
                    S.op("dve", lambda h, dst=dst, bk=bk: h.tensor_copy(out=dst, in_=bk.f32), reads=[bk.tok], writes=[ts])
                else:
                    S.op("act", lambda h, dst=dst, bk=bk: h.activation(out=dst, in_=bk.f32, func=AF.Copy),
                         reads=[bk.tok], writes=[ts])
            S.dma("sp", lambda h, sem, st=st, t=t: h.dma_start(
                out=self.y_d[self.loop_i, t * 128:(t + 1) * 128, :], in_=st).then_inc(sem, 16), 1,
                  reads=[ts], writes=[S.tok("ydram", t % 4)], is_out=True)

    def norm(self, l, s, which):
        S = self.S
        sm = self.small
        ga = sm[:, 128 + which * 8:128 + which * 8 + 8]
        tg = S.tok("ga", which)
        gn = sm[:, 64 + (l * 2 + which) * 8:64 + (l * 2 + which) * 8 + 8]
        scale = self.mod(l, 1 + 3 * which, s)
        shift = self.mod(l, 0 + 3 * which, s)
        S.op("dve", lambda h: h.scalar_tensor_tensor(out=ga, in0=scale, scalar=1.0, in1=gn, op0=ALU.add, op1=ALU.mult),
             writes=[tg])
        ones = self.cb(2)
        for tb in range(4):
            blk = slice(tb * 512, (tb + 1) * 512)
            bk = self.bank("AUX")
            for c in range(8):
                sq = self.sq[c % 2]
                tq = S.tok("sq", c % 2)
                S.op("act", lambda h, sq=sq, c=c, blk=blk: h.activation(out=sq, in_=self.xT[:, c, blk], func=AF.Square),
                     reads=[S.tok("xT", tb)], writes=[tq])
                S.op("pe", self.MM(bk, bk.f32, ones, sq), reads=[tq], writes=[bk.tok])
            rs = self.rstd[tb % 2]
            tr = S.tok("rstd", tb % 2)
            S.op("act", lambda h, bk=bk: h.activation(out=self.lnt, in_=bk.f32, func=AF.Ln, bias=self.epsD, scale=1.0),
                 reads=[bk.tok], writes=[S.tok("lnt")])
            S.op("act", lambda h, rs=rs: h.activation(out=rs, in_=self.lnt, func=AF.Exp, scale=-0.5),
                 reads=[S.tok("lnt")], writes=[tr])
            for c in range(8):
                tf = self.tmpf[c % 2]
                tt = S.tok("tmpf", c % 2)
                S.op("dve", lambda h, tf=tf, c=c, blk=blk, rs=rs: h.tensor_tensor(
                    out=tf, in0=self.xT[:, c, blk], in1=rs, op=ALU.mult),
                    reads=[S.tok("xT", tb), tr], writes=[tt])
                if c % 2 == 0:
                    S.op("act", lambda h, tf=tf, c=c, blk=blk: h.activation(
                        out=self.hT[:, c, blk], in_=tf, func=AF.Identity, bias=shift[:, c:c + 1], scale=ga[:, c:c + 1]),
                        reads=[tt, tg], writes=[S.tok("hT", tb)])
                else:
                    S.op("dve", lambda h, tf=tf, c=c, blk=blk: h.tensor_scalar(
                        out=self.hT[:, c, blk], in0=tf, scalar1=ga[:, c:c + 1], scalar2=shift[:, c:c + 1],
                        op0=ALU.mult, op1=ALU.add), reads=[tt, tg], writes=[S.tok("hT", tb)])

    def proj_qk(self, wslot, wtok, wcol0, dst, dtok_name, dchunk, gvec, bd, dsz, multi=None):
        S = self.S
        epsb = self.eps64 if dsz == 64 else self.eps32

        def stage_a(tb):
            blk = slice(tb * 512, (tb + 1) * 512)
            bk = self.bank("S")
            fns = [self.MM(bk, bk.f32, self._w(wslot, kc, wcol0, 128), self.hT[:, kc, blk]) for kc in range(8)]
            S.op("pe", seq(*fns), reads=[wtok, S.tok("hT", tb)], writes=[bk.tok])
            sq = self.sq[tb % 2]
            tq = S.tok("sq", tb % 2)
            S.op("act", lambda h, sq=sq, bk=bk: h.activation(out=sq, in_=bk.f32, func=AF.Square),
                 reads=[bk.tok], writes=[tq])
            return (tb, blk, bk, sq, tq)

        def stage_b(st):
            tb, blk, bk, sq, tq = st
            b2 = self.bank("AUX")
            S.op("pe", self.MM(b2, b2.f32, bd, sq), reads=[tq], writes=[b2.tok])
            rs = self.rstd[tb % 2]
            tr = S.tok("rstd", tb % 2)
            S.op("act", lambda h, b2=b2: h.activation(out=self.lnt, in_=b2.f32, func=AF.Ln, bias=epsb, scale=1.0),
                 reads=[b2.tok], writes=[S.tok("lnt")])
            S.op("act", lambda h, rs=rs: h.activation(out=rs, in_=self.lnt, func=AF.Exp, scale=-0.5),
                 reads=[S.tok("lnt")], writes=[tr])
            if multi is None:
                S.op("dve", lambda h, bk=bk, rs=rs, blk=blk: h.scalar_tensor_tensor(
                    out=dst[:, dchunk, blk], in0=bk.f32, scalar=gvec, in1=rs, op0=ALU.mult, op1=ALU.mult),
                    reads=[bk.tok, tr], writes=[S.tok(dtok_name, dchunk, tb)])
            else:
                for (dv, tn, ti, gcol) in multi:
                    S.op("dve", lambda h, bk=bk, rs=rs, blk=blk, dv=dv, gcol=gcol: h.scalar_tensor_tensor(
                        out=dv[:, blk], in0=bk.f32, scalar=gcol, in1=rs, op0=ALU.mult, op1=ALU.mult),
                        reads=[bk.tok, tr], writes=[S.tok(tn, ti, tb)])

        pend = None
        for tb in range(4):
            st = stage_a(tb)
            if pend is not None:
                stage_b(pend)
            pend = st
        stage_b(pend)

    def _w(self, wslot, kc, col0, ncols, wtot=None):
        return wslot[:, 0:8 * self._wtot].rearrange("p (a b) -> p a b", a=8)[:, kc, col0:col0 + ncols]

    def proj_v(self, wslot, wtok, ncols, nh, V, hoff, rows64=False):
        S = self.S
        if not rows64:
            for t in range(16):
                bk = self.bank("S")
                fns = [self.MM(bk, bk.f32[:, 0:ncols], self.hT[:, kc, t * 128:(t + 1) * 128],
                               self._w(wslot, kc, 0, ncols)) for kc in range(8)]
                S.op("pe", seq(*fns), reads=[wtok, S.tok("hT", t // 4)], writes=[bk.tok])
                S.op("dve", lambda h, bk=bk, t=t: h.tensor_copy(
                    out=V[:, t, hoff:hoff + nh, 0:64], in_=bk.f32[:, 0:ncols].rearrange("p (a b) -> p a b", a=nh)),
                    reads=[bk.tok], writes=[S.tok("V", t // 4)])
        else:
            for r2 in range(16):
                bk = self.bank("S")
                fns = []
                for rr in range(2):
                    r = 2 * r2 + rr
                    fns += [self.MM(bk, bk.f32[0:64, rr * ncols:(rr + 1) * ncols], self.hT[:, kc, r * 64:(r + 1) * 64],
                                    self._w(wslot, kc, 0, ncols)) for kc in range(8)]
                S.op("pe", seq(*fns), reads=[wtok, S.tok("hT", r2 // 4)], writes=[bk.tok])
                S.op("dve", lambda h, bk=bk, r2=r2: h.tensor_copy(
                    out=V[0:64, 2 * r2:2 * r2 + 2, :, 0:64],
                    in_=bk.f32[0:64, 0:2 * ncols].rearrange("p (r a b) -> p r a b", r=2, a=nh)),
                    reads=[bk.tok], writes=[S.tok("V", r2 // 4)])

    def ones_col(self, V, nparts=128):
        S = self.S
        vt = [S.tok("V", i) for i in range(4)]
        S.op("pool", lambda h: h.memset(V[0:nparts, :, :, 64:65], 1.0), writes=vt)

    def wout_pass(self, l, s, slots, nks, kbase0=0):
        S = self.S
        gate = self.mod(l, 2, s)
        for m in range(8):
            for tb in range(4):
                blk = slice(tb * 512, (tb + 1) * 512)
                bk = self.bank("ALL7")
                fns = []
                reads = []
                kbase = kbase0
                for (slot, tk, nk) in slots:
                    wv = slot[:, 0:nk * 1024].rearrange("p (a b) -> p a b", a=nk)
                    for kc in range(nk):
                        fns.append(self.MM(bk, bk.f32, wv[:, kc, m * 128:(m + 1) * 128], self.mixedT[:, kbase + kc, blk]))
                        reads.append(S.tok("mixedT", kbase + kc, tb))
                    reads.append(tk)
                    kbase += nk
                S.op("pe", seq(*fns), reads=reads, writes=[bk.tok])
                S.op("dve", lambda h, bk=bk, m=m, blk=blk: h.scalar_tensor_tensor(
                    out=self.xT[:, m, blk], in0=bk.f32, scalar=gate[:, m:m + 1], in1=self.xT[:, m, blk],
                    op0=ALU.mult, op1=ALU.add), reads=[bk.tok], writes=[S.tok("xT", tb)])

    def transposes(self, src_fn, ntile_cols, nchunks, dst_cols, tb, src_tok, kparts=128, c0=0):
        S = self.S
        bk = self.bank("AUX")
        ident = self.cb(0)
        fns = [(lambda h, c=c: h.transpose(bk.bf[:, c * ntile_cols:(c + 1) * ntile_cols], src_fn(c),
                                           ident[0:kparts, 0:kparts])) for c in range(nchunks)]
        S.op("pe", seq(*fns), reads=[src_tok], writes=[bk.tok])
        S.op("act", lambda h: h.activation(
            out=self.mixedT[:, c0:c0 + nchunks, dst_cols],
            in_=bk.bf[:, 0:nchunks * ntile_cols].rearrange("p (a b) -> p a b", a=nchunks), func=AF.Copy),
            reads=[bk.tok], writes=[S.tok("mixedT", c0 + c, tb) for c in range(nchunks)])

    def load_strip(self, names):
        S = self.S
        ts = S.tok("strip")
        views = {}
        fl = []
        off = 0
        for nm in names:
            band, hb, nh = STRIP_INFO[nm]
            W = STRIP_W[nm]
            v = self.strip[:, off:off + nh * W].rearrange("p (a b) -> p a b", a=nh)
            views[nm] = v
            for hh in range(nh):
                src = bass.AP(self.Fd.tensor, (hb + hh) * F_LEN + STRIP_OFF[nm], [[1, 128], [1, W]])
                fl.append((v[:, hh, :], src))
            off += nh * W

        def f(h, sem):
            for dst, src in fl:
                h.dma_start(out=dst, in_=src).then_inc(sem, 16)
        S.dma("sp", f, len(fl), writes=[ts])
        return views

    def banded_heads(self, pipe, qt, heads, acc, post):
        S = self.S
        J = self.cb(1)
        steps = []
        for hd in heads:
            band = hd["band"]
            kts = [kt for kt in range(qt - band, qt + band + 1) if 0 <= kt < 16]
            for g0 in range(0, len(kts), 4):
                steps.append((hd, kts[g0:g0 + 4]))
        for si, (hd, grp) in enumerate(steps):
            st = {}

            def s1(hd=hd, grp=grp, st=st):
                band = hd["band"]
                bk = self.bank("S")
                st["bk"] = bk
                fns = []
                fnj = []
                reads = [S.tok("QT", hd["qc"], qt // 4), S.tok("strip")]
                for i, kt in enumerate(grp):
                    o = bk.f32[:, i * 128:(i + 1) * 128]
                    pb = hd["pb"]
                    fns.append(self.MM(bk, o, self.KT[pb:pb + 64, hd["kc"], kt * 128:(kt + 1) * 128],
                                       self.QT[pb:pb + 64, hd["qc"], qt * 128:(qt + 1) * 128]))
                    tk = S.tok("KT", hd["kc"], kt // 4)
                    if tk not in reads:
                        reads.append(tk)
                for i, kt in enumerate(grp):
                    o = bk.f32[:, i * 128:(i + 1) * 128]
                    off = 128 * (band - (kt - qt))
                    fnj.append(self.MM(bk, o, J, hd["strip"][:, off:off + 128]))
                S.op("pe", seq(*(fns + fnj)), reads=reads, writes=[bk.tok])

            def s23(hd=hd, grp=grp, st=st):
                bk = st["bk"]
                n = len(grp) * 128
                pt = self.PT[self._pti % NPT]
                tp = S.tok("PT", self._pti % NPT)
                self._pti += 1
                S.op("act", lambda h, bk=bk, pt=pt, n=n: h.activation(out=pt[:, 0:n], in_=bk.f32[:, 0:n], func=AF.Exp),
                     reads=[bk.tok], writes=[tp])
                fns = []
                reads = [tp]
                for i, kt in enumerate(grp):
                    fns.append(self.MM(acc, acc.f32[:, hd["col"] * 65:hd["col"] * 65 + 65], pt[:, i * 128:(i + 1) * 128],
                                       hd["V"][:, kt, hd["vh"], :]))
                    tv = S.tok("V", kt // 4)
                    if tv not in reads:
                        reads.append(tv)
                S.op("pe", seq(*fns), reads=reads, writes=[acc.tok])
            pipe.push(s1, s23, post if si == len(steps) - 1 else None)

    def layer(self, s, l):
        S = self.S
        self._pti = 0
        self.ov_fence()
        self.norm(l, s, 0)
        if "A" in self.mixers:
            self.mixer_A(s, l)
        if "B" in self.mixers:
            self.mixer_B(s, l)
        if "C" in self.mixers:
            self.mixer_C(s, l)
        if "D" in self.mixers:
            self.mixer_D(s, l)
        if self.ffn:
            self.ov_fence()
            self.norm(l, s, 1)
            self.ffn_block(s, l)

    def mixer_A(self, s, l):
        S = self.S
        bd64 = self.cb(3)
        J64 = self.cb(5)
        V = self.Vr[:, 0:32 * 2 * 65].rearrange("p (r a b) -> p r a b", r=32, a=2)
        ts = S.tok("strip")
        stripA = self.strip[:, 0:1920].rearrange("p (a b) -> p a b", a=2)
        src = self.RAd[l, :, :]
        if not DBG.get("A_nostrip"):
            S.dma("sp", lambda h, sem: h.dma_start(out=self.strip[:, 0:1920], in_=src).then_inc(sem, 16), 1, writes=[ts])
        for p in range(2):
            self._wtot = 128
            wq, tq = self.wnext(l, ("A", p, "q"))
            self.proj_qk(wq, tq, 0, self.QT, "QT", 0, self.qkg(l, 0, 0), bd64, 64)
            wk, tk = self.wnext(l, ("A", p, "k"))
            self.proj_qk(wk, tk, 0, self.KT, "KT", 0, self.qkg(l, 0, 1), bd64, 64)
            wv, tv = self.wnext(l, ("A", p, "v"))
            self.ones_col(V, 64)
            self.proj_v(wv, tv, 128, 2, V, 0, rows64=True)
            wo, to = self.wnext(l, ("A", p, "o"))
            pipe = Pipe(PDEPTH)
            ident = self.cb(0)
            for r in range(32):
                rs_ = min(max(r - 4, 0), 24)
                dr0 = rs_ - r + 7
                ost = self.ostage[(r // 4) % 2]
                tos = S.tok("ost", (r // 4) % 2)
                acc = self.bank("ACC")

                def post_row(r=r, acc=acc, ost=ost, tos=tos):
                    rd = self.tmpf[r % 2]
                    trd = S.tok("tmpf", r % 2)
                    accv = acc.f32[0:64, 0:130].rearrange("p (a b) -> p a b", a=2)
                    S.op("dve", lambda h: h.reciprocal(out=rd[0:64, 0:2], in_=accv[:, :, 64]),
                         reads=[acc.tok], writes=[trd])
                    S.op("dve", lambda h: h.tensor_tensor(
                        out=ost[0:64, (r % 4) * 128:(r % 4) * 128 + 128].rearrange("p (a b) -> p a b", a=2),
                        in0=accv[:, :, 0:64], in1=rd[0:64, 0:2].unsqueeze(2).broadcast_to([64, 2, 64]), op=ALU.mult),
                        reads=[acc.tok, trd], writes=[tos])
                    if r % 4 == 3:
                        r0 = r - 3
                        bk = self.bank("AUX")
                        fns = [(lambda h, c=c: h.transpose(bk.bf[:, c * 64:(c + 1) * 64], ost[0:64, c * 128:(c + 1) * 128],
                                                           ident[0:64, 0:64])) for c in range(4)]
                        S.op("pe", seq(*fns), reads=[tos], writes=[bk.tok])
                        S.op("act", lambda h: h.activation(
                            out=self.mixedT[:, 0, r0 * 64:r0 * 64 + 256], in_=bk.bf[:, 0:256], func=AF.Copy),
                            reads=[bk.tok], writes=[S.tok("mixedT", 0, r0 // 8)])

                for hh in range(2):
                    pb = 64 * hh
                    st = {}

                    def s1(r=r, rs_=rs_, dr0=dr0, pb=pb, st=st, p=p):
                        bk = self.bank("S")
                        st["bk"] = bk
                        fns = []
                        reads = [S.tok("QT", 0, r // 8), ts]
                        for i in range(8):
                            kr = rs_ + i
                            fns.append(self.MM(bk, bk.f32[0:64, i * 64:(i + 1) * 64],
                                               self.KT[pb:pb + 64, 0, kr * 64:(kr + 1) * 64],
                                               self.QT[pb:pb + 64, 0, r * 64:(r + 1) * 64]))
                            tk_ = S.tok("KT", 0, kr // 8)
                            if tk_ not in reads:
                                reads.append(tk_)
                        fns.append(self.MM(bk, bk.f32[0:64, :], J64[pb:pb + 64, 0:64],
                                           stripA[pb:pb + 64, p, dr0 * 64:dr0 * 64 + 512]))
                        S.op("pe", seq(*fns), reads=reads, writes=[bk.tok])

                    def s23(rs_=rs_, hh=hh, st=st, acc=acc):
                        bk = st["bk"]
                        pt = self.PT[self._pti % NPT]
                        tp = S.tok("PT", self._pti % NPT)
                        self._pti += 1
                        S.op("act", lambda h: h.activation(out=pt[0:64, :], in_=bk.f32[0:64, :], func=AF.Exp),
                             reads=[bk.tok], writes=[tp])
                        fns = []
                        reads = [tp]
                        for i in range(8):
                            kr = rs_ + i
                            fns.append(self.MM(acc, acc.f32[0:64, hh * 65:hh * 65 + 65], pt[0:64, i * 64:(i + 1) * 64],
                                               V[0:64, kr, hh, :]))
                            tv_ = S.tok("V", kr // 8)
                            if tv_ not in reads:
                                reads.append(tv_)
                        S.op("pe", seq(*fns), reads=reads, writes=[acc.tok])
                    pipe.push(s1, s23, post_row if hh == 1 else None)
            pipe.flush()
            self.wout_pass(l, s, [(wo, to, 1)], None)

    def mixer_B(self, s, l):
        S = self.S
        bd64 = self.cb(3)
        V = self.Vr[:, 0:16 * 2 * 65].rearrange("p (t a b) -> p t a b", t=16, a=2)
        sv = self.load_strip(["B"])["B"]
        self._wtot = 256
        wq, tq = self.wnext(l, ("B", "q"))
        for c in range(2):
            self.proj_qk(wq, tq, c * 128, self.QT, "QT", c, self.qkg(l, 1, 0), bd64, 64)
        wk, tk = self.wnext(l, ("B", "k"))
        for c in range(2):
            self.proj_qk(wk, tk, c * 128, self.KT, "KT", c, self.qkg(l, 1, 1), bd64, 64)
        self._wtot = 128
        wv, tv = self.wnext(l, ("B", "v"))
        self.ones_col(V)
        self.proj_v(wv, tv, 128, 2, V, 0)
        wo, to = self.wnext(l, ("B", "o"))
        esink = self.small[:, 32 + l * 4:32 + l * 4 + 4]
        pipe = Pipe(PDEPTH)
        for qt in range(16):
            acc = self.bank("ACC")
            heads = [dict(qc=h // 2, pb=64 * (h % 2), kc=h // 2, vh=h // 2, strip=sv[:, h, :], band=1, col=h, V=V)
                     for h in range(4)]

            def post(qt=qt, acc=acc):
                accv = acc.f32[:, 0:260].rearrange("p (a b) -> p a b", a=4)
                rd = self.tmpf[qt % 2]
                trd = S.tok("tmpf", qt % 2)
                S.op("dve", lambda h, rd=rd, accv=accv: h.tensor_tensor(out=rd[:, 0:4], in0=accv[:, :, 64], in1=esink,
                                                                       op=ALU.add), reads=[acc.tok], writes=[trd])
                S.op("dve", lambda h, rd=rd: h.reciprocal(out=rd[:, 4:8], in_=rd[:, 0:4]), reads=[trd], writes=[trd])
                ost = self.ostage[qt % 2]
                tos = S.tok("ost", qt % 2)
                S.op("dve", lambda h, rd=rd, accv=accv, ost=ost: h.tensor_tensor(
                    out=ost[:, 0:256].rearrange("p (a b) -> p a b", a=4), in0=accv[:, :, 0:64],
                    in1=rd[:, 4:8].unsqueeze(2).broadcast_to([128, 4, 64]), op=ALU.mult),
                    reads=[acc.tok, trd], writes=[tos])
                self.transposes(lambda c, ost=ost: ost[:, c * 128:(c + 1) * 128], 128, 2,
                                slice(qt * 128, (qt + 1) * 128), qt // 4, tos)
            self.banded_heads(pipe, qt, heads, acc, post)
        pipe.flush()
        self.wout_pass(l, s, [(wo, to, 2)], None)

    def mixer_C(self, s, l):
        S = self.S
        bd64 = self.cb(3)
        V = self.Vr[:, 0:16 * 6 * 65].rearrange("p (t a b) -> p t a b", t=16, a=6)
        svs = self.load_strip(["C0", "C1", "C2"])
        self._wtot = 256
        w, t = self.wnext(l, ("C", "q0"))
        for c in range(2):
            self.proj_qk(w, t, c * 128, self.QT, "QT", c, self.qkg(l, 2, 0), bd64, 64)
        self._wtot = 128
        w, t = self.wnext(l, ("C", "q1"))
        self.proj_qk(w, t, 0, self.QT, "QT", 2, self.qkg(l, 2, 0), bd64, 64)
        self._wtot = 256
        w, t = self.wnext(l, ("C", "k0"))
        for c in range(2):
            self.proj_qk(w, t, c * 128, self.KT, "KT", c, self.qkg(l, 2, 1), bd64, 64)
        self._wtot = 128
        w, t = self.wnext(l, ("C", "k1"))
        self.proj_qk(w, t, 0, self.KT, "KT", 2, self.qkg(l, 2, 1), bd64, 64)
        self.ones_col(V)
        self._wtot = 192
        w, t = self.wnext(l, ("C", "v0"))
        self.proj_v(w, t, 192, 3, V, 0)
        w, t = self.wnext(l, ("C", "v1"))
        self.proj_v(w, t, 192, 3, V, 3)
        bands = (1, 2, 8)
        names = ("C0", "C1", "C2")
        pipe = Pipe(PDEPTH)
        for qt in range(16):
            acc = self.bank("ACC")
            heads = []
            for g in range(3):
                for j in range(2):
                    hh = 2 * g + j
                    heads.append(dict(qc=g, pb=64 * j, kc=g, vh=hh, strip=svs[names[g]][:, j, :], band=bands[g],
                                      col=hh, V=V))

            def post(qt=qt, acc=acc):
                accv = acc.f32[:, 0:390].rearrange("p (g j b) -> p g j b", g=3, j=2)
                rd = self.tmpf[qt % 2]
                trd = S.tok("tmpf", qt % 2)
                S.op("dve", lambda h, rd=rd, accv=accv: h.tensor_copy(
                    out=rd[:, 8:14].rearrange("p (g j) -> p g j", g=3), in_=accv[:, :, :, 64]),
                    reads=[acc.tok], writes=[trd])
                S.op("dve", lambda h, rd=rd: h.tensor_tensor(out=rd[:, 0:2], in0=rd[:, 8:10], in1=rd[:, 10:12], op=ALU.add),
                     reads=[trd], writes=[trd])
                S.op("dve", lambda h, rd=rd: h.tensor_tensor(out=rd[:, 0:2], in0=rd[:, 0:2], in1=rd[:, 12:14], op=ALU.add),
                     reads=[trd], writes=[trd])
                S.op("dve", lambda h, rd=rd: h.reciprocal(out=rd[:, 4:6], in_=rd[:, 0:2]), reads=[trd], writes=[trd])
                ost = self.ostage[qt % 2]
                tos = S.tok("ost", qt % 2)
                for g in range(3):
                    S.op("dve", lambda h, rd=rd, accv=accv, ost=ost, g=g: h.tensor_tensor(
                        out=ost[:, g * 128:(g + 1) * 128].rearrange("p (a b) -> p a b", a=2), in0=accv[:, g, :, 0:64],
                        in1=rd[:, 4:6].unsqueeze(2).broadcast_to([128, 2, 64]), op=ALU.mult),
                        reads=[acc.tok, trd], writes=[tos])
                self.transposes(lambda c, ost=ost: ost[:, c * 128:(c + 1) * 128], 128, 3,
                                slice(qt * 128, (qt + 1) * 128), qt // 4, tos)
            self.banded_heads(pipe, qt, heads, acc, post)
        pipe.flush()
        wo0, to0 = self.wnext(l, ("C", "o0"))
        self.wout_pass(l, s, [(wo0, to0, 2)], None)
        wo1, to1 = self.wnext(l, ("C", "o1"))
        self.wout_pass(l, s, [(wo1, to1, 1)], None, kbase0=2)

    def mixer_D(self, s, l):
        S = self.S
        bd32 = self.cb(4)
        J = self.cb(1)
        V = self.Vr[:, 0:16 * 4 * 65].rearrange("p (t a b) -> p t a b", t=16, a=4)
        sv = self.load_strip(["D"])["D"]
        nlam = self.small[:, 48 + l:49 + l]
        sub = self.vec("subln", l * 64, 64)
        om = 1.0 - lambda_init(l)
        ktm = [(self.KT[:, 0, :], "KT", 0), (self.KT[:, 1, :], "KT", 1), (self.KT[:, 2, :], "KT", 2),
               (self.QT[:, 2, :], "QT", 2)]
        for c in range(2):
            self._wtot = 128
            w, t = self.wnext(l, ("D", "q%d" % c))
            self.proj_qk(w, t, 0, self.QT, "QT", c, self.qkg(l, 3, 0), bd32, 32)
            w, t = self.wnext(l, ("D", "k%d" % c))
            multi = [(ktm[b][0], ktm[b][1], ktm[b][2], self.small[:, 160 + l * 4 + b:161 + l * 4 + b]) for b in range(4)]
            self.proj_qk(w, t, 0, None, None, None, None, bd32, 32, multi=multi)
            if c == 0:
                self.ones_col(V)
                self._wtot = 256
                w, t = self.wnext(l, ("D", "v"))
                self.proj_v(w, t, 256, 4, V, 0)
            pipe = Pipe(PDEPTH)
            for qb in range(4):
                ost = self.ostage[(2 * c + qb) % 2]
                tos = S.tok("ost", (2 * c + qb) % 2)
                for hh in (2 * c, 2 * c + 1):
                    accs = [self.bank("ACC"), self.bank("ACC")]

                    def post_head(qb=qb, hh=hh, accs=accs, ost=ost, tos=tos):
                        a0 = accs[0].f32[:, 0:260].rearrange("p (a b) -> p a b", a=4)
                        a1 = accs[1].f32[:, 0:260].rearrange("p (a b) -> p a b", a=4)
                        rd = self.rstd[hh % 2]
                        trd = S.tok("rstd", hh % 2)
                        t0 = self.tmpf[0]
                        t1 = self.tmpf[1]
                        t0v = t0[:, 0:256].rearrange("p (a b) -> p a b", a=4)
                        t1v = t1[:, 0:256].rearrange("p (a b) -> p a b", a=4)
                        S.op("dve", lambda h: h.reciprocal(out=rd[:, 0:4], in_=a0[:, :, 64]),
                             reads=[accs[0].tok], writes=[trd])
                        S.op("dve", lambda h: h.reciprocal(out=rd[:, 4:8], in_=a1[:, :, 64]),
                             reads=[accs[1].tok, trd], writes=[trd])
                        S.op("dve", lambda h: h.tensor_scalar(out=rd[:, 4:8], in0=rd[:, 4:8], scalar1=nlam, scalar2=None,
                                                              op0=ALU.mult), reads=[trd], writes=[trd])
                        S.op("dve", lambda h: h.tensor_tensor(
                            out=t0v, in0=a0[:, :, 0:64], in1=rd[:, 0:4].unsqueeze(2).broadcast_to([128, 4, 64]), op=ALU.mult),
                            reads=[accs[0].tok, trd], writes=[S.tok("tmpf", 0)])
                        S.op("dve", lambda h: h.tensor_tensor(
                            out=t1v, in0=a1[:, :, 0:64], in1=rd[:, 4:8].unsqueeze(2).broadcast_to([128, 4, 64]), op=ALU.mult),
                            reads=[accs[1].tok, trd], writes=[S.tok("tmpf", 1)])
                        S.op("dve", lambda h: h.tensor_tensor(out=t0[:, 0:256], in0=t0[:, 0:256], in1=t1[:, 0:256], op=ALU.add),
                             reads=[S.tok("tmpf", 1)], writes=[S.tok("tmpf", 0)])
                        S.op("act", lambda h: h.activation(out=t1[:, 0:256], in_=t0[:, 0:256], func=AF.Square),
                             reads=[S.tok("tmpf", 0)], writes=[S.tok("tmpf", 1)])
                        S.op("dve", lambda h: h.tensor_reduce(out=rd[:, 8:12], in_=t1v, axis=AX.X, op=ALU.add),
                             reads=[S.tok("tmpf", 1)], writes=[trd])
                        S.op("act", lambda h: h.activation(out=rd[:, 8:12], in_=rd[:, 8:12], func=AF.Ln, bias=self.eps64,
                                                           scale=1.0), reads=[trd], writes=[trd])
                        S.op("act", lambda h: h.activation(out=rd[:, 8:12], in_=rd[:, 8:12], func=AF.Exp, scale=-0.5),
                             reads=[trd], writes=[trd])
                        S.op("dve", lambda h: h.tensor_tensor(
                            out=t0v, in0=t0v, in1=rd[:, 8:12].unsqueeze(2).broadcast_to([128, 4, 64]), op=ALU.mult),
                            reads=[trd], writes=[S.tok("tmpf", 0)])
                        S.op("dve", lambda h: h.scalar_tensor_tensor(
                            out=ost[:, 0:1024].rearrange("p (q a b) -> p q a b", q=4, a=4)[:, :, hh, :], in0=t0v,
                            scalar=8.0 * om, in1=sub.unsqueeze(1).broadcast_to([128, 4, 64]), op0=ALU.mult, op1=ALU.mult),
                            reads=[S.tok("tmpf", 0)], writes=[tos])
                        if hh % 2 == 1:
                            cc = hh // 2
                            for jq in range(4):
                                qt = 4 * qb + jq
                                self.transposes(lambda c_, jq=jq, cc=cc: ost[:, jq * 256 + cc * 128:jq * 256 + (cc + 1) * 128],
                                                128, 1, slice(qt * 128, (qt + 1) * 128), qb, tos, c0=cc)

                    for i in range(2):
                        pb = 32 * (2 * (hh % 2) + i)
                        acc = accs[i]
                        for kt in range(16):
                            d0 = kt - 4 * qb
                            mixed = -1 <= d0 <= 4
                            st = {}

                            def s1(pb=pb, kt=kt, d0=d0, mixed=mixed, st=st, c=c, hh=hh, qb=qb):
                                bk = self.bank("S")
                                st["bk"] = bk
                                kb = ktm[pb // 32]
                                fns = [self.MM(bk, bk.f32, kb[0][:, kt * 128:(kt + 1) * 128],
                                               self.QT[:, c, qb * 512:(qb + 1) * 512])]
                                if mixed:
                                    off = 128 * (4 - d0)
                                    fns.append(self.MM(bk, bk.f32, J, sv[:, hh, off:off + 512]))
                                S.op("pe", seq(*fns), reads=[S.tok("QT", c, qb), S.tok(kb[1], kb[2], kt // 4), S.tok("strip")],
                                     writes=[bk.tok])

                            def s23(kt=kt, d0=d0, mixed=mixed, st=st, hh=hh, acc=acc):
                                bk = st["bk"]
                                pt = self.PT[self._pti % NPT]
                                tp = S.tok("PT", self._pti % NPT)
                                self._pti += 1
                                if mixed:
                                    S.op("act", lambda h: h.activation(out=pt, in_=bk.f32, func=AF.Exp),
                                         reads=[bk.tok], writes=[tp])
                                else:
                                    cb_ = self.vec("t5c", (0 if d0 < 0 else 4) + hh)
                                    S.op("act", lambda h: h.activation(out=pt, in_=bk.f32, func=AF.Exp, bias=cb_, scale=1.0),
                                         reads=[bk.tok], writes=[tp])
                                fns = [self.MM(acc, acc.f32[:, jq * 65:jq * 65 + 65], pt[:, jq * 128:(jq + 1) * 128],
                                               V[:, kt, hh, :]) for jq in range(4)]
                                S.op("pe", seq(*fns), reads=[tp, S.tok("V", kt // 4)], writes=[acc.tok])
                            pipe.push(s1, s23, post_head if (i == 1 and kt == 15) else None)
            pipe.flush()
        wo, to = self.wnext(l, ("D", "o"))
        self.wout_pass(l, s, [(wo, to, 2)], None)

    def ffn_block(self, s, l):
        S = self.S
        gate = self.mod(l, 5, s)
        co, _ = VC["conv"]

        def cw(tap, jc):
            c = co + (l * 4 + tap) * 44 + jc
            return self.vecs[:, c:c + 1]
        self._wtot = 256
        blkc = 0
        for qi, (j0, nj) in enumerate(QUARTERS):
            for jj in range(nj):
                j = j0 + jj
                w, tw = self.wnext(l, ("U", j))
                for (o0, o1, i0, i1) in FFN_BLOCKS:
                    nin = i1 - i0
                    nout = o1 - o0
                    bks = []
                    for vg in range(2):
                        bk = self.bank("ALL8")
                        fns = [self.MM(bk, bk.f32[:, 0:nin], self._w(w, kc, vg * 128, 128), self.hT[:, kc, i0:i1])
                               for kc in range(8)]
                        rt = list({S.tok("hT", i0 // 512), S.tok("hT", (i1 - 1) // 512)})
                        S.op("pe", seq(*fns), reads=[tw] + rt, writes=[bk.tok])
                        bks.append(bk)
                    ab = self.acc[blkc % 4]
                    tas = [S.tok("acc", blkc % 4, 0), S.tok("acc", blkc % 4, 1)]
                    sgb = self.sg[blkc % 4]
                    tsg = S.tok("sg", blkc % 4)
                    blkc += 1
                    for vg in range(2):
                        ta = tas[vg]
                        bk = bks[vg]
                        jc = j + 22 * vg
                        a = ab[:, vg, :]
                        c1 = o0 - i0
                        S.op("act", lambda h, a=a, bk=bk, jc=jc, c1=c1, nout=nout: h.activation(
                            out=a[:, 0:nout], in_=bk.f32[:, c1:c1 + nout], func=AF.Identity, bias=cw(3, jc),
                            scale=cw(1, jc)), reads=[bk.tok], writes=[ta])
                        ta0 = max(o0, 1)
                        n0 = o1 - ta0
                        S.op("dve", lambda h, a=a, bk=bk, jc=jc, ta0=ta0, n0=n0, o0=o0, i0=i0: h.scalar_tensor_tensor(
                            out=a[:, ta0 - o0:ta0 - o0 + n0], in0=bk.f32[:, ta0 - 1 - i0:ta0 - 1 - i0 + n0],
                            scalar=cw(0, jc), in1=a[:, ta0 - o0:ta0 - o0 + n0], op0=ALU.mult, op1=ALU.add),
                            reads=[bk.tok, ta], writes=[ta])
                        e2 = min(o1, N - 1)
                        n2 = e2 - o0
                        S.op("dve", lambda h, a=a, bk=bk, jc=jc, n2=n2, o0=o0, i0=i0: h.scalar_tensor_tensor(
                            out=a[:, 0:n2], in0=bk.f32[:, o0 + 1 - i0:o0 + 1 - i0 + n2],
                            scalar=cw(2, jc), in1=a[:, 0:n2], op0=ALU.mult, op1=ALU.add),
                            reads=[bk.tok, ta], writes=[ta])
                    S.op("act", lambda h, ab=ab, sgb=sgb, nout=nout: h.activation(out=sgb[:, 0:nout], in_=ab[:, 1, 0:nout],
                                                                                func=AF.Silu),
                         reads=[tas[1]], writes=[tsg])
                    gt = list({S.tok("gT", jj, o0 // 512), S.tok("gT", jj, (o1 - 1) // 512)})
                    S.op("pool", lambda h, ab=ab, sgb=sgb, nout=nout, jj=jj, o0=o0, o1=o1: h.tensor_tensor(
                        out=self.gT[:, jj, o0:o1], in0=ab[:, 0, 0:nout], in1=sgb[:, 0:nout], op=ALU.mult),
                        reads=[tas[0], tsg], writes=gt)
            for mp in range(4):
                w, tw = self.wnext(l, ("Dn", qi, mp))
                wv = w[:, 0:2 * nj * 128].rearrange("p (m a b) -> p m a b", m=2, a=nj)
                for mm in range(2):
                    m = 2 * mp + mm
                    for tb in range(4):
                        blk = slice(tb * 512, (tb + 1) * 512)
                        bk = self.bank("ALL7")
                        fns = [self.MM(bk, bk.f32, wv[:, mm, jj, :], self.gT[:, jj, blk]) for jj in range(nj)]
                        S.op("pe", seq(*fns), reads=[tw] + [S.tok("gT", jj, tb) for jj in range(nj)], writes=[bk.tok])
                        S.op("dve", lambda h, bk=bk, m=m, blk=blk: h.scalar_tensor_tensor(
                            out=self.xT[:, m, blk], in0=bk.f32, scalar=gate[:, m:m + 1], in1=self.xT[:, m, blk],
                            op0=ALU.mult, op1=ALU.add), reads=[bk.tok], writes=[S.tok("xT", tb)])

    def build(self):
        nc = self.nc
        self.setup_mem()
        self.banks_init()
        self.epsD = self.small[:, 250:251]
        self.eps64 = self.small[:, 251:252]
        self.eps32 = self.small[:, 252:253]
        nc = self.nc
        handles = {"pe": nc.tensor, "act": nc.scalar, "dve": nc.vector, "pool": nc.gpsimd, "sp": nc.sync}
        battr = {"pe": "tensor", "act": "scalar", "dve": "vector", "pool": "gpsimd", "sp": "sync"}

        def replay(eng, h, semh):
            for it in eng.items:
                if it[0] == "w":
                    h.wait_ge(semh[it[1]], it[2])
                elif it[0] == "i":
                    it[1](h).then_inc(semh[it[2]], 1)
                else:
                    it[1](h, semh[it[2]])

        S1 = self.S
        S1.op("pool", lambda h: h.memset(self.small[:, 250:251], EPS * D), writes=[S1.tok("pro")])
        S1.op("pool", lambda h: h.memset(self.small[:, 251:252], EPS * 64), writes=[S1.tok("pro")])
        S1.op("pool", lambda h: h.memset(self.small[:, 252:253], EPS * 32), writes=[S1.tok("pro")])
        self.prologue()
        with ExitStack() as es:
            semh = {}
            for i, k in enumerate(S1.all_keys()):
                semh[k] = es.enter_context(nc.semaphore("p%d" % i))
            block = es.enter_context(nc.Block())
            for en in Sched.ENG:
                getattr(block, battr[en])(lambda h, en=en: replay(S1.E[en], h, semh))
        S2 = Sched()
        self.S = S2
        self.banks_init()
        S2.dry = True
        self._banks_rr = {}
        self.body()
        S2.dry = False
        self._banks_rr = {}
        self.wi = 0
        self.w_issued = 0
        self.body()
        S2._waits("sp", {k: v for k, v in S2.dma_cnt.items() if v})
        self.n_items = {en: len(S2.E[en].items) for en in Sched.ENG}
        with ExitStack() as es:
            semh = {}
            keys = S2.all_keys()
            for i, k in enumerate(keys):
                semh[k] = es.enter_context(nc.semaphore("b%d" % i))
            with nc.Fori(0, self.nseq) as li:
                self.loop_i = li
                for en in Sched.ENG:
                    replay(S2.E[en], handles[en], semh)
                nc.all_engine_barrier()
                for k in keys:
                    nc.sync.sem_clear(semh[k])
                nc.all_engine_barrier()
        return nc


def host_inputs(inp, seqs_x, seqs_c):
    f = np.float32
    w_ada = np.asarray(inp["w_ada"], f)
    wada = w_ada.reshape(DEPTH, 8, 128, 48, 128).transpose(0, 3, 2, 1, 4).reshape(DEPTH * 48, 128, 1024)
    wada = np.ascontiguousarray(wada)
    wl = np.empty((DEPTH, 128, WL_COLS), f)
    for l in range(DEPTH):
        ch = layer_chunks(np.asarray(inp["w_in"][l], f), np.asarray(inp["w_out"][l], f),
                          np.asarray(inp["w_up"][l], f), np.asarray(inp["w_down"][l], f))
        assert [k for k, _ in ch] == [k for k, _ in PLAN]
        o = 0
        for (k, a), (_, n) in zip(ch, PLAN):
            assert a.shape == (128, n), (k, a.shape, n)
            wl[l, :, o:o + n] = a
            o += n
    vecs = np.zeros((128, NVEC), f)

    def put(name, arr):
        o, n = VC[name]
        assert arr.shape == (128, n), (name, arr.shape, n)
        vecs[:, o:o + n] = arr
    b_ada = np.asarray(inp["b_ada"], f)
    put("b_ada", b_ada.reshape(DEPTH, 48, 128).transpose(2, 0, 1).reshape(128, DEPTH * 48))
    gn = np.stack([np.asarray(inp["norm_attn_g"], f), np.asarray(inp["norm_ffn_g"], f)], 1)
    put("gnorm", gn.reshape(DEPTH, 2, 8, 128).transpose(3, 0, 1, 2).reshape(128, DEPTH * 16))
    p = np.arange(128)
    qk = np.zeros((128, DEPTH, 4, 2), f)
    for m, nm in enumerate(["qkn_a", "qkn_b", "qkn_c", "qkn_d"]):
        g = np.asarray(inp[nm], f)
        d = g.shape[2]
        qk[:, :, m, :] = g[:, :, p % d].transpose(2, 0, 1)
    put("qkg", qk.reshape(128, DEPTH * 8))
    cw = np.concatenate([np.asarray(inp["conv_w"], f), np.asarray(inp["conv_b"], f)[:, None, :]], 1)
    put("conv", cw.reshape(DEPTH, 4, 44, 128).transpose(3, 0, 1, 2).reshape(128, DEPTH * 4 * 44))
    t5 = np.asarray(inp["t5_table"], f)
    put("t5c", np.broadcast_to(np.concatenate([t5[15, 10:14], t5[31, 10:14]])[None, :], (128, 8)))
    put("sink", np.broadcast_to(np.asarray(inp["sink_b"], f).reshape(1, 16), (128, 16)))
    put("linit", np.broadcast_to(np.array([lambda_init(l) for l in range(DEPTH)], f)[None, :], (128, DEPTH)))
    put("lam", np.broadcast_to(np.asarray(inp["lam_d"], f).reshape(1, -1), (128, DEPTH * 128)))
    put("subln", np.broadcast_to(np.asarray(inp["subln_d"], f).reshape(1, -1), (128, DEPTH * 64)))
    cst = np.zeros((128, 7, 128), f)
    cst[:, 0] = np.eye(128)
    cst[:, 1] = np.eye(128)
    cst[:, 2] = np.eye(128)[::-1]
    cst[:, 3] = 1.0
    cst[:, 4] = np.kron(np.eye(2), np.ones((64, 64)))
    cst[:, 5] = np.kron(np.eye(4), np.ones((32, 32)))
    pc = np.arange(64)
    kc_ = 63 - pc[:, None]
    c_ = pc[None, :]
    cs = np.clip(c_ - 8, 0, 48)
    okm = (kc_ >= cs) & (kc_ < cs + 16)
    for hb in (0, 64):
        cst[hb:hb + 64, 6, 0:64] = np.eye(64)[::-1]
        cst[hb:hb + 64, 6, 64:128] = np.where(okm, 0.0, MASKV)
    t5aug = np.concatenate([t5, np.full((1, 14), MASKV, f)], 0)
    oh = build_onehot()
    rpb = np.asarray(inp["rpb_a"], f)
    rp = np.zeros((DEPTH, 4, 15, 127), f)
    rp[..., 48:79] = rpb[..., ::-1]
    rp = rp.reshape(16, 15 * 127)
    maps = []
    for xs, cs_ in zip(seqs_x, seqs_c):
        ns = xs.shape[0]
        cT = np.ascontiguousarray(cs_.reshape(ns, 8, 128).transpose(2, 1, 0).reshape(128, 8 * ns))
        maps.append({"x": np.ascontiguousarray(xs), "cT": cT, "wada": wada, "wl": wl, "vecs": vecs,
                     "cst": cst.reshape(128, 7 * 128), "t5aug": t5aug, "oh": oh, "rpbrp": rp})
    return maps


_CACHE = {}


def get_program(nseq, layers=(0, 1, 2, 3), mixers=("A", "B", "C", "D"), ffn=True):
    key = (nseq, tuple(layers), tuple(mixers), ffn)
    if key not in _CACHE:
        b = Builder(nseq, list(layers), mixers, ffn)
        _CACHE[key] = b.build()
    return _CACHE[key]


def kernel(**inp):
    xp = np.asarray(inp["x_prompt"], np.float32)
    xs = np.asarray(inp["x_sample"], np.float32)
    cp = np.asarray(inp["c_prompt"], np.float32)
    cs = np.asarray(inp["c_sample"], np.float32)
    seqs_x, seqs_c = [], []
    for c in range(8):
        seqs_x.append(np.concatenate([xp[4 * c:4 * c + 4], xs[2 * c:2 * c + 2]], 0))
        seqs_c.append(np.concatenate([cp[4 * c:4 * c + 4], cs[2 * c:2 * c + 2]], 0))
    maps = host_inputs(inp, seqs_x, seqs_c)
    nc = get_program(6)
    res = run_bass_kernel_spmd(nc, maps, core_ids=list(range(8)))
    yp = np.empty_like(xp)
    ys = np.empty_like(xs)
    for c in range(8):
        y = res.results[c]["y"]
        yp[4 * c:4 * c + 4] = y[0:4]
        ys[2 * c:2 * c + 2] = y[4:6]
    return (yp, ys)
```
